# Optimizing a Trainium2 kernel written in Bass

```python
import math
import jax, jax.numpy as jnp
from jax import lax
import numpy as np

D_MODEL = 1024
BATCH = 8
SEQ = 4096
DEPTH = 4

ATTN_WIDTH = D_MODEL // 2
SSM_WIDTH = D_MODEL - ATTN_WIDTH
HEAD_DIM = 64
N_HEADS = ATTN_WIDTH // HEAD_DIM
SSM_GROUP = 16
N_SSM_GROUPS = SSM_WIDTH // SSM_GROUP
SSM_STATE = 64
IN_WIDTH = 3 * ATTN_WIDTH + SSM_WIDTH
D_FF = ((8 * D_MODEL // 3) + 127) // 128 * 128
CONV_WIDTH = 3
BLOCK_Q = 128
N_MOD = 6
EPS = 1e-6

kernel_name = "hybrid_stickbreak_s5_convffn_adaln"


def _rmsnorm(x, g):
    xf = x.astype(jnp.float32)
    inv = lax.rsqrt(jnp.mean(xf * xf, axis=-1, keepdims=True) + EPS)
    return (xf * inv * g.astype(jnp.float32)).astype(x.dtype)


def _modulate(h, shift, scale):
    return h * (1.0 + scale[:, None, :]) + shift[:, None, :]


def _stick_breaking(q, k, v):
    B, H, L, Dh = q.shape
    nb = L // BLOCK_Q
    qf = q.astype(jnp.float32)
    kf = k.astype(jnp.float32)
    vf = v.astype(jnp.float32)
    q_blocks = qf.reshape(B, H, nb, BLOCK_Q, Dh).transpose(2, 0, 1, 3, 4)
    k_pos = jnp.arange(L, dtype=jnp.int32)
    inv_sqrt = 1.0 / math.sqrt(Dh)

    def block_fn(args):
        qb, bi = args
        z = jnp.einsum('bhqd,bhkd->bhqk', qb, kf) * inv_sqrt
        q_pos = bi * BLOCK_Q + jnp.arange(BLOCK_Q, dtype=jnp.int32)
        mask = k_pos[None, :] < q_pos[:, None]
        log_1m = jnp.where(mask, jax.nn.log_sigmoid(-z), 0.0)
        tail = lax.cumsum(log_1m, axis=3, reverse=True) - log_1m
        w = jnp.where(mask, jnp.exp(jax.nn.log_sigmoid(z) + tail), 0.0)
        return jnp.einsum('bhqk,bhkd->bhqd', w, vf)

    o = lax.map(block_fn, (q_blocks, jnp.arange(nb, dtype=jnp.int32)))
    o = o.transpose(1, 0, 3, 2, 4).reshape(B, L, H * Dh)
    return o.astype(q.dtype)


def _s5(u, a_re, a_im, log_dt, b_re, b_im, c_re, c_im, d_skip, glu_w, glu_b):
    Bsz, L, _ = u.shape
    f32 = jnp.float32
    ug = u.astype(f32).reshape(Bsz, L, N_SSM_GROUPS, SSM_GROUP)
    dt = jnp.exp(log_dt.astype(f32))[:, None]
    ar = a_re.astype(f32)
    ai = a_im.astype(f32)
    mag = jnp.exp(dt * ar)
    abar_re = mag * jnp.cos(dt * ai)
    abar_im = mag * jnp.sin(dt * ai)
    em_re = abar_re - 1.0
    em_im = abar_im
    den = ar * ar + ai * ai
    f_re = (em_re * ar + em_im * ai) / den
    f_im = (em_im * ar - em_re * ai) / den
    br = b_re.astype(f32)
    bim = b_im.astype(f32)
    bb_re = f_re[..., None] * br - f_im[..., None] * bim
    bb_im = f_re[..., None] * bim + f_im[..., None] * br
    bu_re = jnp.einsum('blgh,gph->blgp', ug, bb_re)
    bu_im = jnp.einsum('blgh,gph->blgp', ug, bb_im)
    a_r = jnp.broadcast_to(abar_re, bu_re.shape)
    a_i = jnp.broadcast_to(abar_im, bu_re.shape)

    def combine(e1, e2):
        a1r, a1i, b1r, b1i = e1
        a2r, a2i, b2r, b2i = e2
        return (a2r * a1r - a2i * a1i,
                a2r * a1i + a2i * a1r,
                a2r * b1r - a2i * b1i + b2r,
                a2r * b1i + a2i * b1r + b2i)

    _, _, s_re, s_im = lax.associative_scan(combine, (a_r, a_i, bu_re, bu_im), axis=1)
    y = (jnp.einsum('ghp,blgp->blgh', c_re.astype(f32), s_re)
         - jnp.einsum('ghp,blgp->blgh', c_im.astype(f32), s_im)
         + d_skip.astype(f32) * ug)
    y = jax.nn.gelu(y)
    gate = jax.nn.sigmoid(jnp.einsum('blgh,ghk->blgk', y, glu_w.astype(f32)) + glu_b.astype(f32))
    return (y * gate).reshape(Bsz, L, SSM_WIDTH).astype(u.dtype)


def _causal_dwconv(h, w, b):
    L = h.shape[1]
    hp = jnp.pad(h, ((0, 0), (CONV_WIDTH - 1, 0), (0, 0)))
    out = b
    for i in range(CONV_WIDTH):
        out = out + hp[:, i:i + L, :] * w[i]
    return out


def setup_inputs(seed: int = 0) -> dict:
    key = jax.random.key(seed)
    ks = jax.random.split(key, 32)
    f32 = jnp.float32
    D, G, P, H = D_MODEL, N_SSM_GROUPS, SSM_STATE, SSM_GROUP
    nrm = lambda k, shape, s: jax.random.normal(k, shape, f32) * s
    n_idx = jnp.arange(P, dtype=f32)
    return {
        "x": jax.random.normal(ks[0], (BATCH, SEQ, D), f32),
        "c": jax.random.normal(ks[1], (BATCH, D), f32),
        "ada_w": nrm(ks[2], (DEPTH, D, N_MOD * D), 0.5 * D ** -0.5),
        "ada_b": nrm(ks[3], (DEPTH, N_MOD * D), 0.01),
        "norm1_g": 1.0 + nrm(ks[4], (DEPTH, D), 0.02),
        "w_in": nrm(ks[5], (DEPTH, D, IN_WIDTH), D ** -0.5),
        "q_norm_g": 1.0 + nrm(ks[6], (DEPTH, HEAD_DIM), 0.02),
        "k_norm_g": 1.0 + nrm(ks[7], (DEPTH, HEAD_DIM), 0.02),
        "ssm_a_re": -0.5 + nrm(ks[8], (DEPTH, G, P), 0.01),
        "ssm_a_im": math.pi * n_idx + nrm(ks[9], (DEPTH, G, P), 0.01),
        "ssm_log_dt": jax.random.uniform(ks[10], (DEPTH, G), f32, math.log(1e-3), math.log(1e-1)),
        "ssm_b_re": nrm(ks[11], (DEPTH, G, P, H), (2.0 * H) ** -0.5),
        "ssm_b_im": nrm(ks[12], (DEPTH, G, P, H), (2.0 * H) ** -0.5),
        "ssm_c_re": nrm(ks[13], (DEPTH, G, H, P), (2.0 * P) ** -0.5),
        "ssm_c_im": nrm(ks[14], (DEPTH, G, H, P), (2.0 * P) ** -0.5),
        "ssm_d": nrm(ks[15], (DEPTH, G, H), 1.0),
        "glu_w": nrm(ks[16], (DEPTH, G, H, H), H ** -0.5),
        "glu_b": nrm(ks[17], (DEPTH, G, H), 0.01),
        "attn_out_g": 1.0 + nrm(ks[18], (DEPTH, ATTN_WIDTH), 0.02),
        "ssm_out_g": 1.0 + nrm(ks[19], (DEPTH, SSM_WIDTH), 0.02),
        "w_out": nrm(ks[20], (DEPTH, D, D), D ** -0.5),
        "norm2_g": 1.0 + nrm(ks[21], (DEPTH, D), 0.02),
        "ffn_w_up": nrm(ks[22], (DEPTH, D, 2 * D_FF), D ** -0.5),
        "ffn_conv_w": nrm(ks[23], (DEPTH, CONV_WIDTH, 2 * D_FF), CONV_WIDTH ** -0.5),
        "ffn_conv_b": nrm(ks[24], (DEPTH, 2 * D_FF), 0.01),
        "ffn_w_down": nrm(ks[25], (DEPTH, D_FF, D), D_FF ** -0.5),
    }


def reference(x, c, ada_w, ada_b, norm1_g, w_in, q_norm_g, k_norm_g,
              ssm_a_re, ssm_a_im, ssm_log_dt, ssm_b_re, ssm_b_im, ssm_c_re, ssm_c_im,
              ssm_d, glu_w, glu_b, attn_out_g, ssm_out_g, w_out, norm2_g,
              ffn_w_up, ffn_conv_w, ffn_conv_b, ffn_w_down):
    B, L, D = x.shape
    c_act = jax.nn.silu(c)
    for l in range(DEPTH):
        mod = c_act @ ada_w[l] + ada_b[l]
        sh1, sc1, g1, sh2, sc2, g2 = jnp.split(mod, N_MOD, axis=-1)

        h = _modulate(_rmsnorm(x, norm1_g[l]), sh1, sc1)
        p = h @ w_in[l]
        q = p[..., :ATTN_WIDTH].reshape(B, L, N_HEADS, HEAD_DIM)
        k = p[..., ATTN_WIDTH:2 * ATTN_WIDTH].reshape(B, L, N_HEADS, HEAD_DIM)
        v = p[..., 2 * ATTN_WIDTH:3 * ATTN_WIDTH].reshape(B, L, N_HEADS, HEAD_DIM)
        u = p[..., 3 * ATTN_WIDTH:]
        q = _rmsnorm(q, q_norm_g[l]).transpose(0, 2, 1, 3)
        k = _rmsnorm(k, k_norm_g[l]).transpose(0, 2, 1, 3)
        v = v.transpose(0, 2, 1, 3)
        o_attn = _stick_breaking(q, k, v)
        o_ssm = _s5(u, ssm_a_re[l], ssm_a_im[l], ssm_log_dt[l], ssm_b_re[l], ssm_b_im[l],
                    ssm_c_re[l], ssm_c_im[l], ssm_d[l], glu_w[l], glu_b[l])
        o = jnp.concatenate([_rmsnorm(o_attn, attn_out_g[l]),
                             _rmsnorm(o_ssm, ssm_out_g[l])], axis=-1)
        x = x + g1[:, None, :] * (o @ w_out[l])

        h = _modulate(_rmsnorm(x, norm2_g[l]), sh2, sc2)
        up = _causal_dwconv(h @ ffn_w_up[l], ffn_conv_w[l], ffn_conv_b[l])
        val, gate = jnp.split(up, 2, axis=-1)
        x = x + g2[:, None, :] * ((jax.nn.gelu(gate) * val) @ ffn_w_down[l])
    return x
```

```python
import numpy as np
from contextlib import ExitStack
import concourse.bass as bass
import concourse.mybir as mybir
from concourse.bass_utils import run_bass_kernel_spmd

F32 = mybir.dt.float32
BF16 = mybir.dt.bfloat16
I32 = mybir.dt.int32
AF = mybir.ActivationFunctionType
ALU = mybir.AluOpType

D = 1024
L = 4096
DEPTH = 4
TT = 512
NT = L // TT
KC = D // 128
DFF = 2816
NFF = DFF // 128
EPS = 1e-6
PI = float(np.pi)

_cols = {}
_off = 0
for _n, _w in [("ada_b", 48), ("n1g", 8), ("n2g", 8), ("qg", 1), ("kg", 1), ("aog", 4), ("sog", 4),
               ("aBre", 256), ("aBim", 256), ("ldtB", 4), ("BreB", 256), ("BimB", 256),
               ("aSre", 16), ("aSim", 16), ("ldtS", 16), ("CreS", 256), ("CimS", 256),
               ("dsk", 4), ("GS", 64), ("glub", 4), ("cw", 132), ("cb", 44)]:
    _cols[_n] = (_off, _off + _w)
    _off += _w
NPAR = _off


class R:
    __slots__ = ("w", "r", "excl")

    def __init__(self, excl=False):
        self.w = None
        self.r = []
        self.excl = excl


class _Cap:
    def __getattr__(self, name):
        return lambda *a, **k: (name, a, k)


_CAP = _Cap()


class Sched:
    ND = 24

    def __init__(self, nc):
        self.nc = nc
        self.eng = {"pe": nc.tensor, "act": nc.scalar, "dve": nc.vector, "pool": nc.gpsimd, "sp": nc.sync}
        self.sem = {e: nc.alloc_semaphore("c_" + e) for e in self.eng}
        self.cnt = {e: 0 for e in self.eng}
        self.seen = {e: {x: 0 for x in self.eng} for e in self.eng}
        self.dsem = [nc.alloc_semaphore("d_%d" % i) for i in range(self.ND)]
        self.dcnt = [0] * self.ND
        self.dseen = {e: [0] * self.ND for e in self.eng}
        self.dnext = 0
        self.nwait = 0
        self.nops = 0
        self.tinyops = set()

    def _wait_tok(self, eng, tok, is_dma=False, tiny=False):
        e = self.eng[eng]
        if tok[0] == "c":
            _, x, idx = tok
            if x == eng:
                if not is_dma:
                    if eng == "pe" or self.cnt[eng] - idx >= 3:
                        return
                    if not tiny and (x, idx) not in self.tinyops:
                        return
            if self.seen[eng][x] >= idx:
                return
            e.wait_ge(self.sem[x], idx)
            self.seen[eng][x] = idx
            self.nwait += 1
        else:
            _, slot, val = tok
            if self.dseen[eng][slot] >= val:
                return
            e.wait_ge(self.dsem[slot], val)
            self.dseen[eng][slot] = val
            self.nwait += 1

    def _deps(self, eng, reads, writes, is_dma=False, tiny=False):
        toks = []
        for r in reads:
            if r.w is not None:
                toks.append(r.w)
        for w in writes:
            if w.w is not None:
                toks.append(w.w)
            toks.extend(w.r)
        for t in toks:
            self._wait_tok(eng, t, is_dma, tiny)

    def _commit(self, tok, reads, writes):
        for r in reads:
            r.r.append(tok)
        for w in writes:
            w.w = tok
            w.r = []

    def op(self, eng, fn, Rd=(), Wr=(), tiny=None):
        if any(r.excl for r in Rd):
            Wr = list(Wr) + [r for r in Rd if r.excl and r not in Wr]
            Rd = [r for r in Rd if not r.excl]
        name, a, k = fn(_CAP)
        if tiny is None:
            o = k.get("out", None)
            if o is None:
                o = k.get("ap", None)
            if o is None:
                o = a[0]
            n = 1
            for d in list(o.shape)[1:]:
                n *= int(d)
            tiny = n < 192
        self._deps(eng, Rd, Wr, False, tiny)
        ins = getattr(self.eng[eng], name)(*a, **k)
        self.cnt[eng] += 1
        if tiny:
            self.tinyops.add((eng, self.cnt[eng]))
        ins.then_inc(self.sem[eng], 1)
        self._commit(("c", eng, self.cnt[eng]), Rd, Wr)
        self.nops += 1
        return ins

    def dma(self, q, out, in_, Rd=(), Wr=()):
        self._deps(q, Rd, Wr, True)
        slot = self.dnext % self.ND
        self.dnext += 1
        if self.dcnt[slot] > 0:
            self._wait_tok(q, ("d", slot, 16 * self.dcnt[slot]))
        ins = self.eng[q].dma_start(out=out, in_=in_)
        self.dcnt[slot] += 1
        ins.then_inc(self.dsem[slot], 16)
        self._commit(("d", slot, 16 * self.dcnt[slot]), Rd, Wr)
        self.nops += 1
        return ins

    def barrier(self, engines=None):
        for e in self.eng:
            for x in self.eng:
                if x != e and self.cnt[x] > 0:
                    self._wait_tok(e, ("c", x, self.cnt[x]))
            for s in range(self.ND):
                if self.dcnt[s] > 0:
                    self._wait_tok(e, ("d", s, 16 * self.dcnt[s]))


def build(n_layers=DEPTH, dbg=False, nq_tiles=NT, phases="ABCDEF", dumps=None):
    nc = bass.Bass("TRN2", target_bir_lowering=False)
    S = Sched(nc)
    _dd = {}

    def dump(name, tile, Rd, cond=True):
        if dumps is None or not cond or name in _dd:
            return
        shp = list(tile.shape)
        dt_ = tile.dtype
        dten = nc.dram_tensor("dump_" + name, shp, dt_, kind="ExternalOutput")
        _dd[name] = dten
        dumps.append("dump_" + name)
        S.dma("sp", dten.ap() if len(shp) == 2 else dten.ap(), tile, Rd=Rd)

    xT_in = nc.dram_tensor("xT", [D, L], F32, kind="ExternalInput")
    cT_in = nc.dram_tensor("cT", [128, KC], F32, kind="ExternalInput")
    par_in = nc.dram_tensor("par", [DEPTH, 128, NPAR], F32, kind="ExternalInput")
    adaw_in = nc.dram_tensor("ada_w", [DEPTH, D, 6 * D], F32, kind="ExternalInput")
    win_in = nc.dram_tensor("w_in", [DEPTH, D, 2048], F32, kind="ExternalInput")
    wout_in = nc.dram_tensor("w_out", [DEPTH, D, D], F32, kind="ExternalInput")
    wup_in = nc.dram_tensor("w_up", [DEPTH, D, 2 * DFF], F32, kind="ExternalInput")
    wdn_in = nc.dram_tensor("w_dn", [DEPTH, DFF, D], F32, kind="ExternalInput")
    yT_out = nc.dram_tensor("yT", [D, L], F32, kind="ExternalOutput")

    x_scr = nc.dram_tensor("x_scr", [D, L], F32, kind="Internal")
    qk_scr = nc.dram_tensor("qk_scr", [1024, L], BF16, kind="Internal")
    v_scr = nc.dram_tensor("v_scr", [L, 512], BF16, kind="Internal")
    u_scr = nc.dram_tensor("u_scr", [512, L], BF16, kind="Internal")
    o_scr = nc.dram_tensor("o_scr", [D, L], F32, kind="Internal")
    act_scr = nc.dram_tensor("act_scr", [DFF, L], BF16, kind="Internal")
    if dbg:
        dbg_mods = nc.dram_tensor("dbg_mods", [128, DEPTH * 48], F32, kind="ExternalOutput")
        dbg_qk = nc.dram_tensor("dbg_qk", [1024, L], BF16, kind="ExternalOutput")
        dbg_v = nc.dram_tensor("dbg_v", [L, 512], BF16, kind="ExternalOutput")
        dbg_u = nc.dram_tensor("dbg_u", [512, L], BF16, kind="ExternalOutput")
        dbg_o = nc.dram_tensor("dbg_o", [D, L], F32, kind="ExternalOutput")
        dbg_x1 = nc.dram_tensor("dbg_x1", [D, L], F32, kind="ExternalOutput")

    xr = [R() for _ in range(NT)]
    qkr = [R() for _ in range(8)]
    vr = R()
    ur = [R() for _ in range(4)]
    ors = [R() for _ in range(8)]
    acr = [[R() for _ in range(NFF)] for _ in range(NT)]

    with ExitStack() as top:
        _uid = [0]

        def sb(name, shape, dt, stack=top):
            _uid[0] += 1
            return stack.enter_context(nc.sbuf_tensor("%s_%d" % (name, _uid[0]), shape, dt))

        banks = [top.enter_context(nc.psum_tensor("ps%d" % i, [128, 512], F32)) for i in range(8)]
        bR = [R(excl=True) for _ in range(8)]

        ones_bf = sb("ones_bf", [128, 128], BF16)
        blk_bf = sb("blk_bf", [128, 128], BF16)
        negU = sb("negU", [128, 128], BF16)
        negOnes = sb("negOnes", [128, 128], BF16)
        masks = sb("masks", [128, 4, 512], BF16)
        M8 = sb("M8", [128, 8], F32)
        M2 = sb("M2", [128, 4, 8], F32)
        iot = sb("iot", [128, 520], F32)
        mods = sb("mods", [128, DEPTH, 48], F32)
        cR = R()
        modsR = R()
        with ExitStack() as ph:
            tmpi = sb("tmpi", [128, 2048], I32, ph)
            tmpf = sb("tmpf", [128, 2048], F32, ph)
            tmpg = sb("tmpg", [128, 2048], F32, ph)
            S.op("pool", lambda e: e.memset(ones_bf[:], 1.0), Wr=[cR])
            S.op("pool", lambda e: e.memset(negOnes[:], -1.0), Wr=[cR])
            S.op("pool", lambda e: e.memset(blk_bf[:], 0.0), Wr=[cR])
            S.op("pool", lambda e: e.memset(blk_bf[0:64, 0:64], 1.0), Wr=[cR])
            S.op("pool", lambda e: e.memset(blk_bf[64:128, 64:128], 1.0), Wr=[cR])
            S.op("pool", lambda e: e.iota(tmpi[:, 0:128], [[-1, 128]], base=0, channel_multiplier=1), Wr=[cR])
            S.op("dve", lambda e: e.tensor_copy(tmpf[:, 0:128], tmpi[:, 0:128]), Rd=[cR], Wr=[cR])
            S.op("dve", lambda e: e.tensor_scalar(negU[:], tmpf[:, 0:128], 0.0, -1.0, ALU.is_ge, ALU.mult), Rd=[cR], Wr=[cR])
            S.op("pool", lambda e: e.iota(tmpi[:, 0:2048], [[-128, 4], [1, 512]], base=0, channel_multiplier=-1), Wr=[cR])
            S.op("dve", lambda e: e.tensor_copy(tmpf[:, 0:2048], tmpi[:, 0:2048]), Rd=[cR], Wr=[cR])
            S.op("dve", lambda e: e.tensor_scalar(masks[:].rearrange("p o f -> p (o f)"), tmpf[:, 0:2048], 0.0, None, ALU.is_gt), Rd=[cR], Wr=[cR])
            S.op("pool", lambda e: e.iota(tmpi[:, 0:8], [[-16, 8]], base=0, channel_multiplier=1), Wr=[cR])
            S.op("dve", lambda e: e.tensor_copy(tmpf[:, 0:8], tmpi[:, 0:8]), Rd=[cR], Wr=[cR])
            S.op("dve", lambda e: e.tensor_scalar(tmpg[:, 0:8], tmpf[:, 0:8], 0.0, None, ALU.is_ge), Rd=[cR], Wr=[cR])
            S.op("dve", lambda e: e.tensor_scalar(tmpf[:, 0:8], tmpf[:, 0:8], 15.5, None, ALU.is_le), Rd=[cR], Wr=[cR])
            S.op("dve", lambda e: e.tensor_tensor(M8[:], tmpf[:, 0:8], tmpg[:, 0:8], ALU.mult), Rd=[cR], Wr=[cR])
            for g2 in range(2):
                S.op("pool", lambda e, g2=g2: e.iota(tmpi[64 * g2:64 * g2 + 64, 0:32], [[-2, 4], [1, 8]], base=-g2, channel_multiplier=0), Wr=[cR])
            S.op("dve", lambda e: e.tensor_copy(tmpf[:, 0:32], tmpi[:, 0:32]), Rd=[cR], Wr=[cR])
            S.op("dve", lambda e: e.tensor_scalar(M2[:].rearrange("p a b -> p (a b)"), tmpf[:, 0:32], 0.0, None, ALU.is_equal), Rd=[cR], Wr=[cR])
            S.op("pool", lambda e: e.iota(tmpi[:, 0:520], [[1, 520]], base=0, channel_multiplier=0), Wr=[cR])
            S.op("dve", lambda e: e.tensor_copy(iot[:], tmpi[:, 0:520]), Rd=[cR], Wr=[cR])

            dump("M8", M8[:], [cR]); dump("M2", M2[:].rearrange("p a b -> p (a b)"), [cR]); dump("iot", iot[:], [cR])
            cT = sb("cT_sb", [128, KC], F32, ph)
            cact = sb("cact", [128, KC], F32, ph)
            abT = sb("abT", [128, DEPTH, 48], F32, ph)
            awb = [sb("awb%d" % i, [128, KC, 512], F32, ph) for i in range(2)]
            awR = [R(), R()]
            S.dma("sp", cT[:], cT_in[:, :], Wr=[cR])
            for l in range(n_layers):
                S.dma("sp", abT[:, l, :], par_in[l, :, _cols["ada_b"][0]:_cols["ada_b"][1]], Wr=[cR])
            S.op("act", lambda e: e.activation(out=cact[:], in_=cT[:], func=AF.Silu), Rd=[cR], Wr=[cR])
            it = 0
            for l in range(n_layers):
                for jb in range(12):
                    b = it % 2
                    it += 1
                    S.dma("sp", awb[b][:], adaw_in[l, :, jb * 512:(jb + 1) * 512].rearrange("(kc p) n -> p kc n", p=128), Wr=[awR[b]])
                    for jj in range(4):
                        col = l * 48 + jb * 4 + jj
                        for kc in range(KC):
                            S.op("pe", lambda e, b=b, jj=jj, kc=kc, col=col: e.matmul(
                                banks[0][:, col:col + 1], awb[b][:, kc, jj * 128:(jj + 1) * 128], cact[:, kc:kc + 1],
                                start=(kc == 0), stop=(kc == KC - 1)), Rd=[awR[b], cR], Wr=[bR[0]])
            S.op("dve", lambda e: e.tensor_tensor(mods[:, 0:n_layers, :].rearrange("p l j -> p (l j)"), banks[0][:, 0:n_layers * 48],
                                                  abT[:, 0:n_layers, :].rearrange("p l j -> p (l j)"), ALU.add), Rd=[bR[0], cR], Wr=[modsR])
            if dbg:
                S.dma("sp", dbg_mods[:, 0:n_layers * 48], mods[:, 0:n_layers, :].rearrange("p l j -> p (l j)"), Rd=[modsR])
            S.barrier()

        for l in range(n_layers):
            x_src = xT_in if l == 0 else x_scr
            with ExitStack() as lay:
                par = sb("par_sb", [128, NPAR], F32, lay)
                parR = R()
                S.dma("sp", par[:], par_in[l, :, :], Wr=[parR])

                def pc(name, a=None, b=None):
                    lo, hi = _cols[name]
                    if a is None:
                        return par[:, lo:hi]
                    return par[:, lo + a:lo + (b if b is not None else a + 1)]

                der = sb("der", [128, 32], F32, lay)
                derR = R()
                S.op("dve", lambda e: e.scalar_tensor_tensor(der[:, 0:8], mods[:, l, 8:16], 1.0, pc("n1g"), ALU.add, ALU.mult), Rd=[modsR, parR], Wr=[derR])
                S.op("dve", lambda e: e.scalar_tensor_tensor(der[:, 8:16], mods[:, l, 32:40], 1.0, pc("n2g"), ALU.add, ALU.mult), Rd=[modsR, parR], Wr=[derR])
                S.op("dve", lambda e: e.tensor_scalar(der[:, 16:17], pc("qg"), 0.125, None, ALU.mult), Rd=[parR], Wr=[derR])
                gs1 = lambda kc: der[:, kc:kc + 1]
                gs2 = lambda kc: der[:, 8 + kc:9 + kc]
                gq = der[:, 16:17]
                sh1 = lambda kc: mods[:, l, kc:kc + 1]
                g1 = lambda kc: mods[:, l, 16 + kc:17 + kc]
                sh2 = lambda kc: mods[:, l, 24 + kc:25 + kc]
                g2 = lambda kc: mods[:, l, 40 + kc:41 + kc]

                with ExitStack() as ph:
                    win = sb("win", [128, KC, 2048], BF16, ph)
                    winR = [R() for _ in range(8)]
                    wst = [sb("wstA%d" % i, [128, KC, 256], F32, ph) for i in range(2)]
                    wstR = [R(), R()]
                    xt = [sb("xtA%d" % i, [128, KC, TT], F32, ph) for i in range(2)]
                    xtR = [R(), R()]
                    sqb = sb("sqbA", [128, KC, TT], BF16, ph)
                    sqR = R()
                    std = sb("stdA", [128, TT], F32, ph)
                    stdR = R()
                    xn = [sb("xnA%d" % i, [128, TT], F32, ph) for i in range(2)]
                    xnR = [R(), R()]
                    hb = [sb("hbA%d" % i, [128, KC, TT], BF16, ph) for i in range(2)]
                    hR = [R(), R()]
                    stage = [sb("stgA%d" % i, [128, 16, TT], BF16, ph) for i in range(2)]
                    stgR = [R(), R()]
                    sq2 = [sb("sq2A%d" % i, [128, TT], BF16, ph) for i in range(3)]
                    sq2R = [R(), R(), R()]
                    std2 = [sb("std2A%d" % i, [128, TT], F32, ph) for i in range(3)]
                    std2R = [R(), R(), R()]
                    mcnt = [0]

                    def a_load(tt):
                        b = tt % 2
                        ts = slice(tt * TT, (tt + 1) * TT)
                        S.dma("sp", xt[b][:], x_src[:, ts].rearrange("(kc p) t -> p kc t", p=128), Rd=([xr[tt]] if l > 0 else []), Wr=[xtR[b]])

                    def a_pro(tt):
                        b = tt % 2
                        S.op("act", lambda e: e.activation(out=sqb[:], in_=xt[b][:], func=AF.Square), Rd=[xtR[b]], Wr=[sqR])
                        for kc in range(KC):
                            S.op("pe", lambda e, kc=kc: e.matmul(banks[0][:], ones_bf[:], sqb[:, kc, :], start=(kc == 0), stop=(kc == KC - 1)), Rd=[sqR, cR], Wr=[bR[0]])
                        S.op("act", lambda e: e.activation(out=std[:], in_=banks[0][:], func=AF.Ln, bias=EPS, scale=1.0 / D), Rd=[bR[0]], Wr=[stdR])
                        S.op("act", lambda e: e.activation(out=std[:], in_=std[:], func=AF.Exp, scale=-0.5), Rd=[stdR], Wr=[stdR])
                        for kc in range(KC):
                            i2 = kc % 2
                            S.op("dve", lambda e, kc=kc: e.tensor_tensor(xn[i2][:], xt[b][:, kc, :], std[:], ALU.mult), Rd=[xtR[b], stdR], Wr=[xnR[i2]])
                            S.op("act", lambda e, kc=kc: e.activation(out=hb[b][:, kc, :], in_=xn[i2][:], func=AF.Identity, bias=sh1(kc), scale=gs1(kc)), Rd=[xnR[i2], derR, modsR], Wr=[hR[b]])

                    a_load(0)
                    for pi in range(8):
                        b = pi % 2
                        S.dma("sp", wst[b][:], win_in[l, :, pi * 256:(pi + 1) * 256].rearrange("(kc p) n -> p kc n", p=128), Wr=[wstR[b]])
                        S.op("act", lambda e, b=b, pi=pi: e.copy(win[:, :, pi * 256:(pi + 1) * 256], wst[b][:]), Rd=[wstR[b]], Wr=[winR[pi]])
                        if pi == 1:
                            a_load(1)
                    a_pro(0)
                    for tt in range(NT):
                        b = tt % 2
                        ts = slice(tt * TT, (tt + 1) * TT)
                        hbt = hb[b]
                        for c in range(8):
                            pb = 1 + (mcnt[0] % 3)
                            mcnt[0] += 1
                            hbk = 4 + (c % 2)
                            i2 = c % 3
                            for kc in range(KC):
                                S.op("pe", lambda e, kc=kc, c=c, pb=pb: e.matmul(banks[pb][:], win[:, kc, c * 128:(c + 1) * 128], hbt[:, kc, :], start=(kc == 0), stop=(kc == KC - 1)), Rd=[winR[c // 2], hR[b]], Wr=[bR[pb]])
                            S.op("act", lambda e, pb=pb, i2=i2: e.activation(out=sq2[i2][:], in_=banks[pb][:], func=AF.Square), Rd=[bR[pb]], Wr=[sq2R[i2]])
                            S.op("pe", lambda e, hbk=hbk, i2=i2: e.matmul(banks[hbk][:], blk_bf[:], sq2[i2][:], start=True, stop=True), Rd=[sq2R[i2], cR], Wr=[bR[hbk]])
                            S.op("act", lambda e, hbk=hbk, i2=i2: e.activation(out=std2[i2][:], in_=banks[hbk][:], func=AF.Ln, bias=EPS, scale=1.0 / 64), Rd=[bR[hbk]], Wr=[std2R[i2]])
                            S.op("act", lambda e, i2=i2: e.activation(out=std2[i2][:], in_=std2[i2][:], func=AF.Exp, scale=-0.5), Rd=[std2R[i2]], Wr=[std2R[i2]])
                            gcol = gq if c < 4 else pc("kg")
                            S.op("dve", lambda e, pb=pb, i2=i2, c=c, b=b, gcol=gcol: e.scalar_tensor_tensor(stage[b][:, c, :], banks[pb][:], gcol, std2[i2][:], ALU.mult, ALU.mult),
                                 Rd=[bR[pb], std2R[i2], derR, parR], Wr=[stgR[b]])
                        if tt + 2 < NT:
                            a_load(tt + 2) if False else None
                        for s4 in range(4):
                            pb = 6 + (s4 % 2)
                            for kc in range(KC):
                                S.op("pe", lambda e, kc=kc, s4=s4, pb=pb: e.matmul(banks[pb][:], hbt[:, kc, s4 * 128:(s4 + 1) * 128], win[:, kc, 1024:1536], start=(kc == 0), stop=(kc == KC - 1)), Rd=[winR[4], winR[5], hR[b]], Wr=[bR[pb]])
                            S.op("act", lambda e, pb=pb, s4=s4, b=b: e.copy(stage[b][:, 8 + s4, :], banks[pb][:]), Rd=[bR[pb]], Wr=[stgR[b]])
                        if tt + 1 < NT:
                            a_pro(tt + 1)
                        for c in range(4):
                            pb = 1 + (mcnt[0] % 3)
                            mcnt[0] += 1
                            for kc in range(KC):
                                S.op("pe", lambda e, kc=kc, c=c, pb=pb: e.matmul(banks[pb][:], win[:, kc, 1536 + c * 128:1536 + (c + 1) * 128], hbt[:, kc, :], start=(kc == 0), stop=(kc == KC - 1)), Rd=[winR[6 + c // 2], hR[b]], Wr=[bR[pb]])
                            S.op("dve", lambda e, pb=pb, c=c, b=b: e.tensor_copy(stage[b][:, 12 + c, :], banks[pb][:]), Rd=[bR[pb]], Wr=[stgR[b]])
                        if tt + 2 < NT:
                            a_load(tt + 2)
                        S.dma("pool", qk_scr[:, ts].rearrange("(c p) t -> p c t", p=128), stage[b][:, 0:8, :], Rd=[stgR[b]], Wr=qkr)
                        S.dma("pool", v_scr[tt * TT:(tt + 1) * TT, :].rearrange("(s p) f -> p s f", p=128), stage[b][:, 8:12, :], Rd=[stgR[b]], Wr=[vr])
                        S.dma("pool", u_scr[:, ts].rearrange("(c p) t -> p c t", p=128), stage[b][:, 12:16, :], Rd=[stgR[b]], Wr=ur)
                    S.barrier()

                with ExitStack() as ph:
                    qT = [sb("qTB%d" % i, [128, L], BF16, ph) for i in range(2)]
                    kT = [sb("kTB%d" % i, [128, L], BF16, ph) for i in range(2)]
                    vt = [sb("vtB%d" % i, [128, 32, 128], BF16, ph) for i in range(2)]
                    ldR = [R(), R()]
                    ost = [sb("ostB%d" % i, [128, L], F32, ph) for i in range(2)]
                    ostR = [R(), R()]
                    NB3 = 3
                    eb = [sb("ebB%d" % i, [128, TT], F32, ph) for i in range(2)]
                    lb = [sb("lbB%d" % i, [128, TT], BF16, ph) for i in range(NB3)]
                    wb = [sb("wbB%d" % i, [128, TT], BF16, ph) for i in range(NB3)]
                    Ls = [sb("LsB%d" % i, [128, TT], BF16, ph) for i in range(2)]
                    eR = [R(), R()]; lR = [R() for _ in range(NB3)]; wR = [R() for _ in range(NB3)]; LsR = [R(), R()]
                    steps = []
                    nqt = nq_tiles if "B" in phases else 0
                    sweep = 0
                    for hp in range(4):
                        for hh in range(2):
                            for qi in range(nqt):
                                nk = 4 * (qi + 1)
                                lcur = 0
                                for idx, kt in enumerate(range(nk - 1, -1, -1)):
                                    st = dict(hp=hp, b=hp % 2, po=64 * hh, qi=qi, kt=kt, idx=idx, nk=nk, ob=4 + (sweep % 2),
                                              diag=(kt >= 4 * qi), o_=kt - 4 * qi, first_of_pair=(hh == 0 and qi == 0 and idx == 0),
                                              last_of_pair=(hh == 1 and qi == nqt - 1 and idx == nk - 1))
                                    st["ls_read"] = lcur if idx > 0 else None
                                    if idx < nk - 1:
                                        st["ls_write"] = 1 - lcur
                                        lcur = 1 - lcur
                                    else:
                                        st["ls_write"] = None
                                    steps.append(st)
                                sweep += 1
                    NS = len(steps)

                    def loads(hp):
                        b = hp % 2
                        S.dma("sp", qT[b][:], qk_scr[hp * 128:(hp + 1) * 128, :], Rd=[qkr[hp]], Wr=[ldR[b]])
                        S.dma("sp", kT[b][:], qk_scr[512 + hp * 128:512 + (hp + 1) * 128, :], Rd=[qkr[4 + hp]], Wr=[ldR[b]])
                        S.dma("sp", vt[b][:], v_scr[:, hp * 128:(hp + 1) * 128].rearrange("(kt p) f -> p kt f", p=128), Rd=[vr], Wr=[ldR[b]])

                    def stA(i):
                        st = steps[i]; b = st["b"]; po = st["po"]; a = i % 4
                        if st["first_of_pair"]:
                            if st["hp"] == 0:
                                loads(0)
                            if st["hp"] + 1 < 4:
                                loads(st["hp"] + 1)
                        ks = slice(st["kt"] * 128, (st["kt"] + 1) * 128); qs = slice(st["qi"] * TT, (st["qi"] + 1) * TT)
                        S.op("pe", lambda e: e.matmul(banks[a][:], kT[b][po:po + 64, ks], qT[b][po:po + 64, qs], start=True, stop=False), Rd=[ldR[b]], Wr=[bR[a]])

                    def stL(i):
                        st = steps[i]; a = i % 4; e2 = i % 2; l3 = i % NB3
                        S.op("act", lambda e: e.activation(out=eb[e2][:], in_=banks[a][:], func=AF.Exp), Rd=[bR[a]], Wr=[eR[e2]])
                        S.op("act", lambda e: e.activation(out=lb[l3][:], in_=eb[e2][:], func=AF.Ln, bias=1.0), Rd=[eR[e2]], Wr=[lR[l3]])
                        if st["diag"]:
                            S.op("dve", lambda e: e.tensor_tensor(lb[l3][:], lb[l3][:], masks[:, st["o_"], :], ALU.mult), Rd=[lR[l3], cR], Wr=[lR[l3]])

                    def stB(i):
                        st = steps[i]; a = i % 4; l3 = i % NB3
                        lr_, lw_ = st["ls_read"], st["ls_write"]
                        S.op("pe", lambda e: e.matmul(banks[a][:], negU[:], lb[l3][:], start=False, stop=(lr_ is None)), Rd=[lR[l3], cR], Wr=[bR[a]])
                        if lr_ is not None:
                            S.op("pe", lambda e: e.matmul(banks[a][:], negOnes[:], Ls[lr_][:], start=False, stop=True), Rd=[LsR[lr_], cR], Wr=[bR[a]])
                        if lw_ is not None:
                            if lr_ is None:
                                S.op("pool", lambda e: e.tensor_copy(Ls[lw_][:], lb[l3][:]), Rd=[lR[l3]], Wr=[LsR[lw_]])
                            else:
                                S.op("pool", lambda e: e.tensor_tensor(Ls[lw_][:], Ls[lr_][:], lb[l3][:], ALU.add), Rd=[lR[l3], LsR[lr_]], Wr=[LsR[lw_]])
                        S.op("act", lambda e: e.activation(out=wb[l3][:], in_=banks[a][:], func=AF.Exp), Rd=[bR[a]], Wr=[wR[l3]])
                        if st["diag"]:
                            S.op("dve", lambda e: e.tensor_tensor(wb[l3][:], wb[l3][:], masks[:, st["o_"], :], ALU.mult), Rd=[wR[l3], cR], Wr=[wR[l3]])

                    def stO(i):
                        st = steps[i]; b = st["b"]; po = st["po"]; l3 = i % NB3; ob = st["ob"]; kt = st["kt"]
                        S.op("pe", lambda e: e.matmul(banks[ob][po:po + 64, :], vt[b][:, kt, po:po + 64], wb[l3][:], start=(st["idx"] == 0), stop=(st["idx"] == st["nk"] - 1)),
                             Rd=[wR[l3], ldR[b]], Wr=[bR[ob]])
                        if st["idx"] == st["nk"] - 1:
                            qs = slice(st["qi"] * TT, (st["qi"] + 1) * TT)
                            S.op("dve", lambda e: e.tensor_copy(ost[b][po:po + 64, qs], banks[ob][po:po + 64, :]), Rd=[bR[ob]], Wr=[ostR[b]])
                        if st["last_of_pair"]:
                            hp = st["hp"]
                            S.dma("pool", o_scr[hp * 128:(hp + 1) * 128, :], ost[b][:], Rd=[ostR[b]], Wr=[ors[hp]])

                    for slot in range(-2, NS + 1):
                        if 0 <= slot + 2 < NS:
                            stA(slot + 2)
                        if 0 <= slot + 1 < NS:
                            stL(slot + 1)
                        if 0 <= slot < NS:
                            stB(slot)
                        if 0 <= slot - 1 < NS:
                            stO(slot - 1)
                    S.barrier()

                with ExitStack() as ph:
                    def ftile(name, n=256):
                        return sb(name, [128, n], F32, ph)
                    zR = R()
                    dtB = ftile("dtB", 4)
                    lre = ftile("lre"); lim = ftile("lim"); mag = ftile("mag"); cosl = ftile("cosl"); sinl = ftile("sinl")
                    ta = ftile("ta"); tb = ftile("tb"); tcc = ftile("tcc"); td = ftile("td")
                    tiq = sb("tiq", [128, 520], I32, ph)
                    tq1 = ftile("tq1", 520); tq2 = ftile("tq2", 520); tq3 = ftile("tq3", 520)
                    INV2PI = 1.0 / (2 * PI)

                    def frac_turns(eng, t, n):
                        S.op(eng, lambda e: e.tensor_copy(tiq[:, 0:n], t), Rd=[zR], Wr=[zR])
                        S.op(eng, lambda e: e.tensor_copy(tq1[:, 0:n], tiq[:, 0:n]), Rd=[zR], Wr=[zR])
                        S.op(eng, lambda e: e.tensor_tensor(t, t, tq1[:, 0:n], ALU.subtract), Rd=[zR], Wr=[zR])
                        S.op(eng, lambda e: e.tensor_scalar(tq1[:, 0:n], t, 0.5, None, ALU.is_gt), Rd=[zR], Wr=[zR])
                        S.op(eng, lambda e: e.tensor_tensor(t, t, tq1[:, 0:n], ALU.subtract), Rd=[zR], Wr=[zR])
                        S.op(eng, lambda e: e.tensor_scalar(tq1[:, 0:n], t, -0.5, None, ALU.is_lt), Rd=[zR], Wr=[zR])
                        S.op(eng, lambda e: e.tensor_tensor(t, t, tq1[:, 0:n], ALU.add), Rd=[zR], Wr=[zR])

                    def sincos_turns(eng, sin_out, cos_out, t, n):
                        S.op("act", lambda e: e.activation(out=sin_out, in_=t, func=AF.Sin, scale=2 * PI), Rd=[zR], Wr=[zR])
                        S.op(eng, lambda e: e.tensor_scalar(tq2[:, 0:n], t, 0.25, None, ALU.add), Rd=[zR], Wr=[zR])
                        S.op(eng, lambda e: e.tensor_scalar(tq1[:, 0:n], tq2[:, 0:n], 0.5, None, ALU.is_gt), Rd=[zR], Wr=[zR])
                        S.op(eng, lambda e: e.tensor_tensor(tq2[:, 0:n], tq2[:, 0:n], tq1[:, 0:n], ALU.subtract), Rd=[zR], Wr=[zR])
                        S.op("act", lambda e: e.activation(out=cos_out, in_=tq2[:, 0:n], func=AF.Sin, scale=2 * PI), Rd=[zR], Wr=[zR])

                    S.op("act", lambda e: e.activation(out=dtB[:], in_=pc("ldtB"), func=AF.Exp), Rd=[parR], Wr=[zR])
                    for k in range(4):
                        ksl = slice(k * 64, (k + 1) * 64)
                        S.op("dve", lambda e, k=k, ksl=ksl: e.tensor_scalar(lre[:, ksl], pc("aBre")[:, ksl], dtB[:, k:k + 1], None, ALU.mult), Rd=[parR, zR], Wr=[zR])
                        S.op("dve", lambda e, k=k, ksl=ksl: e.tensor_scalar(lim[:, ksl], pc("aBim")[:, ksl], dtB[:, k:k + 1], INV2PI, ALU.mult, ALU.mult), Rd=[parR, zR], Wr=[zR])
                    S.op("act", lambda e: e.activation(out=mag[:], in_=lre[:], func=AF.Exp), Rd=[zR], Wr=[zR])
                    frac_turns("dve", lim[:], 256)
                    sincos_turns("dve", sinl[:], cosl[:], lim[:], 256)
                    S.op("dve", lambda e: e.tensor_tensor(ta[:], mag[:], cosl[:], ALU.mult), Rd=[zR], Wr=[zR])
                    S.op("dve", lambda e: e.tensor_scalar(ta[:], ta[:], -1.0, None, ALU.add), Rd=[zR], Wr=[zR])
                    S.op("dve", lambda e: e.tensor_tensor(tb[:], mag[:], sinl[:], ALU.mult), Rd=[zR], Wr=[zR])
                    S.op("dve", lambda e: e.tensor_tensor(mag[:], pc("aBre"), pc("aBre"), ALU.mult), Rd=[zR, parR], Wr=[zR])
                    S.op("dve", lambda e: e.tensor_tensor(cosl[:], pc("aBim"), pc("aBim"), ALU.mult), Rd=[zR, parR], Wr=[zR])
                    S.op("dve", lambda e: e.tensor_tensor(mag[:], mag[:], cosl[:], ALU.add), Rd=[zR], Wr=[zR])
                    S.op("dve", lambda e: e.reciprocal(mag[:], mag[:]), Rd=[zR], Wr=[zR])
                    S.op("dve", lambda e: e.tensor_tensor(tcc[:], ta[:], pc("aBre"), ALU.mult), Rd=[zR, parR], Wr=[zR])
                    S.op("dve", lambda e: e.tensor_tensor(cosl[:], tb[:], pc("aBim"), ALU.mult), Rd=[zR, parR], Wr=[zR])
                    S.op("dve", lambda e: e.tensor_tensor(tcc[:], tcc[:], cosl[:], ALU.add), Rd=[zR], Wr=[zR])
                    S.op("dve", lambda e: e.tensor_tensor(tcc[:], tcc[:], mag[:], ALU.mult), Rd=[zR], Wr=[zR])
                    S.op("dve", lambda e: e.tensor_tensor(td[:], tb[:], pc("aBre"), ALU.mult), Rd=[zR, parR], Wr=[zR])
                    S.op("dve", lambda e: e.tensor_tensor(cosl[:], ta[:], pc("aBim"), ALU.mult), Rd=[zR, parR], Wr=[zR])
                    S.op("dve", lambda e: e.tensor_tensor(td[:], td[:], cosl[:], ALU.subtract), Rd=[zR], Wr=[zR])
                    S.op("dve", lambda e: e.tensor_tensor(td[:], td[:], mag[:], ALU.mult), Rd=[zR], Wr=[zR])
                    S.op("dve", lambda e: e.tensor_tensor(ta[:], tcc[:], pc("BreB"), ALU.mult), Rd=[zR, parR], Wr=[zR])
                    S.op("dve", lambda e: e.tensor_tensor(cosl[:], td[:], pc("BimB"), ALU.mult), Rd=[zR, parR], Wr=[zR])
                    S.op("dve", lambda e: e.tensor_tensor(ta[:], ta[:], cosl[:], ALU.subtract), Rd=[zR], Wr=[zR])
                    S.op("dve", lambda e: e.tensor_tensor(tb[:], tcc[:], pc("BimB"), ALU.mult), Rd=[zR, parR], Wr=[zR])
                    S.op("dve", lambda e: e.tensor_tensor(cosl[:], td[:], pc("BreB"), ALU.mult), Rd=[zR, parR], Wr=[zR])
                    S.op("dve", lambda e: e.tensor_tensor(tb[:], tb[:], cosl[:], ALU.add), Rd=[zR], Wr=[zR])
                    BpR = sb("BpR", [128, 4, 4, 128], BF16, ph)
                    BpI = sb("BpI", [128, 4, 4, 128], BF16, ph)
                    CpR = sb("CpR", [128, 4, 4, 128], BF16, ph)
                    CpI = sb("CpI", [128, 4, 4, 128], BF16, ph)
                    Gbd = sb("Gbd", [128, 4, 128], BF16, ph)
                    wtR = R()
                    S.op("pool", lambda e: e.memset(CpR[:], 0.0), Wr=[wtR])
                    S.op("pool", lambda e: e.memset(CpI[:], 0.0), Wr=[wtR])
                    for k in range(4):
                        ksl = slice(k * 64, (k + 1) * 64)
                        for j2 in range(4):
                            for g2_ in range(2):
                                col = 2 * j2 + g2_
                                S.op("dve", lambda e, k=k, j2=j2, g2_=g2_, col=col, ksl=ksl: e.tensor_scalar(BpR[:, k, j2, g2_ * 64:(g2_ + 1) * 64], ta[:, ksl], M8[:, col:col + 1], None, ALU.mult), Rd=[zR, cR], Wr=[wtR])
                                S.op("dve", lambda e, k=k, j2=j2, g2_=g2_, col=col, ksl=ksl: e.tensor_scalar(BpI[:, k, j2, g2_ * 64:(g2_ + 1) * 64], tb[:, ksl], M8[:, col:col + 1], None, ALU.mult), Rd=[zR, cR], Wr=[wtR])
                                cs = slice((k * 4 + j2) * 16, (k * 4 + j2) * 16 + 16)
                                S.op("dve", lambda e, k=k, j2=j2, col=col, cs=cs: e.tensor_scalar(CpR[:, k, j2, col * 16:(col + 1) * 16], pc("CreS")[:, cs], M2[:, j2, col:col + 1], None, ALU.mult), Rd=[parR, cR], Wr=[wtR])
                                S.op("dve", lambda e, k=k, j2=j2, col=col, cs=cs: e.tensor_scalar(CpI[:, k, j2, col * 16:(col + 1) * 16], pc("CimS")[:, cs], M2[:, j2, col:col + 1], -1.0, ALU.mult, ALU.mult), Rd=[parR, cR], Wr=[wtR])
                        for g8 in range(8):
                            S.op("dve", lambda e, k=k, g8=g8: e.tensor_scalar(Gbd[:, k, g8 * 16:(g8 + 1) * 16], pc("GS")[:, k * 16:(k + 1) * 16], M8[:, g8:g8 + 1], None, ALU.mult), Rd=[parR, cR], Wr=[wtR])
                    dump("Bbre", ta[:], [zR]); dump("Bbim", tb[:], [zR]); dump("fre", tcc[:], [zR]); dump("fim", td[:], [zR])
                    dump("BpR", BpR[:].rearrange("p a b c -> p (a b c)"), [wtR]); dump("CpR", CpR[:].rearrange("p a b c -> p (a b c)"), [wtR])
                    dump("CpI", CpI[:].rearrange("p a b c -> p (a b c)"), [wtR]); dump("Gbd", Gbd[:].rearrange("p a b -> p (a b)"), [wtR])
                    dtS = ftile("dtS", 16); rS = ftile("rS", 16); thS = ftile("thS", 16)
                    S.op("act", lambda e: e.activation(out=dtS[:], in_=pc("ldtS"), func=AF.Exp), Rd=[parR], Wr=[zR])
                    S.op("dve", lambda e: e.tensor_tensor(rS[:], dtS[:], pc("aSre"), ALU.mult), Rd=[zR, parR], Wr=[zR])
                    S.op("act", lambda e: e.activation(out=rS[:], in_=rS[:], func=AF.Exp), Rd=[zR], Wr=[zR])
                    S.op("dve", lambda e: e.tensor_tensor(thS[:], dtS[:], pc("aSim"), ALU.mult), Rd=[zR, parR], Wr=[zR])
                    S.op("dve", lambda e: e.tensor_scalar(thS[:], thS[:], INV2PI, None, ALU.mult), Rd=[zR], Wr=[zR])
                    frac_turns("dve", thS[:], 16)

                    dump("rS", rS[:], [zR]); dump("thS", thS[:], [zR])
                    CpRn = sb("CpRn", [128, 4, 4, 128], BF16, ph)
                    S.op("pool", lambda e: e.tensor_scalar(CpRn[:].rearrange("p a b c -> p (a b c)"), CpR[:].rearrange("p a b c -> p (a b c)"), -1.0, None, ALU.mult), Rd=[wtR], Wr=[wtR])
                    uT = [sb("uTC%d" % i, [128, L], BF16, ph) for i in range(2)]
                    uR = [R(), R()]
                    cosT = [[sb("cosT%d_%d" % (a, i), [128, 520], BF16, ph) for i in range(4)] for a in range(2)]
                    sinT = [[sb("sinT%d_%d" % (a, i), [128, 520], BF16, ph) for i in range(4)] for a in range(2)]
                    c5s5 = sb("c5s5", [128, 2, 4, 2], F32, ph)
                    rT = [[sb("rT%d_%d" % (a, i), [128, TT], F32, ph) for i in range(4)] for a in range(2)]
                    tabR = [[R() for _ in range(4)] for a in range(2)]
                    ini2 = [sb("ini%d" % a, [128, 4, 2], F32, ph) for a in range(2)]
                    iniR2 = [[R() for _ in range(4)] for a in range(2)]
                    tinyt = sb("tinyt", [128, 8], F32, ph)
                    tinyR = R()
                    NR3 = 3
                    bsr = [sb("bsrC%d" % i, [128, TT], BF16, ph) for i in range(NR3)]
                    bsi = [sb("bsiC%d" % i, [128, TT], BF16, ph) for i in range(NR3)]
                    bsR = [R() for _ in range(NR3)]
                    mm_ = [[sb("mC%d_%d" % (i, q), [128, TT], BF16, ph) for q in range(4)] for i in range(2)]
                    mR = [R(), R()]
                    bpr = [sb("bprC%d" % i, [128, TT], BF16, ph) for i in range(NR3)]
                    bpi = [sb("bpiC%d" % i, [128, TT], BF16, ph) for i in range(NR3)]
                    bpR = [R() for _ in range(NR3)]
                    wre = [sb("wreC%d" % i, [128, TT], F32, ph) for i in range(NR3)]
                    wim = [sb("wimC%d" % i, [128, TT], F32, ph) for i in range(NR3)]
                    wwR = [R() for _ in range(NR3)]
                    wrb = [sb("wrbC%d" % i, [128, TT], BF16, ph) for i in range(NR3)]
                    wib = [sb("wibC%d" % i, [128, TT], BF16, ph) for i in range(NR3)]
                    wbR = [R() for _ in range(NR3)]
                    pp = [[sb("pC%d_%d" % (i, q), [128, TT], BF16, ph) for q in range(4)] for i in range(NR3)]
                    ppR = [R() for _ in range(NR3)]
                    yb = [sb("ybC%d" % i, [128, TT], F32, ph) for i in range(2)]
                    ygb = [sb("ygbC%d" % i, [128, TT], BF16, ph) for i in range(2)]
                    gt = [sb("gtC%d" % i, [128, TT], F32, ph) for i in range(2)]
                    osg = [sb("osC%d" % i, [128, TT], F32, ph) for i in range(2)]
                    ybR = [R(), R()]; ygR = [R(), R()]; gtR = [R(), R()]; osR = [R(), R()]

                    def gen_tables(k, j2):
                        a = k % 2
                        col = k * 4 + j2
                        S.op("dve", lambda e: e.tensor_scalar(tq3[:, 0:520], iot[:, 0:520], thS[:, col:col + 1], None, ALU.mult), Rd=[zR, cR], Wr=[zR])
                        frac_turns("dve", tq3[:, 0:520], 520)
                        S.op("act", lambda e: e.activation(out=sinT[a][j2][:], in_=tq3[:, 0:520], func=AF.Sin, scale=2 * PI), Rd=[zR], Wr=[tabR[a][j2]])
                        S.op("act", lambda e: e.activation(out=c5s5[:, a, j2, 1:2], in_=tq3[:, 512:513], func=AF.Sin, scale=2 * PI), Rd=[zR], Wr=[tabR[a][j2]])
                        S.op("dve", lambda e: e.tensor_scalar(tq2[:, 0:520], tq3[:, 0:520], 0.25, None, ALU.add), Rd=[zR], Wr=[zR])
                        S.op("dve", lambda e: e.tensor_scalar(tq1[:, 0:520], tq2[:, 0:520], 0.5, None, ALU.is_gt), Rd=[zR], Wr=[zR])
                        S.op("dve", lambda e: e.tensor_tensor(tq2[:, 0:520], tq2[:, 0:520], tq1[:, 0:520], ALU.subtract), Rd=[zR], Wr=[zR])
                        S.op("act", lambda e: e.activation(out=cosT[a][j2][:], in_=tq2[:, 0:520], func=AF.Sin, scale=2 * PI), Rd=[zR], Wr=[tabR[a][j2]])
                        S.op("act", lambda e: e.activation(out=c5s5[:, a, j2, 0:1], in_=tq2[:, 512:513], func=AF.Sin, scale=2 * PI), Rd=[zR], Wr=[tabR[a][j2]])
                        S.op("dve", lambda e: e.tensor_scalar(rT[a][j2][:], iot[:, 0:TT], 0.0, rS[:, col:col + 1], ALU.mult, ALU.add), Rd=[zR, cR], Wr=[tabR[a][j2]])

                    ssteps = []
                    for k in range(4):
                        for tt in range(NT if "C" in phases else 0):
                            for j2 in range(4):
                                ssteps.append(dict(k=k, tt=tt, j2=j2, n=len(ssteps)))
                    NSS = len(ssteps)

                    def c_bu(i):
                        st = ssteps[i]; k, tt, j2 = st["k"], st["tt"], st["j2"]; ub = k % 2; s = i % 2
                        ts = slice(tt * TT, (tt + 1) * TT)
                        if tt == 0 and j2 == 0:
                            S.dma("sp", uT[ub][:], u_scr[k * 128:(k + 1) * 128, :], Rd=[ur[k]], Wr=[uR[ub]])
                            if k == 0:
                                for q in range(4):
                                    gen_tables(0, q)
                            for q in range(4):
                                S.op("pool", lambda e, q=q: e.memset(ini2[k % 2][:, q, :], 0.0), Wr=[iniR2[k % 2][q]])
                        if k + 1 < 4 and j2 == 0 and tt in (1, 2, 3, 4):
                            gen_tables(k + 1, tt - 1)
                        S.op("pe", lambda e: e.matmul(banks[s][:], BpR[:, k, j2, :], uT[ub][:, ts], start=True, stop=True), Rd=[wtR, uR[ub]], Wr=[bR[s]])
                        S.op("pe", lambda e: e.matmul(banks[2 + s][:], BpI[:, k, j2, :], uT[ub][:, ts], start=True, stop=True), Rd=[wtR, uR[ub]], Wr=[bR[2 + s]])
                        r3 = i % NR3
                        S.op("act", lambda e: e.copy(bsr[r3][:], banks[s][:]), Rd=[bR[s]], Wr=[bsR[r3]])
                        S.op("act", lambda e: e.copy(bsi[r3][:], banks[2 + s][:]), Rd=[bR[2 + s]], Wr=[bsR[r3]])

                    def c_derot(i):
                        st = ssteps[i]; k, j2 = st["k"], st["j2"]; a = k % 2; r3 = i % NR3; m2 = i % 2
                        cs_, sn_ = cosT[a][j2][:, 0:TT], sinT[a][j2][:, 0:TT]
                        tr = tabR[a][j2]
                        S.op("dve", lambda e: e.tensor_tensor(mm_[m2][0][:], bsr[r3][:], cs_, ALU.mult), Rd=[bsR[r3], tr], Wr=[mR[m2]])
                        S.op("dve", lambda e: e.tensor_tensor(mm_[m2][1][:], bsi[r3][:], sn_, ALU.mult), Rd=[bsR[r3], tr], Wr=[mR[m2]])
                        S.op("dve", lambda e: e.tensor_tensor(mm_[m2][2][:], bsi[r3][:], cs_, ALU.mult), Rd=[bsR[r3], tr], Wr=[mR[m2]])
                        S.op("dve", lambda e: e.tensor_tensor(mm_[m2][3][:], bsr[r3][:], sn_, ALU.mult), Rd=[bsR[r3], tr], Wr=[mR[m2]])

                    def c_derot2(i):
                        r3 = i % NR3; m2 = i % 2
                        S.op("dve", lambda e: e.tensor_tensor(bpr[r3][:], mm_[m2][0][:], mm_[m2][1][:], ALU.add), Rd=[mR[m2]], Wr=[bpR[r3]])
                        S.op("dve", lambda e: e.tensor_tensor(bpi[r3][:], mm_[m2][2][:], mm_[m2][3][:], ALU.subtract), Rd=[mR[m2]], Wr=[bpR[r3]])

                    def c_scan(i):
                        st = ssteps[i]; k, j2 = st["k"], st["j2"]; a = k % 2; r3 = i % NR3
                        tr = tabR[a][j2]
                        ini = ini2[a]; iniR = iniR2[a]
                        S.op("dve", lambda e: e.tensor_tensor_scan(wre[r3][:], rT[a][j2][:], bpr[r3][:], ini[:, j2, 0:1], ALU.mult, ALU.add), Rd=[bpR[r3], tr, iniR[j2]], Wr=[wwR[r3]])
                        S.op("dve", lambda e: e.tensor_tensor_scan(wim[r3][:], rT[a][j2][:], bpi[r3][:], ini[:, j2, 1:2], ALU.mult, ALU.add), Rd=[bpR[r3], tr, iniR[j2]], Wr=[wwR[r3]])
                        S.op("act", lambda e: e.copy(wrb[r3][:], wre[r3][:]), Rd=[wwR[r3]], Wr=[wbR[r3]])
                        S.op("act", lambda e: e.copy(wib[r3][:], wim[r3][:]), Rd=[wwR[r3]], Wr=[wbR[r3]])
                        if st["tt"] < NT - 1:
                            c5, s5 = c5s5[:, a, j2, 0:1], c5s5[:, a, j2, 1:2]
                            wr_e, wi_e = wre[r3][:, 511:512], wim[r3][:, 511:512]
                            S.op("pool", lambda e: e.tensor_tensor(tinyt[:, 0:1], wi_e, s5, ALU.mult), Rd=[wwR[r3], tr, tinyR], Wr=[tinyR])
                            S.op("pool", lambda e: e.tensor_tensor(tinyt[:, 1:2], wr_e, s5, ALU.mult), Rd=[wwR[r3], tr, tinyR], Wr=[tinyR])
                            S.op("pool", lambda e: e.tensor_tensor(tinyt[:, 2:3], wr_e, c5, ALU.mult), Rd=[wwR[r3], tr, tinyR], Wr=[tinyR])
                            S.op("pool", lambda e: e.tensor_tensor(tinyt[:, 3:4], wi_e, c5, ALU.mult), Rd=[wwR[r3], tr, tinyR], Wr=[tinyR])
                            S.op("pool", lambda e: e.tensor_tensor(ini[:, j2, 0:1], tinyt[:, 2:3], tinyt[:, 0:1], ALU.subtract), Rd=[tinyR], Wr=[iniR[j2]])
                            S.op("pool", lambda e: e.tensor_tensor(ini[:, j2, 1:2], tinyt[:, 3:4], tinyt[:, 1:2], ALU.add), Rd=[tinyR], Wr=[iniR[j2]])

                    def c_rot(i):
                        st = ssteps[i]; k, j2 = st["k"], st["j2"]; a = k % 2; r3 = i % NR3
                        cs_, sn_ = cosT[a][j2][:, 0:TT], sinT[a][j2][:, 0:TT]
                        tr = tabR[a][j2]
                        S.op("dve", lambda e: e.tensor_tensor(pp[r3][0][:], wrb[r3][:], cs_, ALU.mult), Rd=[wbR[r3], tr], Wr=[ppR[r3]])
                        S.op("dve", lambda e: e.tensor_tensor(pp[r3][1][:], wib[r3][:], sn_, ALU.mult), Rd=[wbR[r3], tr], Wr=[ppR[r3]])
                        S.op("dve", lambda e: e.tensor_tensor(pp[r3][2][:], wrb[r3][:], sn_, ALU.mult), Rd=[wbR[r3], tr], Wr=[ppR[r3]])
                        S.op("dve", lambda e: e.tensor_tensor(pp[r3][3][:], wib[r3][:], cs_, ALU.mult), Rd=[wbR[r3], tr], Wr=[ppR[r3]])

                    def c_out(i):
                        st = ssteps[i]; k, tt, j2 = st["k"], st["tt"], st["j2"]; ub = k % 2; r3 = i % NR3
                        ts = slice(tt * TT, (tt + 1) * TT)
                        yk = 4 + (tt % 2); gk = 6 + (tt % 2); tb2 = tt % 2
                        for q, wt_ in enumerate((CpR, CpRn, CpI, CpI)):
                            S.op("pe", lambda e, q=q, wt_=wt_: e.matmul(banks[yk][:], wt_[:, k, j2, :], pp[r3][q][:], start=(j2 == 0 and q == 0), stop=(j2 == 3 and q == 3)), Rd=[wtR, ppR[r3]], Wr=[bR[yk]])
                        if j2 == 3:
                            S.op("dve", lambda e: e.scalar_tensor_tensor(yb[tb2][:], uT[ub][:, ts], pc("dsk", k), banks[yk][:], ALU.mult, ALU.add), Rd=[uR[ub], parR, bR[yk]], Wr=[ybR[tb2]])
                            S.op("act", lambda e: e.activation(out=yb[tb2][:], in_=yb[tb2][:], func=AF.Gelu_apprx_tanh), Rd=[ybR[tb2]], Wr=[ybR[tb2]])
                            S.op("act", lambda e: e.copy(ygb[tb2][:], yb[tb2][:]), Rd=[ybR[tb2]], Wr=[ygR[tb2]])
                            S.op("pe", lambda e: e.matmul(banks[gk][:], Gbd[:, k, :], ygb[tb2][:], start=True, stop=True), Rd=[wtR, ygR[tb2]], Wr=[bR[gk]])
                            S.op("act", lambda e: e.activation(out=gt[tb2][:], in_=banks[gk][:], func=AF.Sigmoid, bias=pc("glub", k)), Rd=[bR[gk], parR], Wr=[gtR[tb2]])
                            S.op("pool", lambda e: e.tensor_tensor(osg[tb2][:], yb[tb2][:], gt[tb2][:], ALU.mult), Rd=[ybR[tb2], gtR[tb2]], Wr=[osR[tb2]])
                            S.dma("sp", o_scr[512 + k * 128:512 + (k + 1) * 128, ts], osg[tb2][:], Rd=[osR[tb2]], Wr=[ors[4 + k]])

                    for slot in range(-1, NSS + 3):
                        if 0 <= slot + 1 < NSS:
                            c_bu(slot + 1)
                        if 0 <= slot < NSS:
                            c_derot(slot)
                        if 0 <= slot - 1 < NSS:
                            c_scan(slot - 1)
                        if 0 <= slot - 2 < NSS:
                            c_rot(slot - 2)
                        if 0 <= slot < NSS:
                            c_derot2(slot)
                        if 0 <= slot - 3 < NSS:
                            c_out(slot - 3)
                    S.barrier()

                with ExitStack() as ph:
                    wo = sb("woD", [128, KC, D], BF16, ph)
                    woR = [R() for _ in range(4)]
                    wst = [sb("wstD%d" % i, [128, KC, 256], F32, ph) for i in range(2)]
                    wstR = [R(), R()]
                    ot = [sb("otD%d" % i, [128, KC, TT], F32, ph) for i in range(2)]
                    otR = [R(), R()]
                    xt = [sb("xtD%d" % i, [128, KC, TT], F32, ph) for i in range(2)]
                    xtR = [R(), R()]
                    sqb = sb("sqbD", [128, KC, TT], BF16, ph)
                    sqR = R()
                    stda = sb("stdaD", [128, TT], F32, ph)
                    stds = sb("stdsD", [128, TT], F32, ph)
                    stdR = R()
                    on = [sb("onD%d" % i, [128, KC, TT], BF16, ph) for i in range(2)]
                    onR = [R(), R()]
                    NTD = NT if "D" in phases else 0

                    def d_load(tt):
                        b = tt % 2
                        ts = slice(tt * TT, (tt + 1) * TT)
                        S.dma("sp", ot[b][:], o_scr[:, ts].rearrange("(kc p) t -> p kc t", p=128), Rd=ors, Wr=[otR[b]])
                        S.dma("sp", xt[b][:], x_src[:, ts].rearrange("(kc p) t -> p kc t", p=128), Rd=([xr[tt]] if l > 0 else []), Wr=[xtR[b]])

                    def d_pro(tt):
                        b = tt % 2
                        S.op("act", lambda e: e.activation(out=sqb[:], in_=ot[b][:], func=AF.Square), Rd=[otR[b]], Wr=[sqR])
                        for c in range(8):
                            bk = 0 if c < 4 else 1
                            S.op("pe", lambda e, c=c, bk=bk: e.matmul(banks[bk][:], ones_bf[:], sqb[:, c, :], start=(c % 4 == 0), stop=(c % 4 == 3)), Rd=[sqR, cR], Wr=[bR[bk]])
                        S.op("act", lambda e: e.activation(out=stda[:], in_=banks[0][:], func=AF.Ln, bias=EPS, scale=1.0 / 512), Rd=[bR[0]], Wr=[stdR])
                        S.op("act", lambda e: e.activation(out=stds[:], in_=banks[1][:], func=AF.Ln, bias=EPS, scale=1.0 / 512), Rd=[bR[1]], Wr=[stdR])
                        S.op("act", lambda e: e.activation(out=stda[:], in_=stda[:], func=AF.Exp, scale=-0.5), Rd=[stdR], Wr=[stdR])
                        S.op("act", lambda e: e.activation(out=stds[:], in_=stds[:], func=AF.Exp, scale=-0.5), Rd=[stdR], Wr=[stdR])
                        for c in range(8):
                            gcol = pc("aog", c) if c < 4 else pc("sog", c - 4)
                            sd = stda if c < 4 else stds
                            S.op("dve", lambda e, c=c, gcol=gcol, sd=sd: e.scalar_tensor_tensor(on[b][:, c, :], ot[b][:, c, :], gcol, sd[:], ALU.mult, ALU.mult), Rd=[otR[b], stdR, parR], Wr=[onR[b]])

                    if NTD:
                        d_load(0)
                    for pi in range(4):
                        b = pi % 2
                        S.dma("sp", wst[b][:], wout_in[l, :, pi * 256:(pi + 1) * 256].rearrange("(kc p) n -> p kc n", p=128), Wr=[wstR[b]])
                        S.op("act", lambda e, b=b, pi=pi: e.copy(wo[:, :, pi * 256:(pi + 1) * 256], wst[b][:]), Rd=[wstR[b]], Wr=[woR[pi]])
                    if NTD:
                        d_load(1)
                        d_pro(0)
                    for tt in range(NTD):
                        b = tt % 2
                        ts = slice(tt * TT, (tt + 1) * TT)
                        for co in range(8):
                            bk = 2 + (co % 4)
                            for kc in range(KC):
                                S.op("pe", lambda e, kc=kc, co=co, bk=bk: e.matmul(banks[bk][:], wo[:, kc, co * 128:(co + 1) * 128], on[b][:, kc, :], start=(kc == 0), stop=(kc == KC - 1)), Rd=[woR[co // 2], onR[b]], Wr=[bR[bk]])
                            S.op("dve", lambda e, co=co, bk=bk, b=b: e.scalar_tensor_tensor(xt[b][:, co, :], banks[bk][:], g1(co), xt[b][:, co, :], ALU.mult, ALU.add), Rd=[bR[bk], modsR, xtR[b]], Wr=[xtR[b]])
                            if co == 3 and tt + 1 < NTD:
                                d_pro(tt + 1)
                        S.dma("pool", x_scr[:, ts].rearrange("(kc p) t -> p kc t", p=128), xt[b][:], Rd=[xtR[b]], Wr=[xr[tt]])
                        if tt + 2 < NTD:
                            d_load(tt + 2)
                    S.barrier()

                with ExitStack() as ph:
                    wup = sb("wupE", [128, KC, 2 * DFF], BF16, ph)
                    wupR = [R() for _ in range(22)]
                    wst = [sb("wstE%d" % i, [128, KC, 256], F32, ph) for i in range(2)]
                    wstR = [R(), R()]
                    xt = [sb("xtE%d" % i, [128, KC, TT], F32, ph) for i in range(2)]
                    xtR = [R(), R()]
                    sqb = sb("sqbE", [128, KC, TT], BF16, ph)
                    sqR = R()
                    std = sb("stdE", [128, TT], F32, ph)
                    stdR = R()
                    xn = [sb("xnE%d" % i, [128, TT], F32, ph) for i in range(2)]
                    xnR = [R(), R()]
                    hb = [sb("hbE%d" % i, [128, KC, TT], BF16, ph) for i in range(2)]
                    hR = [R(), R()]
                    hc = [sb("hcE%d" % i, [128, 2 * NFF, 2], F32, ph) for i in range(2)]
                    hcR = [[R() for _ in range(2 * NFF)] for _ in range(2)]
                    hcn = sb("hcnE", [128, 2 * NFF, 2], F32, ph)
                    hcn2 = sb("hcn2E", [128, 2 * NFF, 2], F32, ph)
                    hcnR = [R() for _ in range(2 * NFF)]
                    tv = [sb("tvE%d" % i, [128, TT], F32, ph) for i in range(3)]
                    tg = [sb("tgE%d" % i, [128, TT], F32, ph) for i in range(3)]
                    tvR = [R() for _ in range(3)]; tgR = [R() for _ in range(3)]
                    actb = [sb("actE%d" % i, [128, TT], BF16, ph) for i in range(4)]
                    actR = [R() for _ in range(4)]
                    S.op("pool", lambda e: e.memset(hc[0][:], 0.0), Wr=hcR[0])
                    S.op("pool", lambda e: e.memset(hcn2[:], 0.0), Wr=hcnR)
                    cwc = lambda i, cidx: pc("cw", i * 44 + cidx)
                    NTE = NT if "E" in phases else 0

                    def e_load(tt):
                        b = tt % 2
                        ts = slice(tt * TT, (tt + 1) * TT)
                        S.dma("sp", xt[b][:], x_scr[:, ts].rearrange("(kc p) t -> p kc t", p=128), Rd=[xr[tt]], Wr=[xtR[b]])

                    def e_pro(tt):
                        b = tt % 2
                        S.op("act", lambda e: e.activation(out=sqb[:], in_=xt[b][:], func=AF.Square), Rd=[xtR[b]], Wr=[sqR])
                        for kc in range(KC):
                            S.op("pe", lambda e, kc=kc: e.matmul(banks[0][:], ones_bf[:], sqb[:, kc, :], start=(kc == 0), stop=(kc == KC - 1)), Rd=[sqR, cR], Wr=[bR[0]])
                        S.op("act", lambda e: e.activation(out=std[:], in_=banks[0][:], func=AF.Ln, bias=EPS, scale=1.0 / D), Rd=[bR[0]], Wr=[stdR])
                        S.op("act", lambda e: e.activation(out=std[:], in_=std[:], func=AF.Exp, scale=-0.5), Rd=[stdR], Wr=[stdR])
                        for kc in range(KC):
                            i2 = kc % 2
                            S.op("dve", lambda e, kc=kc: e.tensor_tensor(xn[i2][:], xt[b][:, kc, :], std[:], ALU.mult), Rd=[xtR[b], stdR], Wr=[xnR[i2]])
                            S.op("act", lambda e, kc=kc: e.activation(out=hb[b][:, kc, :], in_=xn[i2][:], func=AF.Identity, bias=sh2(kc), scale=gs2(kc)), Rd=[xnR[i2], derR, modsR], Wr=[hR[b]])

                    if NTE:
                        e_load(0)
                    order = []
                    for q in range(11):
                        order += [q, 11 + q]
                    def e_wload(n_):
                        pi = order[n_]
                        b = n_ % 2
                        S.dma("sp", wst[b][:], wup_in[l, :, pi * 256:(pi + 1) * 256].rearrange("(kc p) n -> p kc n", p=128), Wr=[wstR[b]])
                        S.op("act", lambda e: e.copy(wup[:, :, pi * 256:(pi + 1) * 256], wst[b][:]), Rd=[wstR[b]], Wr=[wupR[pi]])

                    for n_ in range(4):
                        e_wload(n_)
                        if n_ == 1 and NTE:
                            e_load(1)
                            e_pro(0)
                    if not NTE:
                        for n_ in range(4, 22):
                            e_wload(n_)
                    pend = []

                    def e_fin(s, j, tt):
                        ts = slice(tt * TT, (tt + 1) * TT)
                        S.op("act", lambda e: e.activation(out=tg[s][:], in_=tg[s][:], func=AF.Gelu_apprx_tanh), Rd=[tgR[s]], Wr=[tgR[s]])
                        a4 = j % 4
                        S.op("pool", lambda e: e.tensor_tensor(actb[a4][:], tg[s][:], tv[s][:], ALU.mult), Rd=[tgR[s], tvR[s]], Wr=[actR[a4]])
                        S.dma("sp", act_scr[j * 128:(j + 1) * 128, ts], actb[a4][:], Rd=[actR[a4]], Wr=[acr[tt][j]])

                    jstep = 0
                    for tt in range(NTE):
                        b = tt % 2
                        ts = slice(tt * TT, (tt + 1) * TT)
                        for j in range(NFF):
                            s = jstep % 3
                            jstep += 1
                            for which, (cidx, tdst, tdR) in enumerate(((j, tv[s], tvR[s]), (NFF + j, tg[s], tgR[s]))):
                                bk = 1 + 2 * s + which
                                h6 = 2 * s + which
                                for kc in range(KC):
                                    S.op("pe", lambda e, kc=kc, cidx=cidx, bk=bk: e.matmul(banks[bk][:], wup[:, kc, cidx * 128:(cidx + 1) * 128], hb[b][:, kc, :], start=(kc == 0), stop=(kc == KC - 1)), Rd=[wupR[cidx // 2], hR[b]], Wr=[bR[bk]])
                                hp_, hn_ = tt % 2, (tt + 1) % 2
                                S.op("act", lambda e, cidx=cidx, tdst=tdst, bk=bk: e.activation(out=tdst[:], in_=banks[bk][:], func=AF.Identity, bias=pc("cb", cidx), scale=cwc(2, cidx)), Rd=[bR[bk], parR], Wr=[tdR])
                                if tt + 1 < NTE:
                                    S.op("act", lambda e, cidx=cidx, bk=bk: e.activation(out=hcn[:, cidx, 0:2], in_=banks[bk][:, TT - 2:TT], func=AF.Identity, scale=cwc(0, cidx)), Rd=[parR], Wr=[hcnR[cidx], bR[bk]])
                                    S.op("act", lambda e, cidx=cidx, bk=bk: e.activation(out=hcn2[:, cidx, 0:1], in_=banks[bk][:, TT - 1:TT], func=AF.Identity, scale=cwc(1, cidx)), Rd=[parR], Wr=[hcnR[cidx], bR[bk]])
                                S.op("dve", lambda e, cidx=cidx, tdst=tdst, bk=bk: e.scalar_tensor_tensor(tdst[:, 1:TT], banks[bk][:, 0:TT - 1], cwc(1, cidx), tdst[:, 1:TT], ALU.mult, ALU.add), Rd=[parR, tdR], Wr=[tdR, bR[bk]])
                                S.op("dve", lambda e, cidx=cidx, tdst=tdst, bk=bk: e.scalar_tensor_tensor(tdst[:, 2:TT], banks[bk][:, 0:TT - 2], cwc(0, cidx), tdst[:, 2:TT], ALU.mult, ALU.add), Rd=[parR, tdR], Wr=[tdR, bR[bk]])
                                S.op("pool", lambda e, cidx=cidx, tdst=tdst, hp_=hp_: e.tensor_tensor(tdst[:, 0:2], tdst[:, 0:2], hc[hp_][:, cidx, :], ALU.add), Rd=[hcR[hp_][cidx], tdR], Wr=[tdR])
                                if tt + 1 < NTE:
                                    S.op("pool", lambda e, cidx=cidx, hn_=hn_: e.tensor_tensor(hc[hn_][:, cidx, :], hcn[:, cidx, :], hcn2[:, cidx, :], ALU.add), Rd=[hcnR[cidx]], Wr=[hcR[hn_][cidx]])
                            if pend:
                                e_fin(*pend.pop())
                            pend.append((s, j, tt))
                            if tt == 0 and j % 2 == 0 and 4 + j < 22:
                                e_wload(4 + j)
                                e_wload(5 + j)
                            if j == 8 and tt + 1 < NTE:
                                e_pro(tt + 1)
                        if tt + 2 < NTE:
                            e_load(tt + 2)
                    if pend:
                        e_fin(*pend.pop())
                    S.barrier()

                with ExitStack() as ph:
                    wdn = sb("wdnE", [128, NFF, D], BF16, ph)
                    wdnR = [R() for _ in range(11)]
                    wst = [sb("wstF%d" % i, [128, 2, D], F32, ph) for i in range(2)]
                    wstR = [R(), R()]
                    xt = [sb("xtF%d" % i, [128, KC, TT], F32, ph) for i in range(2)]
                    xtR = [R(), R()]
                    actb = [sb("actF%d" % i, [128, NFF, TT], BF16, ph) for i in range(2)]
                    actR = [R(), R()]
                    dst = yT_out if l == n_layers - 1 else x_scr
                    NTE = NT if "F" in phases else 0

                    def f_load(tt):
                        b = tt % 2
                        ts = slice(tt * TT, (tt + 1) * TT)
                        S.dma("sp", actb[b][:], act_scr[:, ts].rearrange("(c p) t -> p c t", p=128), Rd=acr[tt], Wr=[actR[b]])
                        S.dma("sp", xt[b][:], x_scr[:, ts].rearrange("(kc p) t -> p kc t", p=128), Rd=[xr[tt]], Wr=[xtR[b]])

                    if NTE:
                        f_load(0)
                    for pi in range(11):
                        b = pi % 2
                        S.dma("sp", wst[b][:], wdn_in[l, pi * 256:(pi + 1) * 256, :].rearrange("(c p) n -> p c n", p=128), Wr=[wstR[b]])
                        S.op("act", lambda e, b=b, pi=pi: e.copy(wdn[:, 2 * pi:2 * pi + 2, :], wst[b][:]), Rd=[wstR[b]], Wr=[wdnR[pi]])
                    if NTE:
                        f_load(1)
                    for tt in range(NTE):
                        b = tt % 2
                        ts = slice(tt * TT, (tt + 1) * TT)
                        for co in range(8):
                            bk = co % 4
                            for j in range(NFF):
                                S.op("pe", lambda e, j=j, co=co, bk=bk, b=b: e.matmul(banks[bk][:], wdn[:, j, co * 128:(co + 1) * 128], actb[b][:, j, :], start=(j == 0), stop=(j == NFF - 1)), Rd=[wdnR[j // 2], actR[b]], Wr=[bR[bk]])
                            S.op("dve", lambda e, co=co, bk=bk, b=b: e.scalar_tensor_tensor(xt[b][:, co, :], banks[bk][:], g2(co), xt[b][:, co, :], ALU.mult, ALU.add), Rd=[bR[bk], modsR, xtR[b]], Wr=[xtR[b]])
                        S.dma("pool", dst[:, ts].rearrange("(kc p) t -> p kc t", p=128), xt[b][:], Rd=[xtR[b]], Wr=[xr[tt]])
                        if tt + 2 < NTE:
                            f_load(tt + 2)
                    S.barrier()
                S.barrier()
        if dbg:
            with ExitStack() as ph:
                t1 = sb("cp1", [128, 8, 2048], F32, ph)
                tR = R()
                for hhalf in range(2):
                    S.dma("sp", t1[:], o_scr[:, hhalf * 2048:(hhalf + 1) * 2048].rearrange("(c p) t -> p c t", p=128), Rd=ors, Wr=[tR])
                    S.dma("sp", dbg_o[:, hhalf * 2048:(hhalf + 1) * 2048].rearrange("(c p) t -> p c t", p=128), t1[:], Rd=[tR])
                S.barrier()
        S.barrier()
    print("sched: ops=%d waits=%d" % (S.nops, S.nwait))
    return nc


def _pack_params(inp, l):
    P = np.zeros((128, NPAR), np.float32)

    def put(name, arr):
        lo, hi = _cols[name]
        P[:, lo:hi] = np.asarray(arr, np.float32).reshape(128, hi - lo)

    f = lambda k: np.asarray(inp[k][l], np.float32)
    put("ada_b", f("ada_b").reshape(48, 128).T)
    put("n1g", f("norm1_g").reshape(8, 128).T)
    put("n2g", f("norm2_g").reshape(8, 128).T)
    put("qg", np.tile(f("q_norm_g"), 2))
    put("kg", np.tile(f("k_norm_g"), 2))
    put("aog", f("attn_out_g").reshape(4, 128).T)
    put("sog", f("ssm_out_g").reshape(4, 128).T)
    are = f("ssm_a_re").reshape(4, 8, 64)
    aim = f("ssm_a_im").reshape(4, 8, 64)
    ldt = f("ssm_log_dt").reshape(4, 8)
    put("aBre", np.broadcast_to(are.transpose(1, 0, 2)[:, None], (8, 16, 4, 64)))
    put("aBim", np.broadcast_to(aim.transpose(1, 0, 2)[:, None], (8, 16, 4, 64)))
    put("ldtB", np.broadcast_to(ldt.T[:, None], (8, 16, 4)))
    bre = f("ssm_b_re").reshape(4, 8, 64, 16)
    bim = f("ssm_b_im").reshape(4, 8, 64, 16)
    put("BreB", bre.transpose(1, 3, 0, 2))
    put("BimB", bim.transpose(1, 3, 0, 2))
    are2 = f("ssm_a_re").reshape(4, 4, 2, 64)
    aim2 = f("ssm_a_im").reshape(4, 4, 2, 64)
    ldt2 = f("ssm_log_dt").reshape(4, 4, 2)
    put("aSre", are2.transpose(2, 3, 0, 1))
    put("aSim", aim2.transpose(2, 3, 0, 1))
    put("ldtS", np.broadcast_to(ldt2.transpose(2, 0, 1)[:, None], (2, 64, 4, 4)))
    cre = f("ssm_c_re").reshape(4, 4, 2, 16, 64)
    cim = f("ssm_c_im").reshape(4, 4, 2, 16, 64)
    put("CreS", cre.transpose(2, 4, 0, 1, 3))
    put("CimS", cim.transpose(2, 4, 0, 1, 3))
    put("dsk", f("ssm_d").reshape(4, 128).T)
    put("GS", f("glu_w").reshape(4, 8, 16, 16).transpose(1, 2, 0, 3))
    put("glub", f("glu_b").reshape(4, 128).T)
    put("cw", f("ffn_conv_w").reshape(3, 44, 128).transpose(2, 0, 1))
    put("cb", f("ffn_conv_b").reshape(44, 128).T)
    return P


def _in_maps(inp):
    par = np.stack([_pack_params(inp, l) for l in range(DEPTH)], 0)
    shared = {
        "par": par,
        "ada_w": np.ascontiguousarray(inp["ada_w"], np.float32),
        "w_in": np.ascontiguousarray(inp["w_in"], np.float32),
        "w_out": np.ascontiguousarray(inp["w_out"], np.float32),
        "w_up": np.ascontiguousarray(inp["ffn_w_up"], np.float32),
        "w_dn": np.ascontiguousarray(inp["ffn_w_down"], np.float32),
    }
    maps = []
    for b in range(8):
        m = dict(shared)
        m["xT"] = np.ascontiguousarray(np.asarray(inp["x"][b], np.float32).T)
        m["cT"] = np.ascontiguousarray(np.asarray(inp["c"][b], np.float32).reshape(KC, 128).T)
        maps.append(m)
    return maps


def kernel(**inputs):
    inp = {k: np.asarray(v) for k, v in inputs.items()}
    nc = build(DEPTH)
    res = run_bass_kernel_spmd(nc, _in_maps(inp), core_ids=list(range(8)))
    out = np.stack([np.ascontiguousarray(r["yT"].T) for r in res.results], 0)
    return out.astype(np.float32)
```

```python
import numpy as np
from contextlib import ExitStack
import concourse.bass as bass
import concourse.mybir as mybir
from concourse.bass_utils import run_bass_kernel_spmd

F32 = mybir.dt.float32
BF16 = mybir.dt.bfloat16
I32 = mybir.dt.int32
AF = mybir.ActivationFunctionType
ALU = mybir.AluOpType

D = 1024
L = 4096
DEPTH = 4
TT = 512
NT = L // TT
KC = D // 128
DFF = 2816
NFF = DFF // 128
EPS = 1e-6
PI = float(np.pi)

_cols = {}
_off = 0
for _n, _w in [("ada_b", 48), ("n1g", 8), ("n2g", 8), ("qg", 1), ("kg", 1), ("aog", 4), ("sog", 4),
               ("aBre", 256), ("aBim", 256), ("ldtB", 4), ("BreB", 256), ("BimB", 256),
               ("aSre", 16), ("aSim", 16), ("ldtS", 16), ("CreS", 256), ("CimS", 256),
               ("dsk", 4), ("GS", 64), ("glub", 4), ("cw", 132), ("cb", 44)]:
    _cols[_n] = (_off, _off + _w)
    _off += _w
NPAR = _off


class R:
    __slots__ = ("w", "r", "excl")

    def __init__(self, excl=False):
        self.w = None
        self.r = []
        self.excl = excl


class _Cap:
    def __getattr__(self, name):
        return lambda *a, **k: (name, a, k)


_CAP = _Cap()


class Sched:
    ND = 24

    def __init__(self, nc):
        self.nc = nc
        self.eng = {"pe": nc.tensor, "act": nc.scalar, "dve": nc.vector, "pool": nc.gpsimd, "sp": nc.sync}
        self.sem = {e: nc.alloc_semaphore("c_" + e) for e in self.eng}
        self.cnt = {e: 0 for e in self.eng}
        self.seen = {e: {x: 0 for x in self.eng} for e in self.eng}
        self.dsem = [nc.alloc_semaphore("d_%d" % i) for i in range(self.ND)]
        self.dcnt = [0] * self.ND
        self.dseen = {e: [0] * self.ND for e in self.eng}
        self.dnext = 0
        self.nwait = 0
        self.nops = 0
        self.tinyops = set()

    def _wait_tok(self, eng, tok, is_dma=False, tiny=False):
        e = self.eng[eng]
        if tok[0] == "c":
            _, x, idx = tok
            if x == eng:
                if not is_dma:
                    if eng == "pe" or self.cnt[eng] - idx >= 3:
                        return
                    if not tiny and (x, idx) not in self.tinyops:
                        return
            if self.seen[eng][x] >= idx:
                return
            e.wait_ge(self.sem[x], idx)
            self.seen[eng][x] = idx
            self.nwait += 1
        else:
            _, slot, val = tok
            if self.dseen[eng][slot] >= val:
                return
            e.wait_ge(self.dsem[slot], val)
            self.dseen[eng][slot] = val
            self.nwait += 1

    def _deps(self, eng, reads, writes, is_dma=False, tiny=False):
        toks = []
        for r in reads:
            if r.w is not None:
                toks.append(r.w)
        for w in writes:
            if w.w is not None:
                toks.append(w.w)
            toks.extend(w.r)
        for t in toks:
            self._wait_tok(eng, t, is_dma, tiny)

    def _commit(self, tok, reads, writes):
        for r in reads:
            r.r.append(tok)
        for w in writes:
            w.w = tok
            w.r = []

    def op(self, eng, fn, Rd=(), Wr=(), tiny=None):
        if any(r.excl for r in Rd):
            Wr = list(Wr) + [r for r in Rd if r.excl and r not in Wr]
            Rd = [r for r in Rd if not r.excl]
        name, a, k = fn(_CAP)
        if tiny is None:
            o = k.get("out", None)
            if o is None:
                o = k.get("ap", None)
            if o is None:
                o = a[0]
            n = 1
            for d in list(o.shape)[1:]:
                n *= int(d)
            tiny = n < 192
        self._deps(eng, Rd, Wr, False, tiny)
        ins = getattr(self.eng[eng], name)(*a, **k)
        self.cnt[eng] += 1
        if tiny:
            self.tinyops.add((eng, self.cnt[eng]))
        ins.then_inc(self.sem[eng], 1)
        self._commit(("c", eng, self.cnt[eng]), Rd, Wr)
        self.nops += 1
        return ins

    def dma(self, q, out, in_, Rd=(), Wr=()):
        self._deps(q, Rd, Wr, True)
        slot = self.dnext % self.ND
        self.dnext += 1
        if self.dcnt[slot] > 0:
            self._wait_tok(q, ("d", slot, 16 * self.dcnt[slot]))
        ins = self.eng[q].dma_start(out=out, in_=in_)
        self.dcnt[slot] += 1
        ins.then_inc(self.dsem[slot], 16)
        self._commit(("d", slot, 16 * self.dcnt[slot]), Rd, Wr)
        self.nops += 1
        return ins

    def barrier(self, engines=None):
        for e in self.eng:
            for x in self.eng:
                if x != e and self.cnt[x] > 0:
                    self._wait_tok(e, ("c", x, self.cnt[x]))
            for s in range(self.ND):
                if self.dcnt[s] > 0:
                    self._wait_tok(e, ("d", s, 16 * self.dcnt[s]))


def build(n_layers=DEPTH, dbg=False, nq_tiles=NT, phases="ABCDEF", dumps=None):
    nc = bass.Bass("TRN2", target_bir_lowering=False)
    S = Sched(nc)
    _dd = {}

    def dump(name, tile, Rd, cond=True):
        if dumps is None or not cond or name in _dd:
            return
        shp = list(tile.shape)
        dt_ = tile.dtype
        dten = nc.dram_tensor("dump_" + name, shp, dt_, kind="ExternalOutput")
        _dd[name] = dten
        dumps.append("dump_" + name)
        S.dma("sp", dten.ap() if len(shp) == 2 else dten.ap(), tile, Rd=Rd)

    xT_in = nc.dram_tensor("xT", [D, L], F32, kind="ExternalInput")
    cT_in = nc.dram_tensor("cT", [128, KC], F32, kind="ExternalInput")
    par_in = nc.dram_tensor("par", [DEPTH, 128, NPAR], F32, kind="ExternalInput")
    adaw_in = nc.dram_tensor("ada_w", [DEPTH, D, 6 * D], F32, kind="ExternalInput")
    win_in = nc.dram_tensor("w_in", [DEPTH, D, 2048], F32, kind="ExternalInput")
    wout_in = nc.dram_tensor("w_out", [DEPTH, D, D], F32, kind="ExternalInput")
    wup_in = nc.dram_tensor("w_up", [DEPTH, D, 2 * DFF], F32, kind="ExternalInput")
    wdn_in = nc.dram_tensor("w_dn", [DEPTH, DFF, D], F32, kind="ExternalInput")
    yT_out = nc.dram_tensor("yT", [D, L], F32, kind="ExternalOutput")

    x_scr = nc.dram_tensor("x_scr", [D, L], F32, kind="Internal")
    qk_scr = nc.dram_tensor("qk_scr", [1024, L], BF16, kind="Internal")
    v_scr = nc.dram_tensor("v_scr", [L, 512], BF16, kind="Internal")
    u_scr = nc.dram_tensor("u_scr", [512, L], BF16, kind="Internal")
    o_scr = nc.dram_tensor("o_scr", [D, L], F32, kind="Internal")
    act_scr = nc.dram_tensor("act_scr", [DFF, L], BF16, kind="Internal")
    if dbg:
        dbg_mods = nc.dram_tensor("dbg_mods", [128, DEPTH * 48], F32, kind="ExternalOutput")
        dbg_qk = nc.dram_tensor("dbg_qk", [1024, L], BF16, kind="ExternalOutput")
        dbg_v = nc.dram_tensor("dbg_v", [L, 512], BF16, kind="ExternalOutput")
        dbg_u = nc.dram_tensor("dbg_u", [512, L], BF16, kind="ExternalOutput")
        dbg_o = nc.dram_tensor("dbg_o", [D, L], F32, kind="ExternalOutput")
        dbg_x1 = nc.dram_tensor("dbg_x1", [D, L], F32, kind="ExternalOutput")

    xr = [R() for _ in range(NT)]
    qkr = [R() for _ in range(8)]
    vr = R()
    ur = [R() for _ in range(4)]
    ors = [R() for _ in range(8)]
    acr = [[R() for _ in range(NFF)] for _ in range(NT)]

    with ExitStack() as top:
        _uid = [0]

        def sb(name, shape, dt, stack=top):
            _uid[0] += 1
            return stack.enter_context(nc.sbuf_tensor("%s_%d" % (name, _uid[0]), shape, dt))

        banks = [top.enter_context(nc.psum_tensor("ps%d" % i, [128, 512], F32)) for i in range(8)]
        bR = [R(excl=True) for _ in range(8)]

        ones_bf = sb("ones_bf", [128, 128], BF16)
        blk_bf = sb("blk_bf", [128, 128], BF16)
        negU = sb("negU", [128, 128], BF16)
        negOnes = sb("negOnes", [128, 128], BF16)
        masks = sb("masks", [128, 4, 512], BF16)
        M8 = sb("M8", [128, 8], F32)
        M2 = sb("M2", [128, 4, 8], F32)
        iot = sb("iot", [128, 520], F32)
        mods = sb("mods", [128, DEPTH, 48], F32)
        cact = sb("cact", [128, KC], F32)
        abT = sb("abT", [128, DEPTH, 48], F32)
        cR = R()
        modsR = R()
        with ExitStack() as ph:
            tmpi = sb("tmpi", [128, 2048], I32, ph)
            tmpf = sb("tmpf", [128, 2048], F32, ph)
            tmpg = sb("tmpg", [128, 2048], F32, ph)
            S.op("pool", lambda e: e.memset(ones_bf[:], 1.0), Wr=[cR])
            S.op("pool", lambda e: e.memset(negOnes[:], -1.0), Wr=[cR])
            S.op("pool", lambda e: e.memset(blk_bf[:], 0.0), Wr=[cR])
            S.op("pool", lambda e: e.memset(blk_bf[0:64, 0:64], 1.0), Wr=[cR])
            S.op("pool", lambda e: e.memset(blk_bf[64:128, 64:128], 1.0), Wr=[cR])
            S.op("pool", lambda e: e.iota(tmpi[:, 0:128], [[-1, 128]], base=0, channel_multiplier=1), Wr=[cR])
            S.op("dve", lambda e: e.tensor_copy(tmpf[:, 0:128], tmpi[:, 0:128]), Rd=[cR], Wr=[cR])
            S.op("dve", lambda e: e.tensor_scalar(negU[:], tmpf[:, 0:128], 0.0, -1.0, ALU.is_ge, ALU.mult), Rd=[cR], Wr=[cR])
            S.op("pool", lambda e: e.iota(tmpi[:, 0:2048], [[-128, 4], [1, 512]], base=0, channel_multiplier=-1), Wr=[cR])
            S.op("dve", lambda e: e.tensor_copy(tmpf[:, 0:2048], tmpi[:, 0:2048]), Rd=[cR], Wr=[cR])
            S.op("dve", lambda e: e.tensor_scalar(masks[:].rearrange("p o f -> p (o f)"), tmpf[:, 0:2048], 0.0, None, ALU.is_gt), Rd=[cR], Wr=[cR])
            S.op("pool", lambda e: e.iota(tmpi[:, 0:8], [[-16, 8]], base=0, channel_multiplier=1), Wr=[cR])
            S.op("dve", lambda e: e.tensor_copy(tmpf[:, 0:8], tmpi[:, 0:8]), Rd=[cR], Wr=[cR])
            S.op("dve", lambda e: e.tensor_scalar(tmpg[:, 0:8], tmpf[:, 0:8], 0.0, None, ALU.is_ge), Rd=[cR], Wr=[cR])
            S.op("dve", lambda e: e.tensor_scalar(tmpf[:, 0:8], tmpf[:, 0:8], 15.5, None, ALU.is_le), Rd=[cR], Wr=[cR])
            S.op("dve", lambda e: e.tensor_tensor(M8[:], tmpf[:, 0:8], tmpg[:, 0:8], ALU.mult), Rd=[cR], Wr=[cR])
            for g2 in range(2):
                S.op("pool", lambda e, g2=g2: e.iota(tmpi[64 * g2:64 * g2 + 64, 0:32], [[-2, 4], [1, 8]], base=-g2, channel_multiplier=0), Wr=[cR])
            S.op("dve", lambda e: e.tensor_copy(tmpf[:, 0:32], tmpi[:, 0:32]), Rd=[cR], Wr=[cR])
            S.op("dve", lambda e: e.tensor_scalar(M2[:].rearrange("p a b -> p (a b)"), tmpf[:, 0:32], 0.0, None, ALU.is_equal), Rd=[cR], Wr=[cR])
            S.op("pool", lambda e: e.iota(tmpi[:, 0:520], [[1, 520]], base=0, channel_multiplier=0), Wr=[cR])
            S.op("dve", lambda e: e.tensor_copy(iot[:], tmpi[:, 0:520]), Rd=[cR], Wr=[cR])

            dump("M8", M8[:], [cR]); dump("M2", M2[:].rearrange("p a b -> p (a b)"), [cR]); dump("iot", iot[:], [cR])
            cT = sb("cT_sb", [128, KC], F32, ph)
            awb = [sb("awb%d" % i, [128, KC, 512], F32, ph) for i in range(2)]
            awR = [R(), R()]
            S.dma("sp", cT[:], cT_in[:, :], Wr=[cR])
            for l in range(n_layers):
                S.dma("sp", abT[:, l, :], par_in[l, :, _cols["ada_b"][0]:_cols["ada_b"][1]], Wr=[cR])
            S.op("act", lambda e: e.activation(out=cact[:], in_=cT[:], func=AF.Silu), Rd=[cR], Wr=[cR])
            it = 0
            for l in range(1):
                for jb in range(12):
                    b = it % 2
                    it += 1
                    S.dma("sp", awb[b][:], adaw_in[l, :, jb * 512:(jb + 1) * 512].rearrange("(kc p) n -> p kc n", p=128), Wr=[awR[b]])
                    for jj in range(4):
                        col = l * 48 + jb * 4 + jj
                        for kc in range(KC):
                            S.op("pe", lambda e, b=b, jj=jj, kc=kc, col=col: e.matmul(
                                banks[0][:, col:col + 1], awb[b][:, kc, jj * 128:(jj + 1) * 128], cact[:, kc:kc + 1],
                                start=(kc == 0), stop=(kc == KC - 1)), Rd=[awR[b], cR], Wr=[bR[0]])
            S.op("dve", lambda e: e.tensor_tensor(mods[:, 0, :], banks[0][:, 0:48], abT[:, 0, :], ALU.add), Rd=[bR[0], cR], Wr=[modsR])
            if dbg:
                S.dma("sp", dbg_mods[:, 0:n_layers * 48], mods[:, 0:n_layers, :].rearrange("p l j -> p (l j)"), Rd=[modsR])
            S.barrier()

        for l in range(n_layers):
            x_src = xT_in if l == 0 else x_scr
            with ExitStack() as lay:
                par = sb("par_sb", [128, NPAR], F32, lay)
                parR = R()
                S.dma("sp", par[:], par_in[l, :, :], Wr=[parR])

                def pc(name, a=None, b=None):
                    lo, hi = _cols[name]
                    if a is None:
                        return par[:, lo:hi]
                    return par[:, lo + a:lo + (b if b is not None else a + 1)]

                der = sb("der", [128, 32], F32, lay)
                derR = R()
                S.op("dve", lambda e: e.scalar_tensor_tensor(der[:, 0:8], mods[:, l, 8:16], 1.0, pc("n1g"), ALU.add, ALU.mult), Rd=[modsR, parR], Wr=[derR])
                S.op("dve", lambda e: e.scalar_tensor_tensor(der[:, 8:16], mods[:, l, 32:40], 1.0, pc("n2g"), ALU.add, ALU.mult), Rd=[modsR, parR], Wr=[derR])
                S.op("dve", lambda e: e.tensor_scalar(der[:, 16:17], pc("qg"), 0.125, None, ALU.mult), Rd=[parR], Wr=[derR])
                gs1 = lambda kc: der[:, kc:kc + 1]
                gs2 = lambda kc: der[:, 8 + kc:9 + kc]
                gq = der[:, 16:17]
                sh1 = lambda kc: mods[:, l, kc:kc + 1]
                g1 = lambda kc: mods[:, l, 16 + kc:17 + kc]
                sh2 = lambda kc: mods[:, l, 24 + kc:25 + kc]
                g2 = lambda kc: mods[:, l, 40 + kc:41 + kc]

                with ExitStack() as ph:
                    win = sb("win", [128, KC, 2048], BF16, ph)
                    winR = [R() for _ in range(8)]
                    wst = [sb("wstA%d" % i, [128, KC, 256], F32, ph) for i in range(2)]
                    wstR = [R(), R()]
                    xt = [sb("xtA%d" % i, [128, KC, TT], F32, ph) for i in range(2)]
                    xtR = [R(), R()]
                    sqb = sb("sqbA", [128, KC, TT], BF16, ph)
                    sqR = R()
                    std = sb("stdA", [128, TT], F32, ph)
                    stdR = R()
                    xn = [sb("xnA%d" % i, [128, TT], F32, ph) for i in range(2)]
                    xnR = [R(), R()]
                    hb = [sb("hbA%d" % i, [128, KC, TT], BF16, ph) for i in range(2)]
                    hR = [R(), R()]
                    stage = [sb("stgA%d" % i, [128, 16, TT], BF16, ph) for i in range(2)]
                    stgR = [R(), R()]
                    sq2 = [sb("sq2A%d" % i, [128, TT], BF16, ph) for i in range(3)]
                    sq2R = [R(), R(), R()]
                    std2 = [sb("std2A%d" % i, [128, TT], F32, ph) for i in range(3)]
                    std2R = [R(), R(), R()]
                    mcnt = [0]

                    def a_load(tt):
                        b = tt % 2
                        ts = slice(tt * TT, (tt + 1) * TT)
                        S.dma("sp", xt[b][:], x_src[:, ts].rearrange("(kc p) t -> p kc t", p=128), Rd=([xr[tt]] if l > 0 else []), Wr=[xtR[b]])

                    def a_pro(tt):
                        b = tt % 2
                        S.op("act", lambda e: e.activation(out=sqb[:], in_=xt[b][:], func=AF.Square), Rd=[xtR[b]], Wr=[sqR])
                        for kc in range(KC):
                            S.op("pe", lambda e, kc=kc: e.matmul(banks[0][:], ones_bf[:], sqb[:, kc, :], start=(kc == 0), stop=(kc == KC - 1)), Rd=[sqR, cR], Wr=[bR[0]])
                        S.op("act", lambda e: e.activation(out=std[:], in_=banks[0][:], func=AF.Ln, bias=EPS, scale=1.0 / D), Rd=[bR[0]], Wr=[stdR])
                        S.op("act", lambda e: e.activation(out=std[:], in_=std[:], func=AF.Exp, scale=-0.5), Rd=[stdR], Wr=[stdR])
                        for kc in range(KC):
                            i2 = kc % 2
                            S.op("dve", lambda e, kc=kc: e.tensor_tensor(xn[i2][:], xt[b][:, kc, :], std[:], ALU.mult), Rd=[xtR[b], stdR], Wr=[xnR[i2]])
                            S.op("act", lambda e, kc=kc: e.activation(out=hb[b][:, kc, :], in_=xn[i2][:], func=AF.Identity, bias=sh1(kc), scale=gs1(kc)), Rd=[xnR[i2], derR, modsR], Wr=[hR[b]])

                    a_load(0)
                    for pi in range(8):
                        b = pi % 2
                        S.dma("sp", wst[b][:], win_in[l, :, pi * 256:(pi + 1) * 256].rearrange("(kc p) n -> p kc n", p=128), Wr=[wstR[b]])
                        S.op("act", lambda e, b=b, pi=pi: e.copy(win[:, :, pi * 256:(pi + 1) * 256], wst[b][:]), Rd=[wstR[b]], Wr=[winR[pi]])
                        if pi == 1:
                            a_load(1)
                    a_pro(0)
                    for tt in range(NT):
                        b = tt % 2
                        ts = slice(tt * TT, (tt + 1) * TT)
                        hbt = hb[b]
                        for c in range(8):
                            pb = 1 + (mcnt[0] % 3)
                            mcnt[0] += 1
                            hbk = 4 + (c % 2)
                            i2 = c % 3
                            for kc in range(KC):
                                S.op("pe", lambda e, kc=kc, c=c, pb=pb: e.matmul(banks[pb][:], win[:, kc, c * 128:(c + 1) * 128], hbt[:, kc, :], start=(kc == 0), stop=(kc == KC - 1)), Rd=[winR[c // 2], hR[b]], Wr=[bR[pb]])
                            S.op("act", lambda e, pb=pb, i2=i2: e.activation(out=sq2[i2][:], in_=banks[pb][:], func=AF.Square), Rd=[bR[pb]], Wr=[sq2R[i2]])
                            S.op("pe", lambda e, hbk=hbk, i2=i2: e.matmul(banks[hbk][:], blk_bf[:], sq2[i2][:], start=True, stop=True), Rd=[sq2R[i2], cR], Wr=[bR[hbk]])
                            S.op("act", lambda e, hbk=hbk, i2=i2: e.activation(out=std2[i2][:], in_=banks[hbk][:], func=AF.Ln, bias=EPS, scale=1.0 / 64), Rd=[bR[hbk]], Wr=[std2R[i2]])
                            S.op("act", lambda e, i2=i2: e.activation(out=std2[i2][:], in_=std2[i2][:], func=AF.Exp, scale=-0.5), Rd=[std2R[i2]], Wr=[std2R[i2]])
                            gcol = gq if c < 4 else pc("kg")
                            S.op("dve", lambda e, pb=pb, i2=i2, c=c, b=b, gcol=gcol: e.scalar_tensor_tensor(stage[b][:, c, :], banks[pb][:], gcol, std2[i2][:], ALU.mult, ALU.mult),
                                 Rd=[bR[pb], std2R[i2], derR, parR], Wr=[stgR[b]])
                        if tt + 2 < NT:
                            a_load(tt + 2) if False else None
                        for s4 in range(4):
                            pb = 6 + (s4 % 2)
                            for kc in range(KC):
                                S.op("pe", lambda e, kc=kc, s4=s4, pb=pb: e.matmul(banks[pb][:], hbt[:, kc, s4 * 128:(s4 + 1) * 128], win[:, kc, 1024:1536], start=(kc == 0), stop=(kc == KC - 1)), Rd=[winR[4], winR[5], hR[b]], Wr=[bR[pb]])
                            S.op("act", lambda e, pb=pb, s4=s4, b=b: e.copy(stage[b][:, 8 + s4, :], banks[pb][:]), Rd=[bR[pb]], Wr=[stgR[b]])
                        if tt + 1 < NT:
                            a_pro(tt + 1)
                        for c in range(4):
                            pb = 1 + (mcnt[0] % 3)
                            mcnt[0] += 1
                            for kc in range(KC):
                                S.op("pe", lambda e, kc=kc, c=c, pb=pb: e.matmul(banks[pb][:], win[:, kc, 1536 + c * 128:1536 + (c + 1) * 128], hbt[:, kc, :], start=(kc == 0), stop=(kc == KC - 1)), Rd=[winR[6 + c // 2], hR[b]], Wr=[bR[pb]])
                            S.op("dve", lambda e, pb=pb, c=c, b=b: e.tensor_copy(stage[b][:, 12 + c, :], banks[pb][:]), Rd=[bR[pb]], Wr=[stgR[b]])
                        if tt + 2 < NT:
                            a_load(tt + 2)
                        S.dma("pool", qk_scr[:, ts].rearrange("(c p) t -> p c t", p=128), stage[b][:, 0:8, :], Rd=[stgR[b]], Wr=qkr)
                        S.dma("pool", v_scr[tt * TT:(tt + 1) * TT, :].rearrange("(s p) f -> p s f", p=128), stage[b][:, 8:12, :], Rd=[stgR[b]], Wr=[vr])
                        S.dma("pool", u_scr[:, ts].rearrange("(c p) t -> p c t", p=128), stage[b][:, 12:16, :], Rd=[stgR[b]], Wr=ur)
                    S.barrier()

                with ExitStack() as ph:
                    qT = [sb("qTB%d" % i, [128, L], BF16, ph) for i in range(2)]
                    kT = [sb("kTB%d" % i, [128, L], BF16, ph) for i in range(2)]
                    vt = [sb("vtB%d" % i, [128, 32, 128], BF16, ph) for i in range(2)]
                    ldR = [R(), R()]
                    ost = [sb("ostB%d" % i, [128, L], F32, ph) for i in range(2)]
                    ostR = [R(), R()]
                    NB3 = 3
                    eb = [sb("ebB%d" % i, [128, TT], F32, ph) for i in range(2)]
                    lb = [sb("lbB%d" % i, [128, TT], BF16, ph) for i in range(NB3)]
                    wb = [sb("wbB%d" % i, [128, TT], BF16, ph) for i in range(NB3)]
                    Ls = [sb("LsB%d" % i, [128, TT], BF16, ph) for i in range(2)]
                    eR = [R(), R()]; lR = [R() for _ in range(NB3)]; wR = [R() for _ in range(NB3)]; LsR = [R(), R()]
                    steps = []
                    nqt = nq_tiles if "B" in phases else 0
                    sweep = 0
                    for hp in range(4):
                        for hh in range(2):
                            for qi in range(nqt):
                                nk = 4 * (qi + 1)
                                lcur = 0
                                for idx, kt in enumerate(range(nk - 1, -1, -1)):
                                    st = dict(hp=hp, b=hp % 2, po=64 * hh, qi=qi, kt=kt, idx=idx, nk=nk, ob=4 + (sweep % 2),
                                              diag=(kt >= 4 * qi), o_=kt - 4 * qi, first_of_pair=(hh == 0 and qi == 0 and idx == 0),
                                              last_of_pair=(hh == 1 and qi == nqt - 1 and idx == nk - 1))
                                    st["ls_read"] = lcur if idx > 0 else None
                                    if idx < nk - 1:
                                        st["ls_write"] = 1 - lcur
                                        lcur = 1 - lcur
                                    else:
                                        st["ls_write"] = None
                                    steps.append(st)
                                sweep += 1
                    NS = len(steps)

                    def loads(hp):
                        b = hp % 2
                        S.dma("sp", qT[b][:], qk_scr[hp * 128:(hp + 1) * 128, :], Rd=[qkr[hp]], Wr=[ldR[b]])
                        S.dma("sp", kT[b][:], qk_scr[512 + hp * 128:512 + (hp + 1) * 128, :], Rd=[qkr[4 + hp]], Wr=[ldR[b]])
                        S.dma("sp", vt[b][:], v_scr[:, hp * 128:(hp + 1) * 128].rearrange("(kt p) f -> p kt f", p=128), Rd=[vr], Wr=[ldR[b]])

                    def stA(i):
                        st = steps[i]; b = st["b"]; po = st["po"]; a = i % 4
                        if st["first_of_pair"]:
                            if st["hp"] == 0:
                                loads(0)
                            if st["hp"] + 1 < 4:
                                loads(st["hp"] + 1)
                        ks = slice(st["kt"] * 128, (st["kt"] + 1) * 128); qs = slice(st["qi"] * TT, (st["qi"] + 1) * TT)
                        S.op("pe", lambda e: e.matmul(banks[a][:], kT[b][po:po + 64, ks], qT[b][po:po + 64, qs], start=True, stop=False), Rd=[ldR[b]], Wr=[bR[a]])

                    def stL(i):
                        st = steps[i]; a = i % 4; e2 = i % 2; l3 = i % NB3
                        S.op("act", lambda e: e.activation(out=eb[e2][:], in_=banks[a][:], func=AF.Exp), Rd=[bR[a]], Wr=[eR[e2]])
                        S.op("act", lambda e: e.activation(out=lb[l3][:], in_=eb[e2][:], func=AF.Ln, bias=1.0), Rd=[eR[e2]], Wr=[lR[l3]])
                        if st["diag"]:
                            S.op("dve", lambda e: e.tensor_tensor(lb[l3][:], lb[l3][:], masks[:, st["o_"], :], ALU.mult), Rd=[lR[l3], cR], Wr=[lR[l3]])

                    def stB(i):
                        st = steps[i]; a = i % 4; l3 = i % NB3
                        lr_, lw_ = st["ls_read"], st["ls_write"]
                        S.op("pe", lambda e: e.matmul(banks[a][:], negU[:], lb[l3][:], start=False, stop=(lr_ is None)), Rd=[lR[l3], cR], Wr=[bR[a]])
                        if lr_ is not None:
                            S.op("pe", lambda e: e.matmul(banks[a][:], negOnes[:], Ls[lr_][:], start=False, stop=True), Rd=[LsR[lr_], cR], Wr=[bR[a]])
                        if lw_ is not None:
                            if lr_ is None:
                                S.op("pool", lambda e: e.tensor_copy(Ls[lw_][:], lb[l3][:]), Rd=[lR[l3]], Wr=[LsR[lw_]])
                            else:
                                S.op("pool", lambda e: e.tensor_tensor(Ls[lw_][:], Ls[lr_][:], lb[l3][:], ALU.add), Rd=[lR[l3], LsR[lr_]], Wr=[LsR[lw_]])
                        S.op("act", lambda e: e.activation(out=wb[l3][:], in_=banks[a][:], func=AF.Exp), Rd=[bR[a]], Wr=[wR[l3]])
                        if st["diag"]:
                            S.op("dve", lambda e: e.tensor_tensor(wb[l3][:], wb[l3][:], masks[:, st["o_"], :], ALU.mult), Rd=[wR[l3], cR], Wr=[wR[l3]])

                    def stO(i):
                        st = steps[i]; b = st["b"]; po = st["po"]; l3 = i % NB3; ob = st["ob"]; kt = st["kt"]
                        S.op("pe", lambda e: e.matmul(banks[ob][po:po + 64, :], vt[b][:, kt, po:po + 64], wb[l3][:], start=(st["idx"] == 0), stop=(st["idx"] == st["nk"] - 1)),
                             Rd=[wR[l3], ldR[b]], Wr=[bR[ob]])
                        if st["idx"] == st["nk"] - 1:
                            qs = slice(st["qi"] * TT, (st["qi"] + 1) * TT)
                            S.op("dve", lambda e: e.tensor_copy(ost[b][po:po + 64, qs], banks[ob][po:po + 64, :]), Rd=[bR[ob]], Wr=[ostR[b]])
                        if st["last_of_pair"]:
                            hp = st["hp"]
                            S.dma("pool", o_scr[hp * 128:(hp + 1) * 128, :], ost[b][:], Rd=[ostR[b]], Wr=[ors[hp]])

                    for slot in range(-2, NS + 1):
                        if 0 <= slot + 2 < NS:
                            stA(slot + 2)
                        if 0 <= slot + 1 < NS:
                            stL(slot + 1)
                        if 0 <= slot < NS:
                            stB(slot)
                        if 0 <= slot - 1 < NS:
                            stO(slot - 1)
                    S.barrier()

                with ExitStack() as ph:
                    def ftile(name, n=256):
                        return sb(name, [128, n], F32, ph)
                    zR = R()
                    dtB = ftile("dtB", 4)
                    lre = ftile("lre"); lim = ftile("lim"); mag = ftile("mag"); cosl = ftile("cosl"); sinl = ftile("sinl")
                    ta = ftile("ta"); tb = ftile("tb"); tcc = ftile("tcc"); td = ftile("td")
                    tiq = sb("tiq", [128, 520], I32, ph)
                    tq1 = ftile("tq1", 520); tq2 = ftile("tq2", 520); tq3 = ftile("tq3", 520)
                    INV2PI = 1.0 / (2 * PI)

                    def frac_turns(eng, t, n):
                        S.op(eng, lambda e: e.tensor_copy(tiq[:, 0:n], t), Rd=[zR], Wr=[zR])
                        S.op(eng, lambda e: e.tensor_copy(tq1[:, 0:n], tiq[:, 0:n]), Rd=[zR], Wr=[zR])
                        S.op(eng, lambda e: e.tensor_tensor(t, t, tq1[:, 0:n], ALU.subtract), Rd=[zR], Wr=[zR])
                        S.op(eng, lambda e: e.tensor_scalar(tq1[:, 0:n], t, 0.5, None, ALU.is_gt), Rd=[zR], Wr=[zR])
                        S.op(eng, lambda e: e.tensor_tensor(t, t, tq1[:, 0:n], ALU.subtract), Rd=[zR], Wr=[zR])
                        S.op(eng, lambda e: e.tensor_scalar(tq1[:, 0:n], t, -0.5, None, ALU.is_lt), Rd=[zR], Wr=[zR])
                        S.op(eng, lambda e: e.tensor_tensor(t, t, tq1[:, 0:n], ALU.add), Rd=[zR], Wr=[zR])

                    def sincos_turns(eng, sin_out, cos_out, t, n):
                        S.op("act", lambda e: e.activation(out=sin_out, in_=t, func=AF.Sin, scale=2 * PI), Rd=[zR], Wr=[zR])
                        S.op(eng, lambda e: e.tensor_scalar(tq2[:, 0:n], t, 0.25, None, ALU.add), Rd=[zR], Wr=[zR])
                        S.op(eng, lambda e: e.tensor_scalar(tq1[:, 0:n], tq2[:, 0:n], 0.5, None, ALU.is_gt), Rd=[zR], Wr=[zR])
                        S.op(eng, lambda e: e.tensor_tensor(tq2[:, 0:n], tq2[:, 0:n], tq1[:, 0:n], ALU.subtract), Rd=[zR], Wr=[zR])
                        S.op("act", lambda e: e.activation(out=cos_out, in_=tq2[:, 0:n], func=AF.Sin, scale=2 * PI), Rd=[zR], Wr=[zR])

                    S.op("act", lambda e: e.activation(out=dtB[:], in_=pc("ldtB"), func=AF.Exp), Rd=[parR], Wr=[zR])
                    for k in range(4):
                        ksl = slice(k * 64, (k + 1) * 64)
                        S.op("dve", lambda e, k=k, ksl=ksl: e.tensor_scalar(lre[:, ksl], pc("aBre")[:, ksl], dtB[:, k:k + 1], None, ALU.mult), Rd=[parR, zR], Wr=[zR])
                        S.op("dve", lambda e, k=k, ksl=ksl: e.tensor_scalar(lim[:, ksl], pc("aBim")[:, ksl], dtB[:, k:k + 1], INV2PI, ALU.mult, ALU.mult), Rd=[parR, zR], Wr=[zR])
                    S.op("act", lambda e: e.activation(out=mag[:], in_=lre[:], func=AF.Exp), Rd=[zR], Wr=[zR])
                    frac_turns("dve", lim[:], 256)
                    sincos_turns("dve", sinl[:], cosl[:], lim[:], 256)
                    S.op("dve", lambda e: e.tensor_tensor(ta[:], mag[:], cosl[:], ALU.mult), Rd=[zR], Wr=[zR])
                    S.op("dve", lambda e: e.tensor_scalar(ta[:], ta[:], -1.0, None, ALU.add), Rd=[zR], Wr=[zR])
                    S.op("dve", lambda e: e.tensor_tensor(tb[:], mag[:], sinl[:], ALU.mult), Rd=[zR], Wr=[zR])
                    S.op("dve", lambda e: e.tensor_tensor(mag[:], pc("aBre"), pc("aBre"), ALU.mult), Rd=[zR, parR], Wr=[zR])
                    S.op("dve", lambda e: e.tensor_tensor(cosl[:], pc("aBim"), pc("aBim"), ALU.mult), Rd=[zR, parR], Wr=[zR])
                    S.op("dve", lambda e: e.tensor_tensor(mag[:], mag[:], cosl[:], ALU.add), Rd=[zR], Wr=[zR])
                    S.op("dve", lambda e: e.reciprocal(mag[:], mag[:]), Rd=[zR], Wr=[zR])
                    S.op("dve", lambda e: e.tensor_tensor(tcc[:], ta[:], pc("aBre"), ALU.mult), Rd=[zR, parR], Wr=[zR])
                    S.op("dve", lambda e: e.tensor_tensor(cosl[:], tb[:], pc("aBim"), ALU.mult), Rd=[zR, parR], Wr=[zR])
                    S.op("dve", lambda e: e.tensor_tensor(tcc[:], tcc[:], cosl[:], ALU.add), Rd=[zR], Wr=[zR])
                    S.op("dve", lambda e: e.tensor_tensor(tcc[:], tcc[:], mag[:], ALU.mult), Rd=[zR], Wr=[zR])
                    S.op("dve", lambda e: e.tensor_tensor(td[:], tb[:], pc("aBre"), ALU.mult), Rd=[zR, parR], Wr=[zR])
                    S.op("dve", lambda e: e.tensor_tensor(cosl[:], ta[:], pc("aBim"), ALU.mult), Rd=[zR, parR], Wr=[zR])
                    S.op("dve", lambda e: e.tensor_tensor(td[:], td[:], cosl[:], ALU.subtract), Rd=[zR], Wr=[zR])
                    S.op("dve", lambda e: e.tensor_tensor(td[:], td[:], mag[:], ALU.mult), Rd=[zR], Wr=[zR])
                    S.op("dve", lambda e: e.tensor_tensor(ta[:], tcc[:], pc("BreB"), ALU.mult), Rd=[zR, parR], Wr=[zR])
                    S.op("dve", lambda e: e.tensor_tensor(cosl[:], td[:], pc("BimB"), ALU.mult), Rd=[zR, parR], Wr=[zR])
                    S.op("dve", lambda e: e.tensor_tensor(ta[:], ta[:], cosl[:], ALU.subtract), Rd=[zR], Wr=[zR])
                    S.op("dve", lambda e: e.tensor_tensor(tb[:], tcc[:], pc("BimB"), ALU.mult), Rd=[zR, parR], Wr=[zR])
                    S.op("dve", lambda e: e.tensor_tensor(cosl[:], td[:], pc("BreB"), ALU.mult), Rd=[zR, parR], Wr=[zR])
                    S.op("dve", lambda e: e.tensor_tensor(tb[:], tb[:], cosl[:], ALU.add), Rd=[zR], Wr=[zR])
                    BpR = sb("BpR", [128, 4, 4, 128], BF16, ph)
                    BpI = sb("BpI", [128, 4, 4, 128], BF16, ph)
                    CpR = sb("CpR", [128, 4, 4, 128], BF16, ph)
                    CpI = sb("CpI", [128, 4, 4, 128], BF16, ph)
                    Gbd = sb("Gbd", [128, 4, 128], BF16, ph)
                    wtR = R()
                    S.op("pool", lambda e: e.memset(CpR[:], 0.0), Wr=[wtR])
                    S.op("pool", lambda e: e.memset(CpI[:], 0.0), Wr=[wtR])
                    for k in range(4):
                        ksl = slice(k * 64, (k + 1) * 64)
                        for j2 in range(4):
                            for g2_ in range(2):
                                col = 2 * j2 + g2_
                                S.op("dve", lambda e, k=k, j2=j2, g2_=g2_, col=col, ksl=ksl: e.tensor_scalar(BpR[:, k, j2, g2_ * 64:(g2_ + 1) * 64], ta[:, ksl], M8[:, col:col + 1], None, ALU.mult), Rd=[zR, cR], Wr=[wtR])
                                S.op("dve", lambda e, k=k, j2=j2, g2_=g2_, col=col, ksl=ksl: e.tensor_scalar(BpI[:, k, j2, g2_ * 64:(g2_ + 1) * 64], tb[:, ksl], M8[:, col:col + 1], None, ALU.mult), Rd=[zR, cR], Wr=[wtR])
                                cs = slice((k * 4 + j2) * 16, (k * 4 + j2) * 16 + 16)
                                S.op("dve", lambda e, k=k, j2=j2, col=col, cs=cs: e.tensor_scalar(CpR[:, k, j2, col * 16:(col + 1) * 16], pc("CreS")[:, cs], M2[:, j2, col:col + 1], None, ALU.mult), Rd=[parR, cR], Wr=[wtR])
                                S.op("dve", lambda e, k=k, j2=j2, col=col, cs=cs: e.tensor_scalar(CpI[:, k, j2, col * 16:(col + 1) * 16], pc("CimS")[:, cs], M2[:, j2, col:col + 1], -1.0, ALU.mult, ALU.mult), Rd=[parR, cR], Wr=[wtR])
                        for g8 in range(8):
                            S.op("dve", lambda e, k=k, g8=g8: e.tensor_scalar(Gbd[:, k, g8 * 16:(g8 + 1) * 16], pc("GS")[:, k * 16:(k + 1) * 16], M8[:, g8:g8 + 1], None, ALU.mult), Rd=[parR, cR], Wr=[wtR])
                    dump("Bbre", ta[:], [zR]); dump("Bbim", tb[:], [zR]); dump("fre", tcc[:], [zR]); dump("fim", td[:], [zR])
                    dump("BpR", BpR[:].rearrange("p a b c -> p (a b c)"), [wtR]); dump("CpR", CpR[:].rearrange("p a b c -> p (a b c)"), [wtR])
                    dump("CpI", CpI[:].rearrange("p a b c -> p (a b c)"), [wtR]); dump("Gbd", Gbd[:].rearrange("p a b -> p (a b)"), [wtR])
                    dtS = ftile("dtS", 16); rS = ftile("rS", 16); thS = ftile("thS", 16)
                    S.op("act", lambda e: e.activation(out=dtS[:], in_=pc("ldtS"), func=AF.Exp), Rd=[parR], Wr=[zR])
                    S.op("dve", lambda e: e.tensor_tensor(rS[:], dtS[:], pc("aSre"), ALU.mult), Rd=[zR, parR], Wr=[zR])
                    S.op("act", lambda e: e.activation(out=rS[:], in_=rS[:], func=AF.Exp), Rd=[zR], Wr=[zR])
                    S.op("dve", lambda e: e.tensor_tensor(thS[:], dtS[:], pc("aSim"), ALU.mult), Rd=[zR, parR], Wr=[zR])
                    S.op("dve", lambda e: e.tensor_scalar(thS[:], thS[:], INV2PI, None, ALU.mult), Rd=[zR], Wr=[zR])
                    frac_turns("dve", thS[:], 16)

                    dump("rS", rS[:], [zR]); dump("thS", thS[:], [zR])
                    CpRn = sb("CpRn", [128, 4, 4, 128], BF16, ph)
                    S.op("pool", lambda e: e.tensor_scalar(CpRn[:].rearrange("p a b c -> p (a b c)"), CpR[:].rearrange("p a b c -> p (a b c)"), -1.0, None, ALU.mult), Rd=[wtR], Wr=[wtR])
                    uT = [sb("uTC%d" % i, [128, L], BF16, ph) for i in range(2)]
                    uR = [R(), R()]
                    cosT = [[sb("cosT%d_%d" % (a, i), [128, 520], BF16, ph) for i in range(4)] for a in range(2)]
                    sinT = [[sb("sinT%d_%d" % (a, i), [128, 520], BF16, ph) for i in range(4)] for a in range(2)]
                    c5s5 = sb("c5s5", [128, 2, 4, 2], F32, ph)
                    rT = [[sb("rT%d_%d" % (a, i), [128, TT], F32, ph) for i in range(4)] for a in range(2)]
                    tabR = [[R() for _ in range(4)] for a in range(2)]
                    ini2 = [sb("ini%d" % a, [128, 4, 2], F32, ph) for a in range(2)]
                    iniR2 = [[R() for _ in range(4)] for a in range(2)]
                    tinyt = sb("tinyt", [128, 8], F32, ph)
                    tinyR = R()
                    NR3 = 3
                    bsr = [sb("bsrC%d" % i, [128, TT], BF16, ph) for i in range(NR3)]
                    bsi = [sb("bsiC%d" % i, [128, TT], BF16, ph) for i in range(NR3)]
                    bsR = [R() for _ in range(NR3)]
                    mm_ = [[sb("mC%d_%d" % (i, q), [128, TT], BF16, ph) for q in range(4)] for i in range(2)]
                    mR = [R(), R()]
                    bpr = [sb("bprC%d" % i, [128, TT], BF16, ph) for i in range(NR3)]
                    bpi = [sb("bpiC%d" % i, [128, TT], BF16, ph) for i in range(NR3)]
                    bpR = [R() for _ in range(NR3)]
                    wre = [sb("wreC%d" % i, [128, TT], F32, ph) for i in range(NR3)]
                    wim = [sb("wimC%d" % i, [128, TT], F32, ph) for i in range(NR3)]
                    wwR = [R() for _ in range(NR3)]
                    wrb = [sb("wrbC%d" % i, [128, TT], BF16, ph) for i in range(NR3)]
                    wib = [sb("wibC%d" % i, [128, TT], BF16, ph) for i in range(NR3)]
                    wbR = [R() for _ in range(NR3)]
                    pp = [[sb("pC%d_%d" % (i, q), [128, TT], BF16, ph) for q in range(4)] for i in range(NR3)]
                    ppR = [R() for _ in range(NR3)]
                    yb = [sb("ybC%d" % i, [128, TT], F32, ph) for i in range(2)]
                    ygb = [sb("ygbC%d" % i, [128, TT], BF16, ph) for i in range(2)]
                    gt = [sb("gtC%d" % i, [128, TT], F32, ph) for i in range(2)]
                    osg = [sb("osC%d" % i, [128, TT], F32, ph) for i in range(2)]
                    ybR = [R(), R()]; ygR = [R(), R()]; gtR = [R(), R()]; osR = [R(), R()]

                    def gen_tables(k, j2):
                        a = k % 2
                        col = k * 4 + j2
                        S.op("dve", lambda e: e.tensor_scalar(tq3[:, 0:520], iot[:, 0:520], thS[:, col:col + 1], None, ALU.mult), Rd=[zR, cR], Wr=[zR])
                        frac_turns("dve", tq3[:, 0:520], 520)
                        S.op("act", lambda e: e.activation(out=sinT[a][j2][:], in_=tq3[:, 0:520], func=AF.Sin, scale=2 * PI), Rd=[zR], Wr=[tabR[a][j2]])
                        S.op("act", lambda e: e.activation(out=c5s5[:, a, j2, 1:2], in_=tq3[:, 512:513], func=AF.Sin, scale=2 * PI), Rd=[zR], Wr=[tabR[a][j2]])
                        S.op("dve", lambda e: e.tensor_scalar(tq2[:, 0:520], tq3[:, 0:520], 0.25, None, ALU.add), Rd=[zR], Wr=[zR])
                        S.op("dve", lambda e: e.tensor_scalar(tq1[:, 0:520], tq2[:, 0:520], 0.5, None, ALU.is_gt), Rd=[zR], Wr=[zR])
                        S.op("dve", lambda e: e.tensor_tensor(tq2[:, 0:520], tq2[:, 0:520], tq1[:, 0:520], ALU.subtract), Rd=[zR], Wr=[zR])
                        S.op("act", lambda e: e.activation(out=cosT[a][j2][:], in_=tq2[:, 0:520], func=AF.Sin, scale=2 * PI), Rd=[zR], Wr=[tabR[a][j2]])
                        S.op("act", lambda e: e.activation(out=c5s5[:, a, j2, 0:1], in_=tq2[:, 512:513], func=AF.Sin, scale=2 * PI), Rd=[zR], Wr=[tabR[a][j2]])
                        S.op("dve", lambda e: e.tensor_scalar(rT[a][j2][:], iot[:, 0:TT], 0.0, rS[:, col:col + 1], ALU.mult, ALU.add), Rd=[zR, cR], Wr=[tabR[a][j2]])

                    awb2 = [sb("awbC%d" % i, [128, KC, 512], F32, ph) for i in range(2)]
                    awR2 = [R(), R()]

                    def mods_block(jb):
                        b = jb % 2
                        S.dma("sp", awb2[b][:], adaw_in[l + 1, :, jb * 512:(jb + 1) * 512].rearrange("(kc p) n -> p kc n", p=128), Wr=[awR2[b]])
                        for jj in range(4):
                            col = jb * 4 + jj
                            for kc in range(KC):
                                S.op("pe", lambda e, jj=jj, kc=kc, col=col: e.matmul(banks[7][:, col:col + 1], awb2[b][:, kc, jj * 128:(jj + 1) * 128], cact[:, kc:kc + 1],
                                                                                      start=(kc == 0), stop=(kc == KC - 1)), Rd=[awR2[b], cR], Wr=[bR[7]])
                        if jb == 11:
                            S.op("dve", lambda e: e.tensor_tensor(mods[:, l + 1, :], banks[7][:, 0:48], abT[:, l + 1, :], ALU.add), Rd=[bR[7], cR], Wr=[modsR])

                    ssteps = []
                    for k in range(4):
                        for tt in range(NT if "C" in phases else 0):
                            for j2 in range(4):
                                ssteps.append(dict(k=k, tt=tt, j2=j2, n=len(ssteps)))
                    NSS = len(ssteps)

                    def c_bu(i):
                        st = ssteps[i]; k, tt, j2 = st["k"], st["tt"], st["j2"]; ub = k % 2; s = i % 2
                        ts = slice(tt * TT, (tt + 1) * TT)
                        if tt == 0 and j2 == 0:
                            S.dma("sp", uT[ub][:], u_scr[k * 128:(k + 1) * 128, :], Rd=[ur[k]], Wr=[uR[ub]])
                            if k == 0:
                                for q in range(4):
                                    gen_tables(0, q)
                            for q in range(4):
                                S.op("pool", lambda e, q=q: e.memset(ini2[k % 2][:, q, :], 0.0), Wr=[iniR2[k % 2][q]])
                        if k + 1 < 4 and j2 == 0 and tt in (1, 2, 3, 4):
                            gen_tables(k + 1, tt - 1)
                        S.op("pe", lambda e: e.matmul(banks[s][:], BpR[:, k, j2, :], uT[ub][:, ts], start=True, stop=True), Rd=[wtR, uR[ub]], Wr=[bR[s]])
                        S.op("pe", lambda e: e.matmul(banks[2 + s][:], BpI[:, k, j2, :], uT[ub][:, ts], start=True, stop=True), Rd=[wtR, uR[ub]], Wr=[bR[2 + s]])
                        r3 = i % NR3
                        S.op("act", lambda e: e.copy(bsr[r3][:], banks[s][:]), Rd=[bR[s]], Wr=[bsR[r3]])
                        S.op("act", lambda e: e.copy(bsi[r3][:], banks[2 + s][:]), Rd=[bR[2 + s]], Wr=[bsR[r3]])

                    def c_derot(i):
                        st = ssteps[i]; k, j2 = st["k"], st["j2"]; a = k % 2; r3 = i % NR3; m2 = i % 2
                        cs_, sn_ = cosT[a][j2][:, 0:TT], sinT[a][j2][:, 0:TT]
                        tr = tabR[a][j2]
                        S.op("dve", lambda e: e.tensor_tensor(mm_[m2][0][:], bsr[r3][:], cs_, ALU.mult), Rd=[bsR[r3], tr], Wr=[mR[m2]])
                        S.op("dve", lambda e: e.tensor_tensor(mm_[m2][1][:], bsi[r3][:], sn_, ALU.mult), Rd=[bsR[r3], tr], Wr=[mR[m2]])
                        S.op("dve", lambda e: e.tensor_tensor(mm_[m2][2][:], bsi[r3][:], cs_, ALU.mult), Rd=[bsR[r3], tr], Wr=[mR[m2]])
                        S.op("dve", lambda e: e.tensor_tensor(mm_[m2][3][:], bsr[r3][:], sn_, ALU.mult), Rd=[bsR[r3], tr], Wr=[mR[m2]])

                    def c_derot2(i):
                        r3 = i % NR3; m2 = i % 2
                        S.op("dve", lambda e: e.tensor_tensor(bpr[r3][:], mm_[m2][0][:], mm_[m2][1][:], ALU.add), Rd=[mR[m2]], Wr=[bpR[r3]])
                        S.op("dve", lambda e: e.tensor_tensor(bpi[r3][:], mm_[m2][2][:], mm_[m2][3][:], ALU.subtract), Rd=[mR[m2]], Wr=[bpR[r3]])

                    def c_scan(i):
                        st = ssteps[i]; k, j2 = st["k"], st["j2"]; a = k % 2; r3 = i % NR3
                        tr = tabR[a][j2]
                        ini = ini2[a]; iniR = iniR2[a]
                        S.op("dve", lambda e: e.tensor_tensor_scan(wre[r3][:], rT[a][j2][:], bpr[r3][:], ini[:, j2, 0:1], ALU.mult, ALU.add), Rd=[bpR[r3], tr, iniR[j2]], Wr=[wwR[r3]])
                        S.op("dve", lambda e: e.tensor_tensor_scan(wim[r3][:], rT[a][j2][:], bpi[r3][:], ini[:, j2, 1:2], ALU.mult, ALU.add), Rd=[bpR[r3], tr, iniR[j2]], Wr=[wwR[r3]])
                        S.op("act", lambda e: e.copy(wrb[r3][:], wre[r3][:]), Rd=[wwR[r3]], Wr=[wbR[r3]])
                        S.op("act", lambda e: e.copy(wib[r3][:], wim[r3][:]), Rd=[wwR[r3]], Wr=[wbR[r3]])
                        if st["tt"] < NT - 1:
                            c5, s5 = c5s5[:, a, j2, 0:1], c5s5[:, a, j2, 1:2]
                            wr_e, wi_e = wre[r3][:, 511:512], wim[r3][:, 511:512]
                            S.op("pool", lambda e: e.tensor_tensor(tinyt[:, 0:1], wi_e, s5, ALU.mult), Rd=[wwR[r3], tr, tinyR], Wr=[tinyR])
                            S.op("pool", lambda e: e.tensor_tensor(tinyt[:, 1:2], wr_e, s5, ALU.mult), Rd=[wwR[r3], tr, tinyR], Wr=[tinyR])
                            S.op("pool", lambda e: e.tensor_tensor(tinyt[:, 2:3], wr_e, c5, ALU.mult), Rd=[wwR[r3], tr, tinyR], Wr=[tinyR])
                            S.op("pool", lambda e: e.tensor_tensor(tinyt[:, 3:4], wi_e, c5, ALU.mult), Rd=[wwR[r3], tr, tinyR], Wr=[tinyR])
                            S.op("pool", lambda e: e.tensor_tensor(ini[:, j2, 0:1], tinyt[:, 2:3], tinyt[:, 0:1], ALU.subtract), Rd=[tinyR], Wr=[iniR[j2]])
                            S.op("pool", lambda e: e.tensor_tensor(ini[:, j2, 1:2], tinyt[:, 3:4], tinyt[:, 1:2], ALU.add), Rd=[tinyR], Wr=[iniR[j2]])

                    def c_rot(i):
                        st = ssteps[i]; k, j2 = st["k"], st["j2"]; a = k % 2; r3 = i % NR3
                        cs_, sn_ = cosT[a][j2][:, 0:TT], sinT[a][j2][:, 0:TT]
                        tr = tabR[a][j2]
                        S.op("dve", lambda e: e.tensor_tensor(pp[r3][0][:], wrb[r3][:], cs_, ALU.mult), Rd=[wbR[r3], tr], Wr=[ppR[r3]])
                        S.op("dve", lambda e: e.tensor_tensor(pp[r3][1][:], wib[r3][:], sn_, ALU.mult), Rd=[wbR[r3], tr], Wr=[ppR[r3]])
                        S.op("dve", lambda e: e.tensor_tensor(pp[r3][2][:], wrb[r3][:], sn_, ALU.mult), Rd=[wbR[r3], tr], Wr=[ppR[r3]])
                        S.op("dve", lambda e: e.tensor_tensor(pp[r3][3][:], wib[r3][:], cs_, ALU.mult), Rd=[wbR[r3], tr], Wr=[ppR[r3]])

                    def c_out(i):
                        st = ssteps[i]; k, tt, j2 = st["k"], st["tt"], st["j2"]; ub = k % 2; r3 = i % NR3
                        ts = slice(tt * TT, (tt + 1) * TT)
                        yk = 4 + (tt % 2); gk = 6; tb2 = tt % 2
                        for q, wt_ in enumerate((CpR, CpRn, CpI, CpI)):
                            S.op("pe", lambda e, q=q, wt_=wt_: e.matmul(banks[yk][:], wt_[:, k, j2, :], pp[r3][q][:], start=(j2 == 0 and q == 0), stop=(j2 == 3 and q == 3)), Rd=[wtR, ppR[r3]], Wr=[bR[yk]])
                        if j2 == 3:
                            S.op("dve", lambda e: e.scalar_tensor_tensor(yb[tb2][:], uT[ub][:, ts], pc("dsk", k), banks[yk][:], ALU.mult, ALU.add), Rd=[uR[ub], parR, bR[yk]], Wr=[ybR[tb2]])
                            S.op("act", lambda e: e.activation(out=yb[tb2][:], in_=yb[tb2][:], func=AF.Gelu_apprx_tanh), Rd=[ybR[tb2]], Wr=[ybR[tb2]])
                            S.op("act", lambda e: e.copy(ygb[tb2][:], yb[tb2][:]), Rd=[ybR[tb2]], Wr=[ygR[tb2]])
                            S.op("pe", lambda e: e.matmul(banks[gk][:], Gbd[:, k, :], ygb[tb2][:], start=True, stop=True), Rd=[wtR, ygR[tb2]], Wr=[bR[gk]])
                            S.op("act", lambda e: e.activation(out=gt[tb2][:], in_=banks[gk][:], func=AF.Sigmoid, bias=pc("glub", k)), Rd=[bR[gk], parR], Wr=[gtR[tb2]])
                            S.op("pool", lambda e: e.tensor_tensor(osg[tb2][:], yb[tb2][:], gt[tb2][:], ALU.mult), Rd=[ybR[tb2], gtR[tb2]], Wr=[osR[tb2]])
                            S.dma("sp", o_scr[512 + k * 128:512 + (k + 1) * 128, ts], osg[tb2][:], Rd=[osR[tb2]], Wr=[ors[4 + k]])

                    for slot in range(-1, NSS + 3):
                        if l + 1 < n_layers and slot >= 8 and slot % 10 == 0 and slot // 10 - 1 < 12:
                            mods_block(slot // 10 - 1)
                        if 0 <= slot + 1 < NSS:
                            c_bu(slot + 1)
                        if 0 <= slot < NSS:
                            c_derot(slot)
                        if 0 <= slot - 1 < NSS:
                            c_scan(slot - 1)
                        if 0 <= slot - 2 < NSS:
                            c_rot(slot - 2)
                        if 0 <= slot < NSS:
                            c_derot2(slot)
                        if 0 <= slot - 3 < NSS:
                            c_out(slot - 3)
                    S.barrier()

                with ExitStack() as ph:
                    wo = sb("woD", [128, KC, D], BF16, ph)
                    woR = [R() for _ in range(4)]
                    wst = [sb("wstD%d" % i, [128, KC, 256], F32, ph) for i in range(2)]
                    wstR = [R(), R()]
                    ot = [sb("otD%d" % i, [128, KC, TT], F32, ph) for i in range(2)]
                    otR = [R(), R()]
                    xt = [sb("xtD%d" % i, [128, KC, TT], F32, ph) for i in range(2)]
                    xtR = [R(), R()]
                    sqb = sb("sqbD", [128, KC, TT], BF16, ph)
                    sqR = R()
                    stda = sb("stdaD", [128, TT], F32, ph)
                    stds = sb("stdsD", [128, TT], F32, ph)
                    stdR = R()
                    on = [sb("onD%d" % i, [128, KC, TT], BF16, ph) for i in range(2)]
                    onR = [R(), R()]
                    NTD = NT if "D" in phases else 0

                    def d_load(tt):
                        b = tt % 2
                        ts = slice(tt * TT, (tt + 1) * TT)
                        S.dma("sp", ot[b][:], o_scr[:, ts].rearrange("(kc p) t -> p kc t", p=128), Rd=ors, Wr=[otR[b]])
                        S.dma("sp", xt[b][:], x_src[:, ts].rearrange("(kc p) t -> p kc t", p=128), Rd=([xr[tt]] if l > 0 else []), Wr=[xtR[b]])

                    def d_pro(tt):
                        b = tt % 2
                        S.op("act", lambda e: e.activation(out=sqb[:], in_=ot[b][:], func=AF.Square), Rd=[otR[b]], Wr=[sqR])
                        for c in range(8):
                            bk = 0 if c < 4 else 1
                            S.op("pe", lambda e, c=c, bk=bk: e.matmul(banks[bk][:], ones_bf[:], sqb[:, c, :], start=(c % 4 == 0), stop=(c % 4 == 3)), Rd=[sqR, cR], Wr=[bR[bk]])
                        S.op("act", lambda e: e.activation(out=stda[:], in_=banks[0][:], func=AF.Ln, bias=EPS, scale=1.0 / 512), Rd=[bR[0]], Wr=[stdR])
                        S.op("act", lambda e: e.activation(out=stds[:], in_=banks[1][:], func=AF.Ln, bias=EPS, scale=1.0 / 512), Rd=[bR[1]], Wr=[stdR])
                        S.op("act", lambda e: e.activation(out=stda[:], in_=stda[:], func=AF.Exp, scale=-0.5), Rd=[stdR], Wr=[stdR])
                        S.op("act", lambda e: e.activation(out=stds[:], in_=stds[:], func=AF.Exp, scale=-0.5), Rd=[stdR], Wr=[stdR])
                        for c in range(8):
                            gcol = pc("aog", c) if c < 4 else pc("sog", c - 4)
                            sd = stda if c < 4 else stds
                            S.op("dve", lambda e, c=c, gcol=gcol, sd=sd: e.scalar_tensor_tensor(on[b][:, c, :], ot[b][:, c, :], gcol, sd[:], ALU.mult, ALU.mult), Rd=[otR[b], stdR, parR], Wr=[onR[b]])

                    if NTD:
                        d_load(0)
                    for pi in range(4):
                        b = pi % 2
                        S.dma("sp", wst[b][:], wout_in[l, :, pi * 256:(pi + 1) * 256].rearrange("(kc p) n -> p kc n", p=128), Wr=[wstR[b]])
                        S.op("act", lambda e, b=b, pi=pi: e.copy(wo[:, :, pi * 256:(pi + 1) * 256], wst[b][:]), Rd=[wstR[b]], Wr=[woR[pi]])
                    if NTD:
                        d_load(1)
                        d_pro(0)
                    for tt in range(NTD):
                        b = tt % 2
                        ts = slice(tt * TT, (tt + 1) * TT)
                        for co in range(8):
                            bk = 2 + (co % 4)
                            for kc in range(KC):
                                S.op("pe", lambda e, kc=kc, co=co, bk=bk: e.matmul(banks[bk][:], wo[:, kc, co * 128:(co + 1) * 128], on[b][:, kc, :], start=(kc == 0), stop=(kc == KC - 1)), Rd=[woR[co // 2], onR[b]], Wr=[bR[bk]])
                            S.op("dve", lambda e, co=co, bk=bk, b=b: e.scalar_tensor_tensor(xt[b][:, co, :], banks[bk][:], g1(co), xt[b][:, co, :], ALU.mult, ALU.add), Rd=[bR[bk], modsR, xtR[b]], Wr=[xtR[b]])
                            if co == 3 and tt + 1 < NTD:
                                d_pro(tt + 1)
                        S.dma("pool", x_scr[:, ts].rearrange("(kc p) t -> p kc t", p=128), xt[b][:], Rd=[xtR[b]], Wr=[xr[tt]])
                        if tt + 2 < NTD:
                            d_load(tt + 2)
                    S.barrier()

                with ExitStack() as ph:
                    wup = sb("wupE", [128, KC, 2 * DFF], BF16, ph)
                    wupR = [R() for _ in range(22)]
                    wst = [sb("wstE%d" % i, [128, KC, 256], F32, ph) for i in range(2)]
                    wstR = [R(), R()]
                    xt = [sb("xtE%d" % i, [128, KC, TT], F32, ph) for i in range(2)]
                    xtR = [R(), R()]
                    sqb = sb("sqbE", [128, KC, TT], BF16, ph)
                    sqR = R()
                    std = sb("stdE", [128, TT], F32, ph)
                    stdR = R()
                    xn = [sb("xnE%d" % i, [128, TT], F32, ph) for i in range(2)]
                    xnR = [R(), R()]
                    hb = [sb("hbE%d" % i, [128, KC, TT], BF16, ph) for i in range(2)]
                    hR = [R(), R()]
                    hc = [sb("hcE%d" % i, [128, 2 * NFF, 2], F32, ph) for i in range(2)]
                    hcR = [[R() for _ in range(2 * NFF)] for _ in range(2)]
                    hcn = sb("hcnE", [128, 2 * NFF, 2], F32, ph)
                    hcn2 = sb("hcn2E", [128, 2 * NFF, 2], F32, ph)
                    hcnR = [R() for _ in range(2 * NFF)]
                    tv = [sb("tvE%d" % i, [128, TT], F32, ph) for i in range(3)]
                    tg = [sb("tgE%d" % i, [128, TT], F32, ph) for i in range(3)]
                    tvR = [R() for _ in range(3)]; tgR = [R() for _ in range(3)]
                    actb = [sb("actE%d" % i, [128, TT], BF16, ph) for i in range(4)]
                    actR = [R() for _ in range(4)]
                    S.op("pool", lambda e: e.memset(hc[0][:], 0.0), Wr=hcR[0])
                    S.op("pool", lambda e: e.memset(hcn2[:], 0.0), Wr=hcnR)
                    cwc = lambda i, cidx: pc("cw", i * 44 + cidx)
                    NTE = NT if "E" in phases else 0

                    def e_load(tt):
                        b = tt % 2
                        ts = slice(tt * TT, (tt + 1) * TT)
                        S.dma("sp", xt[b][:], x_scr[:, ts].rearrange("(kc p) t -> p kc t", p=128), Rd=[xr[tt]], Wr=[xtR[b]])

                    def e_pro(tt):
                        b = tt % 2
                        S.op("act", lambda e: e.activation(out=sqb[:], in_=xt[b][:], func=AF.Square), Rd=[xtR[b]], Wr=[sqR])
                        for kc in range(KC):
                            S.op("pe", lambda e, kc=kc: e.matmul(banks[0][:], ones_bf[:], sqb[:, kc, :], start=(kc == 0), stop=(kc == KC - 1)), Rd=[sqR, cR], Wr=[bR[0]])
                        S.op("act", lambda e: e.activation(out=std[:], in_=banks[0][:], func=AF.Ln, bias=EPS, scale=1.0 / D), Rd=[bR[0]], Wr=[stdR])
                        S.op("act", lambda e: e.activation(out=std[:], in_=std[:], func=AF.Exp, scale=-0.5), Rd=[stdR], Wr=[stdR])
                        for kc in range(KC):
                            i2 = kc % 2
                            S.op("dve", lambda e, kc=kc: e.tensor_tensor(xn[i2][:], xt[b][:, kc, :], std[:], ALU.mult), Rd=[xtR[b], stdR], Wr=[xnR[i2]])
                            S.op("act", lambda e, kc=kc: e.activation(out=hb[b][:, kc, :], in_=xn[i2][:], func=AF.Identity, bias=sh2(kc), scale=gs2(kc)), Rd=[xnR[i2], derR, modsR], Wr=[hR[b]])

                    if NTE:
                        e_load(0)
                    order = []
                    for q in range(11):
                        order += [q, 11 + q]
                    def e_wload(n_):
                        pi = order[n_]
                        b = n_ % 2
                        S.dma("sp", wst[b][:], wup_in[l, :, pi * 256:(pi + 1) * 256].rearrange("(kc p) n -> p kc n", p=128), Wr=[wstR[b]])
                        S.op("act", lambda e: e.copy(wup[:, :, pi * 256:(pi + 1) * 256], wst[b][:]), Rd=[wstR[b]], Wr=[wupR[pi]])

                    for n_ in range(4):
                        e_wload(n_)
                        if n_ == 1 and NTE:
                            e_load(1)
                            e_pro(0)
                    if not NTE:
                        for n_ in range(4, 22):
                            e_wload(n_)
                    pend = []

                    def e_fin(s, j, tt):
                        ts = slice(tt * TT, (tt + 1) * TT)
                        S.op("act", lambda e: e.activation(out=tg[s][:], in_=tg[s][:], func=AF.Gelu_apprx_tanh), Rd=[tgR[s]], Wr=[tgR[s]])
                        a4 = j % 4
                        S.op("pool", lambda e: e.tensor_tensor(actb[a4][:], tg[s][:], tv[s][:], ALU.mult), Rd=[tgR[s], tvR[s]], Wr=[actR[a4]])
                        S.dma("sp", act_scr[j * 128:(j + 1) * 128, ts], actb[a4][:], Rd=[actR[a4]], Wr=[acr[tt][j]])

                    jstep = 0
                    for tt in range(NTE):
                        b = tt % 2
                        ts = slice(tt * TT, (tt + 1) * TT)
                        for j in range(NFF):
                            s = jstep % 3
                            jstep += 1
                            for which, (cidx, tdst, tdR) in enumerate(((j, tv[s], tvR[s]), (NFF + j, tg[s], tgR[s]))):
                                bk = 1 + 2 * s + which
                                h6 = 2 * s + which
                                for kc in range(KC):
                                    S.op("pe", lambda e, kc=kc, cidx=cidx, bk=bk: e.matmul(banks[bk][:], wup[:, kc, cidx * 128:(cidx + 1) * 128], hb[b][:, kc, :], start=(kc == 0), stop=(kc == KC - 1)), Rd=[wupR[cidx // 2], hR[b]], Wr=[bR[bk]])
                                hp_, hn_ = tt % 2, (tt + 1) % 2
                                S.op("act", lambda e, cidx=cidx, tdst=tdst, bk=bk: e.activation(out=tdst[:], in_=banks[bk][:], func=AF.Identity, bias=pc("cb", cidx), scale=cwc(2, cidx)), Rd=[bR[bk], parR], Wr=[tdR])
                                if tt + 1 < NTE:
                                    S.op("act", lambda e, cidx=cidx, bk=bk: e.activation(out=hcn[:, cidx, 0:2], in_=banks[bk][:, TT - 2:TT], func=AF.Identity, scale=cwc(0, cidx)), Rd=[parR], Wr=[hcnR[cidx], bR[bk]])
                                    S.op("act", lambda e, cidx=cidx, bk=bk: e.activation(out=hcn2[:, cidx, 0:1], in_=banks[bk][:, TT - 1:TT], func=AF.Identity, scale=cwc(1, cidx)), Rd=[parR], Wr=[hcnR[cidx], bR[bk]])
                                S.op("dve", lambda e, cidx=cidx, tdst=tdst, bk=bk: e.scalar_tensor_tensor(tdst[:, 1:TT], banks[bk][:, 0:TT - 1], cwc(1, cidx), tdst[:, 1:TT], ALU.mult, ALU.add), Rd=[parR, tdR], Wr=[tdR, bR[bk]])
                                S.op("dve", lambda e, cidx=cidx, tdst=tdst, bk=bk: e.scalar_tensor_tensor(tdst[:, 2:TT], banks[bk][:, 0:TT - 2], cwc(0, cidx), tdst[:, 2:TT], ALU.mult, ALU.add), Rd=[parR, tdR], Wr=[tdR, bR[bk]])
                                S.op("pool", lambda e, cidx=cidx, tdst=tdst, hp_=hp_: e.tensor_tensor(tdst[:, 0:2], tdst[:, 0:2], hc[hp_][:, cidx, :], ALU.add), Rd=[hcR[hp_][cidx], tdR], Wr=[tdR])
                                if tt + 1 < NTE:
                                    S.op("pool", lambda e, cidx=cidx, hn_=hn_: e.tensor_tensor(hc[hn_][:, cidx, :], hcn[:, cidx, :], hcn2[:, cidx, :], ALU.add), Rd=[hcnR[cidx]], Wr=[hcR[hn_][cidx]])
                            if pend:
                                e_fin(*pend.pop())
                            pend.append((s, j, tt))
                            if tt == 0 and j % 2 == 0 and 4 + j < 22:
                                e_wload(4 + j)
                                e_wload(5 + j)
                            if j == 8 and tt + 1 < NTE:
                                e_pro(tt + 1)
                        if tt + 2 < NTE:
                            e_load(tt + 2)
                    if pend:
                        e_fin(*pend.pop())
                    S.barrier()

                with ExitStack() as ph:
                    wdn = sb("wdnE", [128, NFF, D], BF16, ph)
                    wdnR = [R() for _ in range(11)]
                    wst = [sb("wstF%d" % i, [128, 2, D], F32, ph) for i in range(2)]
                    wstR = [R(), R()]
                    xt = [sb("xtF%d" % i, [128, KC, TT], F32, ph) for i in range(2)]
                    xtR = [R(), R()]
                    actb = [sb("actF%d" % i, [128, NFF, TT], BF16, ph) for i in range(2)]
                    actR = [R(), R()]
                    dst = yT_out if l == n_layers - 1 else x_scr
                    NTE = NT if "F" in phases else 0

                    def f_load(tt):
                        b = tt % 2
                        ts = slice(tt * TT, (tt + 1) * TT)
                        S.dma("sp", actb[b][:], act_scr[:, ts].rearrange("(c p) t -> p c t", p=128), Rd=acr[tt], Wr=[actR[b]])
                        S.dma("sp", xt[b][:], x_scr[:, ts].rearrange("(kc p) t -> p kc t", p=128), Rd=[xr[tt]], Wr=[xtR[b]])

                    if NTE:
                        f_load(0)
                    for pi in range(11):
                        b = pi % 2
                        S.dma("sp", wst[b][:], wdn_in[l, pi * 256:(pi + 1) * 256, :].rearrange("(c p) n -> p c n", p=128), Wr=[wstR[b]])
                        S.op("act", lambda e, b=b, pi=pi: e.copy(wdn[:, 2 * pi:2 * pi + 2, :], wst[b][:]), Rd=[wstR[b]], Wr=[wdnR[pi]])
                    if NTE:
                        f_load(1)
                    for tt in range(NTE):
                        b = tt % 2
                        ts = slice(tt * TT, (tt + 1) * TT)
                        for co in range(8):
                            bk = co % 4
                            for j in range(NFF):
                                S.op("pe", lambda e, j=j, co=co, bk=bk, b=b: e.matmul(banks[bk][:], wdn[:, j, co * 128:(co + 1) * 128], actb[b][:, j, :], start=(j == 0), stop=(j == NFF - 1)), Rd=[wdnR[j // 2], actR[b]], Wr=[bR[bk]])
                            S.op("dve", lambda e, co=co, bk=bk, b=b: e.scalar_tensor_tensor(xt[b][:, co, :], banks[bk][:], g2(co), xt[b][:, co, :], ALU.mult, ALU.add), Rd=[bR[bk], modsR, xtR[b]], Wr=[xtR[b]])
                        S.dma("pool", dst[:, ts].rearrange("(kc p) t -> p kc t", p=128), xt[b][:], Rd=[xtR[b]], Wr=[xr[tt]])
                        if tt + 2 < NTE:
                            f_load(tt + 2)
                    S.barrier()
                S.barrier()
        if dbg:
            with ExitStack() as ph:
                t1 = sb("cp1", [128, 8, 2048], F32, ph)
                tR = R()
                for hhalf in range(2):
                    S.dma("sp", t1[:], o_scr[:, hhalf * 2048:(hhalf + 1) * 2048].rearrange("(c p) t -> p c t", p=128), Rd=ors, Wr=[tR])
                    S.dma("sp", dbg_o[:, hhalf * 2048:(hhalf + 1) * 2048].rearrange("(c p) t -> p c t", p=128), t1[:], Rd=[tR])
                S.barrier()
        S.barrier()
    print("sched: ops=%d waits=%d" % (S.nops, S.nwait))
    return nc


def _pack_params(inp, l):
    P = np.zeros((128, NPAR), np.float32)

    def put(name, arr):
        lo, hi = _cols[name]
        P[:, lo:hi] = np.asarray(arr, np.float32).reshape(128, hi - lo)

    f = lambda k: np.asarray(inp[k][l], np.float32)
    put("ada_b", f("ada_b").reshape(48, 128).T)
    put("n1g", f("norm1_g").reshape(8, 128).T)
    put("n2g", f("norm2_g").reshape(8, 128).T)
    put("qg", np.tile(f("q_norm_g"), 2))
    put("kg", np.tile(f("k_norm_g"), 2))
    put("aog", f("attn_out_g").reshape(4, 128).T)
    put("sog", f("ssm_out_g").reshape(4, 128).T)
    are = f("ssm_a_re").reshape(4, 8, 64)
    aim = f("ssm_a_im").reshape(4, 8, 64)
    ldt = f("ssm_log_dt").reshape(4, 8)
    put("aBre", np.broadcast_to(are.transpose(1, 0, 2)[:, None], (8, 16, 4, 64)))
    put("aBim", np.broadcast_to(aim.transpose(1, 0, 2)[:, None], (8, 16, 4, 64)))
    put("ldtB", np.broadcast_to(ldt.T[:, None], (8, 16, 4)))
    bre = f("ssm_b_re").reshape(4, 8, 64, 16)
    bim = f("ssm_b_im").reshape(4, 8, 64, 16)
    put("BreB", bre.transpose(1, 3, 0, 2))
    put("BimB", bim.transpose(1, 3, 0, 2))
    are2 = f("ssm_a_re").reshape(4, 4, 2, 64)
    aim2 = f("ssm_a_im").reshape(4, 4, 2, 64)
    ldt2 = f("ssm_log_dt").reshape(4, 4, 2)
    put("aSre", are2.transpose(2, 3, 0, 1))
    put("aSim", aim2.transpose(2, 3, 0, 1))
    put("ldtS", np.broadcast_to(ldt2.transpose(2, 0, 1)[:, None], (2, 64, 4, 4)))
    cre = f("ssm_c_re").reshape(4, 4, 2, 16, 64)
    cim = f("ssm_c_im").reshape(4, 4, 2, 16, 64)
    put("CreS", cre.transpose(2, 4, 0, 1, 3))
    put("CimS", cim.transpose(2, 4, 0, 1, 3))
    put("dsk", f("ssm_d").reshape(4, 128).T)
    put("GS", f("glu_w").reshape(4, 8, 16, 16).transpose(1, 2, 0, 3))
    put("glub", f("glu_b").reshape(4, 128).T)
    put("cw", f("ffn_conv_w").reshape(3, 44, 128).transpose(2, 0, 1))
    put("cb", f("ffn_conv_b").reshape(44, 128).T)
    return P


def _in_maps(inp):
    par = np.stack([_pack_params(inp, l) for l in range(DEPTH)], 0)
    shared = {
        "par": par,
        "ada_w": np.ascontiguousarray(inp["ada_w"], np.float32),
        "w_in": np.ascontiguousarray(inp["w_in"], np.float32),
        "w_out": np.ascontiguousarray(inp["w_out"], np.float32),
        "w_up": np.ascontiguousarray(inp["ffn_w_up"], np.float32),
        "w_dn": np.ascontiguousarray(inp["ffn_w_down"], np.float32),
    }
    maps = []
    for b in range(8):
        m = dict(shared)
        m["xT"] = np.ascontiguousarray(np.asarray(inp["x"][b], np.float32).T)
        m["cT"] = np.ascontiguousarray(np.asarray(inp["c"][b], np.float32).reshape(KC, 128).T)
        maps.append(m)
    return maps


def kernel(**inputs):
    inp = {k: np.asarray(v) for k, v in inputs.items()}
    nc = build(DEPTH)
    res = run_bass_kernel_spmd(nc, _in_maps(inp), core_ids=list(range(8)))
    out = np.stack([np.ascontiguousarray(r["yT"].T) for r in res.results], 0)
    return out.astype(np.float32)
```

```python
import numpy as np
from contextlib import ExitStack
import concourse.bass as bass
import concourse.mybir as mybir
from concourse.bass_utils import run_bass_kernel_spmd

F32 = mybir.dt.float32
BF16 = mybir.dt.bfloat16
I32 = mybir.dt.int32
AF = mybir.ActivationFunctionType
ALU = mybir.AluOpType

D = 1024
L = 4096
DEPTH = 4
TT = 512
NT = L // TT
KC = D // 128
DFF = 2816
NFF = DFF // 128
EPS = 1e-6
PI = float(np.pi)

_cols = {}
_off = 0
for _n, _w in [("ada_b", 48), ("n1g", 8), ("n2g", 8), ("qg", 1), ("kg", 1), ("aog", 4), ("sog", 4),
               ("aBre", 256), ("aBim", 256), ("ldtB", 4), ("BreB", 256), ("BimB", 256),
               ("aSre", 16), ("aSim", 16), ("ldtS", 16), ("CreS", 256), ("CimS", 256),
               ("dsk", 4), ("GS", 64), ("glub", 4), ("cw", 132), ("cb", 44)]:
    _cols[_n] = (_off, _off + _w)
    _off += _w
NPAR = _off


class R:
    __slots__ = ("w", "r", "excl")

    def __init__(self, excl=False):
        self.w = None
        self.r = []
        self.excl = excl


class _Cap:
    def __getattr__(self, name):
        return lambda *a, **k: (name, a, k)


_CAP = _Cap()


class Sched:
    ND = 24

    def __init__(self, nc):
        self.nc = nc
        self.eng = {"pe": nc.tensor, "act": nc.scalar, "dve": nc.vector, "pool": nc.gpsimd, "sp": nc.sync}
        self.sem = {e: nc.alloc_semaphore("c_" + e) for e in self.eng}
        self.cnt = {e: 0 for e in self.eng}
        self.seen = {e: {x: 0 for x in self.eng} for e in self.eng}
        self.dsem = [nc.alloc_semaphore("d_%d" % i) for i in range(self.ND)]
        self.dcnt = [0] * self.ND
        self.dseen = {e: [0] * self.ND for e in self.eng}
        self.dnext = 0
        self.nwait = 0
        self.nops = 0
        self.tinyops = set()

    def _wait_tok(self, eng, tok, is_dma=False, tiny=False):
        e = self.eng[eng]
        if tok[0] == "c":
            _, x, idx = tok
            if x == eng:
                if not is_dma:
                    if eng == "pe" or self.cnt[eng] - idx >= 3:
                        return
                    if not tiny and (x, idx) not in self.tinyops:
                        return
            if self.seen[eng][x] >= idx:
                return
            e.wait_ge(self.sem[x], idx)
            self.seen[eng][x] = idx
            self.nwait += 1
        else:
            _, slot, val = tok
            if self.dseen[eng][slot] >= val:
                return
            e.wait_ge(self.dsem[slot], val)
            self.dseen[eng][slot] = val
            self.nwait += 1

    def _deps(self, eng, reads, writes, is_dma=False, tiny=False):
        toks = []
        for r in reads:
            if r.w is not None:
                toks.append(r.w)
        for w in writes:
            if w.w is not None:
                toks.append(w.w)
            toks.extend(w.r)
        for t in toks:
            self._wait_tok(eng, t, is_dma, tiny)

    def _commit(self, tok, reads, writes):
        for r in reads:
            r.r.append(tok)
        for w in writes:
            w.w = tok
            w.r = []

    def op(self, eng, fn, Rd=(), Wr=(), tiny=None):
        if any(r.excl for r in Rd):
            Wr = list(Wr) + [r for r in Rd if r.excl and r not in Wr]
            Rd = [r for r in Rd if not r.excl]
        name, a, k = fn(_CAP)
        if tiny is None:
            o = k.get("out", None)
            if o is None:
                o = k.get("ap", None)
            if o is None:
                o = a[0]
            n = 1
            for d in list(o.shape)[1:]:
                n *= int(d)
            tiny = n < 192
        self._deps(eng, Rd, Wr, False, tiny)
        ins = getattr(self.eng[eng], name)(*a, **k)
        self.cnt[eng] += 1
        if tiny:
            self.tinyops.add((eng, self.cnt[eng]))
        ins.then_inc(self.sem[eng], 1)
        self._commit(("c", eng, self.cnt[eng]), Rd, Wr)
        self.nops += 1
        return ins

    def dma(self, q, out, in_, Rd=(), Wr=()):
        self._deps(q, Rd, Wr, True)
        slot = self.dnext % self.ND
        self.dnext += 1
        if self.dcnt[slot] > 0:
            self._wait_tok(q, ("d", slot, 16 * self.dcnt[slot]))
        ins = self.eng[q].dma_start(out=out, in_=in_)
        self.dcnt[slot] += 1
        ins.then_inc(self.dsem[slot], 16)
        self._commit(("d", slot, 16 * self.dcnt[slot]), Rd, Wr)
        self.nops += 1
        return ins

    def barrier(self, engines=None):
        for e in self.eng:
            for x in self.eng:
                if x != e and self.cnt[x] > 0:
                    self._wait_tok(e, ("c", x, self.cnt[x]))
            for s in range(self.ND):
                if self.dcnt[s] > 0:
                    self._wait_tok(e, ("d", s, 16 * self.dcnt[s]))


def build(n_layers=DEPTH, dbg=False, nq_tiles=NT, phases="ABCDEF", dumps=None):
    nc = bass.Bass("TRN2", target_bir_lowering=False)
    S = Sched(nc)
    _dd = {}

    def dump(name, tile, Rd, cond=True):
        if dumps is None or not cond or name in _dd:
            return
        shp = list(tile.shape)
        dt_ = tile.dtype
        dten = nc.dram_tensor("dump_" + name, shp, dt_, kind="ExternalOutput")
        _dd[name] = dten
        dumps.append("dump_" + name)
        S.dma("sp", dten.ap() if len(shp) == 2 else dten.ap(), tile, Rd=Rd)

    xT_in = nc.dram_tensor("xT", [D, L], F32, kind="ExternalInput")
    cT_in = nc.dram_tensor("cT", [128, KC], F32, kind="ExternalInput")
    par_in = nc.dram_tensor("par", [DEPTH, 128, NPAR], F32, kind="ExternalInput")
    adaw_in = nc.dram_tensor("ada_w", [DEPTH, D, 6 * D], F32, kind="ExternalInput")
    win_in = nc.dram_tensor("w_in", [DEPTH, D, 2048], F32, kind="ExternalInput")
    wout_in = nc.dram_tensor("w_out", [DEPTH, D, D], F32, kind="ExternalInput")
    wup_in = nc.dram_tensor("w_up", [DEPTH, D, 2 * DFF], F32, kind="ExternalInput")
    wdn_in = nc.dram_tensor("w_dn", [DEPTH, DFF, D], F32, kind="ExternalInput")
    yT_out = nc.dram_tensor("yT", [D, L], F32, kind="ExternalOutput")

    x_scr = nc.dram_tensor("x_scr", [D, L], F32, kind="Internal")
    qk_scr = nc.dram_tensor("qk_scr", [1024, L], BF16, kind="Internal")
    v_scr = nc.dram_tensor("v_scr", [L, 512], BF16, kind="Internal")
    u_scr = nc.dram_tensor("u_scr", [512, L], BF16, kind="Internal")
    o_scr = nc.dram_tensor("o_scr", [D, L], F32, kind="Internal")
    act_scr = nc.dram_tensor("act_scr", [DFF, L], BF16, kind="Internal")
    if dbg:
        dbg_mods = nc.dram_tensor("dbg_mods", [128, DEPTH * 48], F32, kind="ExternalOutput")
        dbg_qk = nc.dram_tensor("dbg_qk", [1024, L], BF16, kind="ExternalOutput")
        dbg_v = nc.dram_tensor("dbg_v", [L, 512], BF16, kind="ExternalOutput")
        dbg_u = nc.dram_tensor("dbg_u", [512, L], BF16, kind="ExternalOutput")
        dbg_o = nc.dram_tensor("dbg_o", [D, L], F32, kind="ExternalOutput")
        dbg_x1 = nc.dram_tensor("dbg_x1", [D, L], F32, kind="ExternalOutput")

    xr = [R() for _ in range(NT)]
    qkr = [R() for _ in range(8)]
    vr = R()
    ur = [R() for _ in range(4)]
    ors = [R() for _ in range(8)]
    acr = [[R() for _ in range(NFF)] for _ in range(NT)]

    with ExitStack() as top:
        _uid = [0]

        def sb(name, shape, dt, stack=top):
            _uid[0] += 1
            return stack.enter_context(nc.sbuf_tensor("%s_%d" % (name, _uid[0]), shape, dt))

        banks = [top.enter_context(nc.psum_tensor("ps%d" % i, [128, 512], F32)) for i in range(8)]
        bR = [R(excl=True) for _ in range(8)]

        ones_bf = sb("ones_bf", [128, 128], BF16)
        blk_bf = sb("blk_bf", [128, 128], BF16)
        negU = sb("negU", [128, 128], BF16)
        negOnes = sb("negOnes", [128, 128], BF16)
        masks = sb("masks", [128, 4, 512], BF16)
        M8 = sb("M8", [128, 8], F32)
        M2 = sb("M2", [128, 4, 8], F32)
        iot = sb("iot", [128, 520], F32)
        mods = sb("mods", [128, DEPTH, 48], F32)
        cR = R()
        modsR = R()
        with ExitStack() as ph:
            tmpi = sb("tmpi", [128, 2048], I32, ph)
            tmpf = sb("tmpf", [128, 2048], F32, ph)
            tmpg = sb("tmpg", [128, 2048], F32, ph)
            S.op("pool", lambda e: e.memset(ones_bf[:], 1.0), Wr=[cR])
            S.op("pool", lambda e: e.memset(negOnes[:], -1.0), Wr=[cR])
            S.op("pool", lambda e: e.memset(blk_bf[:], 0.0), Wr=[cR])
            S.op("pool", lambda e: e.memset(blk_bf[0:64, 0:64], 1.0), Wr=[cR])
            S.op("pool", lambda e: e.memset(blk_bf[64:128, 64:128], 1.0), Wr=[cR])
            S.op("pool", lambda e: e.iota(tmpi[:, 0:128], [[-1, 128]], base=0, channel_multiplier=1), Wr=[cR])
            S.op("dve", lambda e: e.tensor_copy(tmpf[:, 0:128], tmpi[:, 0:128]), Rd=[cR], Wr=[cR])
            S.op("dve", lambda e: e.tensor_scalar(negU[:], tmpf[:, 0:128], 0.0, -1.0, ALU.is_ge, ALU.mult), Rd=[cR], Wr=[cR])
            S.op("pool", lambda e: e.iota(tmpi[:, 0:2048], [[-128, 4], [1, 512]], base=0, channel_multiplier=-1), Wr=[cR])
            S.op("dve", lambda e: e.tensor_copy(tmpf[:, 0:2048], tmpi[:, 0:2048]), Rd=[cR], Wr=[cR])
            S.op("dve", lambda e: e.tensor_scalar(masks[:].rearrange("p o f -> p (o f)"), tmpf[:, 0:2048], 0.0, None, ALU.is_gt), Rd=[cR], Wr=[cR])
            S.op("pool", lambda e: e.iota(tmpi[:, 0:8], [[-16, 8]], base=0, channel_multiplier=1), Wr=[cR])
            S.op("dve", lambda e: e.tensor_copy(tmpf[:, 0:8], tmpi[:, 0:8]), Rd=[cR], Wr=[cR])
            S.op("dve", lambda e: e.tensor_scalar(tmpg[:, 0:8], tmpf[:, 0:8], 0.0, None, ALU.is_ge), Rd=[cR], Wr=[cR])
            S.op("dve", lambda e: e.tensor_scalar(tmpf[:, 0:8], tmpf[:, 0:8], 15.5, None, ALU.is_le), Rd=[cR], Wr=[cR])
            S.op("dve", lambda e: e.tensor_tensor(M8[:], tmpf[:, 0:8], tmpg[:, 0:8], ALU.mult), Rd=[cR], Wr=[cR])
            for g2 in range(2):
                S.op("pool", lambda e, g2=g2: e.iota(tmpi[64 * g2:64 * g2 + 64, 0:32], [[-2, 4], [1, 8]], base=-g2, channel_multiplier=0), Wr=[cR])
            S.op("dve", lambda e: e.tensor_copy(tmpf[:, 0:32], tmpi[:, 0:32]), Rd=[cR], Wr=[cR])
            S.op("dve", lambda e: e.tensor_scalar(M2[:].rearrange("p a b -> p (a b)"), tmpf[:, 0:32], 0.0, None, ALU.is_equal), Rd=[cR], Wr=[cR])
            S.op("pool", lambda e: e.iota(tmpi[:, 0:520], [[1, 520]], base=0, channel_multiplier=0), Wr=[cR])
            S.op("dve", lambda e: e.tensor_copy(iot[:], tmpi[:, 0:520]), Rd=[cR], Wr=[cR])

            dump("M8", M8[:], [cR]); dump("M2", M2[:].rearrange("p a b -> p (a b)"), [cR]); dump("iot", iot[:], [cR])
            cT = sb("cT_sb", [128, KC], F32, ph)
            cact = sb("cact", [128, KC], F32, ph)
            abT = sb("abT", [128, DEPTH, 48], F32, ph)
            awb = [sb("awb%d" % i, [128, KC, 512], F32, ph) for i in range(2)]
            awR = [R(), R()]
            S.dma("sp", cT[:], cT_in[:, :], Wr=[cR])
            for l in range(n_layers):
                S.dma("sp", abT[:, l, :], par_in[l, :, _cols["ada_b"][0]:_cols["ada_b"][1]], Wr=[cR])
            S.op("act", lambda e: e.activation(out=cact[:], in_=cT[:], func=AF.Silu), Rd=[cR], Wr=[cR])
            it = 0
            for l in range(n_layers):
                for jb in range(12):
                    b = it % 2
                    it += 1
                    S.dma("sp", awb[b][:], adaw_in[l, :, jb * 512:(jb + 1) * 512].rearrange("(kc p) n -> p kc n", p=128), Wr=[awR[b]])
                    for jj in range(4):
                        col = l * 48 + jb * 4 + jj
                        for kc in range(KC):
                            S.op("pe", lambda e, b=b, jj=jj, kc=kc, col=col: e.matmul(
                                banks[0][:, col:col + 1], awb[b][:, kc, jj * 128:(jj + 1) * 128], cact[:, kc:kc + 1],
                                start=(kc == 0), stop=(kc == KC - 1)), Rd=[awR[b], cR], Wr=[bR[0]])
            S.op("dve", lambda e: e.tensor_tensor(mods[:, 0:n_layers, :].rearrange("p l j -> p (l j)"), banks[0][:, 0:n_layers * 48],
                                                  abT[:, 0:n_layers, :].rearrange("p l j -> p (l j)"), ALU.add), Rd=[bR[0], cR], Wr=[modsR])
            if dbg:
                S.dma("sp", dbg_mods[:, 0:n_layers * 48], mods[:, 0:n_layers, :].rearrange("p l j -> p (l j)"), Rd=[modsR])
            S.barrier()

        for l in range(n_layers):
            x_src = xT_in if l == 0 else x_scr
            with ExitStack() as lay:
                par = sb("par_sb", [128, NPAR], F32, lay)
                parR = R()
                S.dma("sp", par[:], par_in[l, :, :], Wr=[parR])

                def pc(name, a=None, b=None):
                    lo, hi = _cols[name]
                    if a is None:
                        return par[:, lo:hi]
                    return par[:, lo + a:lo + (b if b is not None else a + 1)]

                der = sb("der", [128, 32], F32, lay)
                derR = R()
                S.op("dve", lambda e: e.scalar_tensor_tensor(der[:, 0:8], mods[:, l, 8:16], 1.0, pc("n1g"), ALU.add, ALU.mult), Rd=[modsR, parR], Wr=[derR])
                S.op("dve", lambda e: e.scalar_tensor_tensor(der[:, 8:16], mods[:, l, 32:40], 1.0, pc("n2g"), ALU.add, ALU.mult), Rd=[modsR, parR], Wr=[derR])
                S.op("dve", lambda e: e.tensor_scalar(der[:, 16:17], pc("qg"), 0.125, None, ALU.mult), Rd=[parR], Wr=[derR])
                gs1 = lambda kc: der[:, kc:kc + 1]
                gs2 = lambda kc: der[:, 8 + kc:9 + kc]
                gq = der[:, 16:17]
                sh1 = lambda kc: mods[:, l, kc:kc + 1]
                g1 = lambda kc: mods[:, l, 16 + kc:17 + kc]
                sh2 = lambda kc: mods[:, l, 24 + kc:25 + kc]
                g2 = lambda kc: mods[:, l, 40 + kc:41 + kc]

                with ExitStack() as ph:
                    win = sb("win", [128, KC, 2048], BF16, ph)
                    winR = [R() for _ in range(8)]
                    wst = [sb("wstA%d" % i, [128, KC, 256], F32, ph) for i in range(2)]
                    wstR = [R(), R()]
                    xt = [sb("xtA%d" % i, [128, KC, TT], F32, ph) for i in range(2)]
                    xtR = [R(), R()]
                    sqb = sb("sqbA", [128, KC, TT], BF16, ph)
                    sqR = R()
                    std = sb("stdA", [128, TT], F32, ph)
                    stdR = R()
                    xn = [sb("xnA%d" % i, [128, TT], F32, ph) for i in range(2)]
                    xnR = [R(), R()]
                    hb = [sb("hbA%d" % i, [128, KC, TT], BF16, ph) for i in range(2)]
                    hR = [R(), R()]
                    stage = [sb("stgA%d" % i, [128, 16, TT], BF16, ph) for i in range(2)]
                    stgR = [R(), R()]
                    sq2 = [sb("sq2A%d" % i, [128, TT], BF16, ph) for i in range(3)]
                    sq2R = [R(), R(), R()]
                    std2 = [sb("std2A%d" % i, [128, TT], F32, ph) for i in range(3)]
                    std2R = [R(), R(), R()]
                    mcnt = [0]

                    def a_load(tt):
                        b = tt % 2
                        ts = slice(tt * TT, (tt + 1) * TT)
                        S.dma("sp", xt[b][:], x_src[:, ts].rearrange("(kc p) t -> p kc t", p=128), Rd=([xr[tt]] if l > 0 else []), Wr=[xtR[b]])

                    def a_pro(tt):
                        b = tt % 2
                        S.op("act", lambda e: e.activation(out=sqb[:], in_=xt[b][:], func=AF.Square), Rd=[xtR[b]], Wr=[sqR])
                        for kc in range(KC):
                            S.op("pe", lambda e, kc=kc: e.matmul(banks[0][:], ones_bf[:], sqb[:, kc, :], start=(kc == 0), stop=(kc == KC - 1)), Rd=[sqR, cR], Wr=[bR[0]])
                        S.op("act", lambda e: e.activation(out=std[:], in_=banks[0][:], func=AF.Ln, bias=EPS, scale=1.0 / D), Rd=[bR[0]], Wr=[stdR])
                        S.op("act", lambda e: e.activation(out=std[:], in_=std[:], func=AF.Exp, scale=-0.5), Rd=[stdR], Wr=[stdR])
                        for kc in range(KC):
                            i2 = kc % 2
                            S.op("dve", lambda e, kc=kc: e.tensor_tensor(xn[i2][:], xt[b][:, kc, :], std[:], ALU.mult), Rd=[xtR[b], stdR], Wr=[xnR[i2]])
                            S.op("act", lambda e, kc=kc: e.activation(out=hb[b][:, kc, :], in_=xn[i2][:], func=AF.Identity, bias=sh1(kc), scale=gs1(kc)), Rd=[xnR[i2], derR, modsR], Wr=[hR[b]])

                    a_load(0)
                    for pi in range(8):
                        b = pi % 2
                        S.dma("sp", wst[b][:], win_in[l, :, pi * 256:(pi + 1) * 256].rearrange("(kc p) n -> p kc n", p=128), Wr=[wstR[b]])
                        S.op("act", lambda e, b=b, pi=pi: e.copy(win[:, :, pi * 256:(pi + 1) * 256], wst[b][:]), Rd=[wstR[b]], Wr=[winR[pi]])
                        if pi == 1:
                            a_load(1)
                    a_pro(0)
                    for tt in range(NT):
                        b = tt % 2
                        ts = slice(tt * TT, (tt + 1) * TT)
                        hbt = hb[b]
                        for c in range(8):
                            pb = 1 + (mcnt[0] % 3)
                            mcnt[0] += 1
                            hbk = 4 + (c % 2)
                            i2 = c % 3
                            for kc in range(KC):
                                S.op("pe", lambda e, kc=kc, c=c, pb=pb: e.matmul(banks[pb][:], win[:, kc, c * 128:(c + 1) * 128], hbt[:, kc, :], start=(kc == 0), stop=(kc == KC - 1)), Rd=[winR[c // 2], hR[b]], Wr=[bR[pb]])
                            S.op("act", lambda e, pb=pb, i2=i2: e.activation(out=sq2[i2][:], in_=banks[pb][:], func=AF.Square), Rd=[bR[pb]], Wr=[sq2R[i2]])
                            S.op("pe", lambda e, hbk=hbk, i2=i2: e.matmul(banks[hbk][:], blk_bf[:], sq2[i2][:], start=True, stop=True), Rd=[sq2R[i2], cR], Wr=[bR[hbk]])
                            S.op("act", lambda e, hbk=hbk, i2=i2: e.activation(out=std2[i2][:], in_=banks[hbk][:], func=AF.Ln, bias=EPS, scale=1.0 / 64), Rd=[bR[hbk]], Wr=[std2R[i2]])
                            S.op("act", lambda e, i2=i2: e.activation(out=std2[i2][:], in_=std2[i2][:], func=AF.Exp, scale=-0.5), Rd=[std2R[i2]], Wr=[std2R[i2]])
                            gcol = gq if c < 4 else pc("kg")
                            S.op("dve", lambda e, pb=pb, i2=i2, c=c, b=b, gcol=gcol: e.scalar_tensor_tensor(stage[b][:, c, :], banks[pb][:], gcol, std2[i2][:], ALU.mult, ALU.mult),
                                 Rd=[bR[pb], std2R[i2], derR, parR], Wr=[stgR[b]])
                        if tt + 2 < NT:
                            a_load(tt + 2) if False else None
                        for s4 in range(4):
                            pb = 6 + (s4 % 2)
                            for kc in range(KC):
                                S.op("pe", lambda e, kc=kc, s4=s4, pb=pb: e.matmul(banks[pb][:], hbt[:, kc, s4 * 128:(s4 + 1) * 128], win[:, kc, 1024:1536], start=(kc == 0), stop=(kc == KC - 1)), Rd=[winR[4], winR[5], hR[b]], Wr=[bR[pb]])
                            S.op("act", lambda e, pb=pb, s4=s4, b=b: e.copy(stage[b][:, 8 + s4, :], banks[pb][:]), Rd=[bR[pb]], Wr=[stgR[b]])
                        if tt + 1 < NT:
                            a_pro(tt + 1)
                        for c in range(4):
                            pb = 1 + (mcnt[0] % 3)
                            mcnt[0] += 1
                            for kc in range(KC):
                                S.op("pe", lambda e, kc=kc, c=c, pb=pb: e.matmul(banks[pb][:], win[:, kc, 1536 + c * 128:1536 + (c + 1) * 128], hbt[:, kc, :], start=(kc == 0), stop=(kc == KC - 1)), Rd=[winR[6 + c // 2], hR[b]], Wr=[bR[pb]])
                            S.op("dve", lambda e, pb=pb, c=c, b=b: e.tensor_copy(stage[b][:, 12 + c, :], banks[pb][:]), Rd=[bR[pb]], Wr=[stgR[b]])
                        if tt + 2 < NT:
                            a_load(tt + 2)
                        S.dma("pool", qk_scr[:, ts].rearrange("(c p) t -> p c t", p=128), stage[b][:, 0:8, :], Rd=[stgR[b]], Wr=qkr)
                        S.dma("pool", v_scr[tt * TT:(tt + 1) * TT, :].rearrange("(s p) f -> p s f", p=128), stage[b][:, 8:12, :], Rd=[stgR[b]], Wr=[vr])
                        S.dma("pool", u_scr[:, ts].rearrange("(c p) t -> p c t", p=128), stage[b][:, 12:16, :], Rd=[stgR[b]], Wr=ur)
                    S.barrier()

                with ExitStack() as ph:
                    qT = [sb("qTB%d" % i, [128, L], BF16, ph) for i in range(2)]
                    kT = [sb("kTB%d" % i, [128, L], BF16, ph) for i in range(2)]
                    vt = [sb("vtB%d" % i, [128, 32, 128], BF16, ph) for i in range(2)]
                    ldR = [R(), R()]
                    ost = [sb("ostB%d" % i, [128, L], F32, ph) for i in range(2)]
                    ostR = [R(), R()]
                    NB3 = 3
                    eb = [sb("ebB%d" % i, [128, TT], F32, ph) for i in range(2)]
                    lb = [sb("lbB%d" % i, [128, TT], BF16, ph) for i in range(NB3)]
                    wb = [sb("wbB%d" % i, [128, TT], BF16, ph) for i in range(NB3)]
                    Ls = [sb("LsB%d" % i, [128, TT], BF16, ph) for i in range(2)]
                    eR = [R(), R()]; lR = [R() for _ in range(NB3)]; wR = [R() for _ in range(NB3)]; LsR = [R(), R()]
                    for i3 in range(NB3):
                        S.op("pool", lambda e, i3=i3: e.memset(lb[i3][:], 0.0), Wr=[lR[i3]])
                        S.op("pool", lambda e, i3=i3: e.memset(wb[i3][:], 0.0), Wr=[wR[i3]])
                    steps = []
                    nqt = nq_tiles if "B" in phases else 0
                    sweep = 0
                    for hp in range(4):
                        for hh in range(2):
                            for qi in range(nqt):
                                nk = 4 * (qi + 1)
                                lcur = 0
                                for idx, kt in enumerate(range(nk - 1, -1, -1)):
                                    st = dict(hp=hp, b=hp % 2, po=64 * hh, qi=qi, kt=kt, idx=idx, nk=nk, ob=4 + (sweep % 2),
                                              diag=(kt >= 4 * qi), o_=kt - 4 * qi, first_of_pair=(hh == 0 and qi == 0 and idx == 0),
                                              last_of_pair=(hh == 1 and qi == nqt - 1 and idx == nk - 1))
                                    st["ls_read"] = lcur if idx > 0 else None
                                    if idx < nk - 1:
                                        st["ls_write"] = 1 - lcur
                                        lcur = 1 - lcur
                                    else:
                                        st["ls_write"] = None
                                    steps.append(st)
                                sweep += 1
                    NS = len(steps)

                    def loads(hp):
                        b = hp % 2
                        S.dma("sp", qT[b][:], qk_scr[hp * 128:(hp + 1) * 128, :], Rd=[qkr[hp]], Wr=[ldR[b]])
                        S.dma("sp", kT[b][:], qk_scr[512 + hp * 128:512 + (hp + 1) * 128, :], Rd=[qkr[4 + hp]], Wr=[ldR[b]])
                        S.dma("sp", vt[b][:], v_scr[:, hp * 128:(hp + 1) * 128].rearrange("(kt p) f -> p kt f", p=128), Rd=[vr], Wr=[ldR[b]])

                    def stA(i):
                        st = steps[i]; b = st["b"]; po = st["po"]; a = i % 4
                        if st["first_of_pair"]:
                            if st["hp"] == 0:
                                loads(0)
                            if st["hp"] + 1 < 4:
                                loads(st["hp"] + 1)
                        ks = slice(st["kt"] * 128, (st["kt"] + 1) * 128); qs = slice(st["qi"] * TT, (st["qi"] + 1) * TT)
                        S.op("pe", lambda e: e.matmul(banks[a][:], kT[b][po:po + 64, ks], qT[b][po:po + 64, qs], start=True, stop=False), Rd=[ldR[b]], Wr=[bR[a]])

                    def stL(i):
                        st = steps[i]; a = i % 4; e2 = i % 2; l3 = i % NB3
                        c0 = 128 * st["o_"] if st["diag"] else 0
                        S.op("act", lambda e: e.activation(out=eb[e2][:, c0:], in_=banks[a][:, c0:], func=AF.Exp), Rd=[bR[a]], Wr=[eR[e2]])
                        S.op("act", lambda e: e.activation(out=lb[l3][:, c0:], in_=eb[e2][:, c0:], func=AF.Ln, bias=1.0), Rd=[eR[e2]], Wr=[lR[l3]])
                        if st["diag"]:
                            S.op("dve", lambda e: e.tensor_tensor(lb[l3][:], lb[l3][:], masks[:, st["o_"], :], ALU.mult), Rd=[lR[l3], cR], Wr=[lR[l3]])

                    def stB(i):
                        st = steps[i]; a = i % 4; l3 = i % NB3
                        lr_, lw_ = st["ls_read"], st["ls_write"]
                        S.op("pe", lambda e: e.matmul(banks[a][:], negU[:], lb[l3][:], start=False, stop=(lr_ is None)), Rd=[lR[l3], cR], Wr=[bR[a]])
                        if lr_ is not None:
                            S.op("pe", lambda e: e.matmul(banks[a][:], negOnes[:], Ls[lr_][:], start=False, stop=True), Rd=[LsR[lr_], cR], Wr=[bR[a]])
                        if lw_ is not None:
                            if lr_ is None:
                                S.op("pool", lambda e: e.tensor_copy(Ls[lw_][:], lb[l3][:]), Rd=[lR[l3]], Wr=[LsR[lw_]])
                            else:
                                S.op("pool", lambda e: e.tensor_tensor(Ls[lw_][:], Ls[lr_][:], lb[l3][:], ALU.add), Rd=[lR[l3], LsR[lr_]], Wr=[LsR[lw_]])
                        c0 = 128 * st["o_"] if st["diag"] else 0
                        S.op("act", lambda e: e.activation(out=wb[l3][:, c0:], in_=banks[a][:, c0:], func=AF.Exp), Rd=[bR[a]], Wr=[wR[l3]])
                        if st["diag"]:
                            S.op("dve", lambda e: e.tensor_tensor(wb[l3][:], wb[l3][:], masks[:, st["o_"], :], ALU.mult), Rd=[wR[l3], cR], Wr=[wR[l3]])

                    def stO(i):
                        st = steps[i]; b = st["b"]; po = st["po"]; l3 = i % NB3; ob = st["ob"]; kt = st["kt"]
                        S.op("pe", lambda e: e.matmul(banks[ob][po:po + 64, :], vt[b][:, kt, po:po + 64], wb[l3][:], start=(st["idx"] == 0), stop=(st["idx"] == st["nk"] - 1)),
                             Rd=[wR[l3], ldR[b]], Wr=[bR[ob]])
                        if st["idx"] == st["nk"] - 1:
                            qs = slice(st["qi"] * TT, (st["qi"] + 1) * TT)
                            S.op("dve", lambda e: e.tensor_copy(ost[b][po:po + 64, qs], banks[ob][po:po + 64, :]), Rd=[bR[ob]], Wr=[ostR[b]])
                        if st["last_of_pair"]:
                            hp = st["hp"]
                            S.dma("pool", o_scr[hp * 128:(hp + 1) * 128, :], ost[b][:], Rd=[ostR[b]], Wr=[ors[hp]])

                    for slot in range(-2, NS + 1):
                        if 0 <= slot + 2 < NS:
                            stA(slot + 2)
                        if 0 <= slot + 1 < NS:
                            stL(slot + 1)
                        if 0 <= slot < NS:
                            stB(slot)
                        if 0 <= slot - 1 < NS:
                            stO(slot - 1)
                    S.barrier()

                with ExitStack() as ph:
                    def ftile(name, n=256):
                        return sb(name, [128, n], F32, ph)
                    zR = R()
                    dtB = ftile("dtB", 4)
                    lre = ftile("lre"); lim = ftile("lim"); mag = ftile("mag"); cosl = ftile("cosl"); sinl = ftile("sinl")
                    ta = ftile("ta"); tb = ftile("tb"); tcc = ftile("tcc"); td = ftile("td")
                    tiq = sb("tiq", [128, 520], I32, ph)
                    tq1 = ftile("tq1", 520); tq2 = ftile("tq2", 520); tq3 = ftile("tq3", 520)
                    INV2PI = 1.0 / (2 * PI)

                    def frac_turns(eng, t, n):
                        S.op(eng, lambda e: e.tensor_copy(tiq[:, 0:n], t), Rd=[zR], Wr=[zR])
                        S.op(eng, lambda e: e.tensor_copy(tq1[:, 0:n], tiq[:, 0:n]), Rd=[zR], Wr=[zR])
                        S.op(eng, lambda e: e.tensor_tensor(t, t, tq1[:, 0:n], ALU.subtract), Rd=[zR], Wr=[zR])
                        S.op(eng, lambda e: e.tensor_scalar(tq1[:, 0:n], t, 0.5, None, ALU.is_gt), Rd=[zR], Wr=[zR])
                        S.op(eng, lambda e: e.tensor_tensor(t, t, tq1[:, 0:n], ALU.subtract), Rd=[zR], Wr=[zR])
                        S.op(eng, lambda e: e.tensor_scalar(tq1[:, 0:n], t, -0.5, None, ALU.is_lt), Rd=[zR], Wr=[zR])
                        S.op(eng, lambda e: e.tensor_tensor(t, t, tq1[:, 0:n], ALU.add), Rd=[zR], Wr=[zR])

                    def sincos_turns(eng, sin_out, cos_out, t, n):
                        S.op("act", lambda e: e.activation(out=sin_out, in_=t, func=AF.Sin, scale=2 * PI), Rd=[zR], Wr=[zR])
                        S.op(eng, lambda e: e.tensor_scalar(tq2[:, 0:n], t, 0.25, None, ALU.add), Rd=[zR], Wr=[zR])
                        S.op(eng, lambda e: e.tensor_scalar(tq1[:, 0:n], tq2[:, 0:n], 0.5, None, ALU.is_gt), Rd=[zR], Wr=[zR])
                        S.op(eng, lambda e: e.tensor_tensor(tq2[:, 0:n], tq2[:, 0:n], tq1[:, 0:n], ALU.subtract), Rd=[zR], Wr=[zR])
                        S.op("act", lambda e: e.activation(out=cos_out, in_=tq2[:, 0:n], func=AF.Sin, scale=2 * PI), Rd=[zR], Wr=[zR])

                    S.op("act", lambda e: e.activation(out=dtB[:], in_=pc("ldtB"), func=AF.Exp), Rd=[parR], Wr=[zR])
                    for k in range(4):
                        ksl = slice(k * 64, (k + 1) * 64)
                        S.op("dve", lambda e, k=k, ksl=ksl: e.tensor_scalar(lre[:, ksl], pc("aBre")[:, ksl], dtB[:, k:k + 1], None, ALU.mult), Rd=[parR, zR], Wr=[zR])
                        S.op("dve", lambda e, k=k, ksl=ksl: e.tensor_scalar(lim[:, ksl], pc("aBim")[:, ksl], dtB[:, k:k + 1], INV2PI, ALU.mult, ALU.mult), Rd=[parR, zR], Wr=[zR])
                    S.op("act", lambda e: e.activation(out=mag[:], in_=lre[:], func=AF.Exp), Rd=[zR], Wr=[zR])
                    frac_turns("dve", lim[:], 256)
                    sincos_turns("dve", sinl[:], cosl[:], lim[:], 256)
                    S.op("dve", lambda e: e.tensor_tensor(ta[:], mag[:], cosl[:], ALU.mult), Rd=[zR], Wr=[zR])
                    S.op("dve", lambda e: e.tensor_scalar(ta[:], ta[:], -1.0, None, ALU.add), Rd=[zR], Wr=[zR])
                    S.op("dve", lambda e: e.tensor_tensor(tb[:], mag[:], sinl[:], ALU.mult), Rd=[zR], Wr=[zR])
                    S.op("dve", lambda e: e.tensor_tensor(mag[:], pc("aBre"), pc("aBre"), ALU.mult), Rd=[zR, parR], Wr=[zR])
                    S.op("dve", lambda e: e.tensor_tensor(cosl[:], pc("aBim"), pc("aBim"), ALU.mult), Rd=[zR, parR], Wr=[zR])
                    S.op("dve", lambda e: e.tensor_tensor(mag[:], mag[:], cosl[:], ALU.add), Rd=[zR], Wr=[zR])
                    S.op("dve", lambda e: e.reciprocal(mag[:], mag[:]), Rd=[zR], Wr=[zR])
                    S.op("dve", lambda e: e.tensor_tensor(tcc[:], ta[:], pc("aBre"), ALU.mult), Rd=[zR, parR], Wr=[zR])
                    S.op("dve", lambda e: e.tensor_tensor(cosl[:], tb[:], pc("aBim"), ALU.mult), Rd=[zR, parR], Wr=[zR])
                    S.op("dve", lambda e: e.tensor_tensor(tcc[:], tcc[:], cosl[:], ALU.add), Rd=[zR], Wr=[zR])
                    S.op("dve", lambda e: e.tensor_tensor(tcc[:], tcc[:], mag[:], ALU.mult), Rd=[zR], Wr=[zR])
                    S.op("dve", lambda e: e.tensor_tensor(td[:], tb[:], pc("aBre"), ALU.mult), Rd=[zR, parR], Wr=[zR])
                    S.op("dve", lambda e: e.tensor_tensor(cosl[:], ta[:], pc("aBim"), ALU.mult), Rd=[zR, parR], Wr=[zR])
                    S.op("dve", lambda e: e.tensor_tensor(td[:], td[:], cosl[:], ALU.subtract), Rd=[zR], Wr=[zR])
                    S.op("dve", lambda e: e.tensor_tensor(td[:], td[:], mag[:], ALU.mult), Rd=[zR], Wr=[zR])
                    S.op("dve", lambda e: e.tensor_tensor(ta[:], tcc[:], pc("BreB"), ALU.mult), Rd=[zR, parR], Wr=[zR])
                    S.op("dve", lambda e: e.tensor_tensor(cosl[:], td[:], pc("BimB"), ALU.mult), Rd=[zR, parR], Wr=[zR])
                    S.op("dve", lambda e: e.tensor_tensor(ta[:], ta[:], cosl[:], ALU.subtract), Rd=[zR], Wr=[zR])
                    S.op("dve", lambda e: e.tensor_tensor(tb[:], tcc[:], pc("BimB"), ALU.mult), Rd=[zR, parR], Wr=[zR])
                    S.op("dve", lambda e: e.tensor_tensor(cosl[:], td[:], pc("BreB"), ALU.mult), Rd=[zR, parR], Wr=[zR])
                    S.op("dve", lambda e: e.tensor_tensor(tb[:], tb[:], cosl[:], ALU.add), Rd=[zR], Wr=[zR])
                    BpR = sb("BpR", [128, 4, 4, 128], BF16, ph)
                    BpI = sb("BpI", [128, 4, 4, 128], BF16, ph)
                    CpR = sb("CpR", [128, 4, 4, 128], BF16, ph)
                    CpI = sb("CpI", [128, 4, 4, 128], BF16, ph)
                    Gbd = sb("Gbd", [128, 4, 128], BF16, ph)
                    wtR = R()
                    S.op("pool", lambda e: e.memset(CpR[:], 0.0), Wr=[wtR])
                    S.op("pool", lambda e: e.memset(CpI[:], 0.0), Wr=[wtR])
                    for k in range(4):
                        ksl = slice(k * 64, (k + 1) * 64)
                        for j2 in range(4):
                            for g2_ in range(2):
                                col = 2 * j2 + g2_
                                S.op("dve", lambda e, k=k, j2=j2, g2_=g2_, col=col, ksl=ksl: e.tensor_scalar(BpR[:, k, j2, g2_ * 64:(g2_ + 1) * 64], ta[:, ksl], M8[:, col:col + 1], None, ALU.mult), Rd=[zR, cR], Wr=[wtR])
                                S.op("dve", lambda e, k=k, j2=j2, g2_=g2_, col=col, ksl=ksl: e.tensor_scalar(BpI[:, k, j2, g2_ * 64:(g2_ + 1) * 64], tb[:, ksl], M8[:, col:col + 1], None, ALU.mult), Rd=[zR, cR], Wr=[wtR])
                                cs = slice((k * 4 + j2) * 16, (k * 4 + j2) * 16 + 16)
                                S.op("dve", lambda e, k=k, j2=j2, col=col, cs=cs: e.tensor_scalar(CpR[:, k, j2, col * 16:(col + 1) * 16], pc("CreS")[:, cs], M2[:, j2, col:col + 1], None, ALU.mult), Rd=[parR, cR], Wr=[wtR])
                                S.op("dve", lambda e, k=k, j2=j2, col=col, cs=cs: e.tensor_scalar(CpI[:, k, j2, col * 16:(col + 1) * 16], pc("CimS")[:, cs], M2[:, j2, col:col + 1], -1.0, ALU.mult, ALU.mult), Rd=[parR, cR], Wr=[wtR])
                        for g8 in range(8):
                            S.op("dve", lambda e, k=k, g8=g8: e.tensor_scalar(Gbd[:, k, g8 * 16:(g8 + 1) * 16], pc("GS")[:, k * 16:(k + 1) * 16], M8[:, g8:g8 + 1], None, ALU.mult), Rd=[parR, cR], Wr=[wtR])
                    dump("Bbre", ta[:], [zR]); dump("Bbim", tb[:], [zR]); dump("fre", tcc[:], [zR]); dump("fim", td[:], [zR])
                    dump("BpR", BpR[:].rearrange("p a b c -> p (a b c)"), [wtR]); dump("CpR", CpR[:].rearrange("p a b c -> p (a b c)"), [wtR])
                    dump("CpI", CpI[:].rearrange("p a b c -> p (a b c)"), [wtR]); dump("Gbd", Gbd[:].rearrange("p a b -> p (a b)"), [wtR])
                    dtS = ftile("dtS", 16); rS = ftile("rS", 16); thS = ftile("thS", 16)
                    S.op("act", lambda e: e.activation(out=dtS[:], in_=pc("ldtS"), func=AF.Exp), Rd=[parR], Wr=[zR])
                    S.op("dve", lambda e: e.tensor_tensor(rS[:], dtS[:], pc("aSre"), ALU.mult), Rd=[zR, parR], Wr=[zR])
                    S.op("act", lambda e: e.activation(out=rS[:], in_=rS[:], func=AF.Exp), Rd=[zR], Wr=[zR])
                    S.op("dve", lambda e: e.tensor_tensor(thS[:], dtS[:], pc("aSim"), ALU.mult), Rd=[zR, parR], Wr=[zR])
                    S.op("dve", lambda e: e.tensor_scalar(thS[:], thS[:], INV2PI, None, ALU.mult), Rd=[zR], Wr=[zR])
                    frac_turns("dve", thS[:], 16)

                    dump("rS", rS[:], [zR]); dump("thS", thS[:], [zR])
                    CpRn = sb("CpRn", [128, 4, 4, 128], BF16, ph)
                    S.op("pool", lambda e: e.tensor_scalar(CpRn[:].rearrange("p a b c -> p (a b c)"), CpR[:].rearrange("p a b c -> p (a b c)"), -1.0, None, ALU.mult), Rd=[wtR], Wr=[wtR])
                    uT = [sb("uTC%d" % i, [128, L], BF16, ph) for i in range(2)]
                    uR = [R(), R()]
                    cosT = [[sb("cosT%d_%d" % (a, i), [128, 520], BF16, ph) for i in range(4)] for a in range(2)]
                    sinT = [[sb("sinT%d_%d" % (a, i), [128, 520], BF16, ph) for i in range(4)] for a in range(2)]
                    c5s5 = sb("c5s5", [128, 2, 4, 2], F32, ph)
                    rT = [[sb("rT%d_%d" % (a, i), [128, TT], F32, ph) for i in range(4)] for a in range(2)]
                    tabR = [[R() for _ in range(4)] for a in range(2)]
                    ini2 = [sb("ini%d" % a, [128, 4, 2], F32, ph) for a in range(2)]
                    iniR2 = [[R() for _ in range(4)] for a in range(2)]
                    tinyt = sb("tinyt", [128, 8], F32, ph)
                    tinyR = R()
                    NR3 = 3
                    bsr = [sb("bsrC%d" % i, [128, TT], BF16, ph) for i in range(NR3)]
                    bsi = [sb("bsiC%d" % i, [128, TT], BF16, ph) for i in range(NR3)]
                    bsR = [R() for _ in range(NR3)]
                    mm_ = [[sb("mC%d_%d" % (i, q), [128, TT], BF16, ph) for q in range(4)] for i in range(2)]
                    mR = [R(), R()]
                    bpr = [sb("bprC%d" % i, [128, TT], BF16, ph) for i in range(NR3)]
                    bpi = [sb("bpiC%d" % i, [128, TT], BF16, ph) for i in range(NR3)]
                    bpR = [R() for _ in range(NR3)]
                    wre = [sb("wreC%d" % i, [128, TT], F32, ph) for i in range(NR3)]
                    wim = [sb("wimC%d" % i, [128, TT], F32, ph) for i in range(NR3)]
                    wwR = [R() for _ in range(NR3)]
                    wrb = [sb("wrbC%d" % i, [128, TT], BF16, ph) for i in range(NR3)]
                    wib = [sb("wibC%d" % i, [128, TT], BF16, ph) for i in range(NR3)]
                    wbR = [R() for _ in range(NR3)]
                    pp = [[sb("pC%d_%d" % (i, q), [128, TT], BF16, ph) for q in range(4)] for i in range(NR3)]
                    ppR = [R() for _ in range(NR3)]
                    yb = [sb("ybC%d" % i, [128, TT], F32, ph) for i in range(2)]
                    ygb = [sb("ygbC%d" % i, [128, TT], BF16, ph) for i in range(2)]
                    gt = [sb("gtC%d" % i, [128, TT], F32, ph) for i in range(2)]
                    osg = [sb("osC%d" % i, [128, TT], F32, ph) for i in range(2)]
                    ybR = [R(), R()]; ygR = [R(), R()]; gtR = [R(), R()]; osR = [R(), R()]

                    def gen_tables(k, j2):
                        a = k % 2
                        col = k * 4 + j2
                        S.op("dve", lambda e: e.tensor_scalar(tq3[:, 0:520], iot[:, 0:520], thS[:, col:col + 1], None, ALU.mult), Rd=[zR, cR], Wr=[zR])
                        frac_turns("dve", tq3[:, 0:520], 520)
                        S.op("act", lambda e: e.activation(out=sinT[a][j2][:], in_=tq3[:, 0:520], func=AF.Sin, scale=2 * PI), Rd=[zR], Wr=[tabR[a][j2]])
                        S.op("act", lambda e: e.activation(out=c5s5[:, a, j2, 1:2], in_=tq3[:, 512:513], func=AF.Sin, scale=2 * PI), Rd=[zR], Wr=[tabR[a][j2]])
                        S.op("dve", lambda e: e.tensor_scalar(tq2[:, 0:520], tq3[:, 0:520], 0.25, None, ALU.add), Rd=[zR], Wr=[zR])
                        S.op("dve", lambda e: e.tensor_scalar(tq1[:, 0:520], tq2[:, 0:520], 0.5, None, ALU.is_gt), Rd=[zR], Wr=[zR])
                        S.op("dve", lambda e: e.tensor_tensor(tq2[:, 0:520], tq2[:, 0:520], tq1[:, 0:520], ALU.subtract), Rd=[zR], Wr=[zR])
                        S.op("act", lambda e: e.activation(out=cosT[a][j2][:], in_=tq2[:, 0:520], func=AF.Sin, scale=2 * PI), Rd=[zR], Wr=[tabR[a][j2]])
                        S.op("act", lambda e: e.activation(out=c5s5[:, a, j2, 0:1], in_=tq2[:, 512:513], func=AF.Sin, scale=2 * PI), Rd=[zR], Wr=[tabR[a][j2]])
                        S.op("dve", lambda e: e.tensor_scalar(rT[a][j2][:], iot[:, 0:TT], 0.0, rS[:, col:col + 1], ALU.mult, ALU.add), Rd=[zR, cR], Wr=[tabR[a][j2]])

                    ssteps = []
                    for k in range(4):
                        for tt in range(NT if "C" in phases else 0):
                            for j2 in range(4):
                                ssteps.append(dict(k=k, tt=tt, j2=j2, n=len(ssteps)))
                    NSS = len(ssteps)

                    def c_bu(i):
                        st = ssteps[i]; k, tt, j2 = st["k"], st["tt"], st["j2"]; ub = k % 2; s = i % 2
                        ts = slice(tt * TT, (tt + 1) * TT)
                        if tt == 0 and j2 == 0:
                            S.dma("sp", uT[ub][:], u_scr[k * 128:(k + 1) * 128, :], Rd=[ur[k]], Wr=[uR[ub]])
                            if k == 0:
                                for q in range(4):
                                    gen_tables(0, q)
                            for q in range(4):
                                S.op("pool", lambda e, q=q: e.memset(ini2[k % 2][:, q, :], 0.0), Wr=[iniR2[k % 2][q]])
                        if k + 1 < 4 and j2 == 0 and tt in (1, 2, 3, 4):
                            gen_tables(k + 1, tt - 1)
                        S.op("pe", lambda e: e.matmul(banks[s][:], BpR[:, k, j2, :], uT[ub][:, ts], start=True, stop=True), Rd=[wtR, uR[ub]], Wr=[bR[s]])
                        S.op("pe", lambda e: e.matmul(banks[2 + s][:], BpI[:, k, j2, :], uT[ub][:, ts], start=True, stop=True), Rd=[wtR, uR[ub]], Wr=[bR[2 + s]])
                        r3 = i % NR3
                        S.op("act", lambda e: e.copy(bsr[r3][:], banks[s][:]), Rd=[bR[s]], Wr=[bsR[r3]])
                        S.op("act", lambda e: e.copy(bsi[r3][:], banks[2 + s][:]), Rd=[bR[2 + s]], Wr=[bsR[r3]])

                    def c_derot(i):
                        st = ssteps[i]; k, j2 = st["k"], st["j2"]; a = k % 2; r3 = i % NR3; m2 = i % 2
                        cs_, sn_ = cosT[a][j2][:, 0:TT], sinT[a][j2][:, 0:TT]
                        tr = tabR[a][j2]
                        S.op("dve", lambda e: e.tensor_tensor(mm_[m2][0][:], bsr[r3][:], cs_, ALU.mult), Rd=[bsR[r3], tr], Wr=[mR[m2]])
                        S.op("dve", lambda e: e.tensor_tensor(mm_[m2][1][:], bsi[r3][:], sn_, ALU.mult), Rd=[bsR[r3], tr], Wr=[mR[m2]])
                        S.op("dve", lambda e: e.tensor_tensor(mm_[m2][2][:], bsi[r3][:], cs_, ALU.mult), Rd=[bsR[r3], tr], Wr=[mR[m2]])
                        S.op("dve", lambda e: e.tensor_tensor(mm_[m2][3][:], bsr[r3][:], sn_, ALU.mult), Rd=[bsR[r3], tr], Wr=[mR[m2]])

                    def c_derot2(i):
                        r3 = i % NR3; m2 = i % 2
                        S.op("dve", lambda e: e.tensor_tensor(bpr[r3][:], mm_[m2][0][:], mm_[m2][1][:], ALU.add), Rd=[mR[m2]], Wr=[bpR[r3]])
                        S.op("dve", lambda e: e.tensor_tensor(bpi[r3][:], mm_[m2][2][:], mm_[m2][3][:], ALU.subtract), Rd=[mR[m2]], Wr=[bpR[r3]])

                    def c_scan(i):
                        st = ssteps[i]; k, j2 = st["k"], st["j2"]; a = k % 2; r3 = i % NR3
                        tr = tabR[a][j2]
                        ini = ini2[a]; iniR = iniR2[a]
                        S.op("dve", lambda e: e.tensor_tensor_scan(wre[r3][:], rT[a][j2][:], bpr[r3][:], ini[:, j2, 0:1], ALU.mult, ALU.add), Rd=[bpR[r3], tr, iniR[j2]], Wr=[wwR[r3]])
                        S.op("dve", lambda e: e.tensor_tensor_scan(wim[r3][:], rT[a][j2][:], bpi[r3][:], ini[:, j2, 1:2], ALU.mult, ALU.add), Rd=[bpR[r3], tr, iniR[j2]], Wr=[wwR[r3]])
                        S.op("act", lambda e: e.copy(wrb[r3][:], wre[r3][:]), Rd=[wwR[r3]], Wr=[wbR[r3]])
                        S.op("act", lambda e: e.copy(wib[r3][:], wim[r3][:]), Rd=[wwR[r3]], Wr=[wbR[r3]])
                        if st["tt"] < NT - 1:
                            c5, s5 = c5s5[:, a, j2, 0:1], c5s5[:, a, j2, 1:2]
                            wr_e, wi_e = wre[r3][:, 511:512], wim[r3][:, 511:512]
                            S.op("pool", lambda e: e.tensor_tensor(tinyt[:, 0:1], wi_e, s5, ALU.mult), Rd=[wwR[r3], tr, tinyR], Wr=[tinyR])
                            S.op("pool", lambda e: e.tensor_tensor(tinyt[:, 1:2], wr_e, s5, ALU.mult), Rd=[wwR[r3], tr, tinyR], Wr=[tinyR])
                            S.op("pool", lambda e: e.tensor_tensor(tinyt[:, 2:3], wr_e, c5, ALU.mult), Rd=[wwR[r3], tr, tinyR], Wr=[tinyR])
                            S.op("pool", lambda e: e.tensor_tensor(tinyt[:, 3:4], wi_e, c5, ALU.mult), Rd=[wwR[r3], tr, tinyR], Wr=[tinyR])
                            S.op("pool", lambda e: e.tensor_tensor(ini[:, j2, 0:1], tinyt[:, 2:3], tinyt[:, 0:1], ALU.subtract), Rd=[tinyR], Wr=[iniR[j2]])
                            S.op("pool", lambda e: e.tensor_tensor(ini[:, j2, 1:2], tinyt[:, 3:4], tinyt[:, 1:2], ALU.add), Rd=[tinyR], Wr=[iniR[j2]])

                    def c_rot(i):
                        st = ssteps[i]; k, j2 = st["k"], st["j2"]; a = k % 2; r3 = i % NR3
                        cs_, sn_ = cosT[a][j2][:, 0:TT], sinT[a][j2][:, 0:TT]
                        tr = tabR[a][j2]
                        S.op("dve", lambda e: e.tensor_tensor(pp[r3][0][:], wrb[r3][:], cs_, ALU.mult), Rd=[wbR[r3], tr], Wr=[ppR[r3]])
                        S.op("dve", lambda e: e.tensor_tensor(pp[r3][1][:], wib[r3][:], sn_, ALU.mult), Rd=[wbR[r3], tr], Wr=[ppR[r3]])
                        S.op("dve", lambda e: e.tensor_tensor(pp[r3][2][:], wrb[r3][:], sn_, ALU.mult), Rd=[wbR[r3], tr], Wr=[ppR[r3]])
                        S.op("dve", lambda e: e.tensor_tensor(pp[r3][3][:], wib[r3][:], cs_, ALU.mult), Rd=[wbR[r3], tr], Wr=[ppR[r3]])

                    def c_out(i):
                        st = ssteps[i]; k, tt, j2 = st["k"], st["tt"], st["j2"]; ub = k % 2; r3 = i % NR3
                        ts = slice(tt * TT, (tt + 1) * TT)
                        yk = 4 + (tt % 2); gk = 6 + (tt % 2); tb2 = tt % 2
                        for q, wt_ in enumerate((CpR, CpRn, CpI, CpI)):
                            S.op("pe", lambda e, q=q, wt_=wt_: e.matmul(banks[yk][:], wt_[:, k, j2, :], pp[r3][q][:], start=(j2 == 0 and q == 0), stop=(j2 == 3 and q == 3)), Rd=[wtR, ppR[r3]], Wr=[bR[yk]])
                        if j2 == 3:
                            S.op("dve", lambda e: e.scalar_tensor_tensor(yb[tb2][:], uT[ub][:, ts], pc("dsk", k), banks[yk][:], ALU.mult, ALU.add), Rd=[uR[ub], parR, bR[yk]], Wr=[ybR[tb2]])
                            S.op("act", lambda e: e.activation(out=yb[tb2][:], in_=yb[tb2][:], func=AF.Gelu_apprx_tanh), Rd=[ybR[tb2]], Wr=[ybR[tb2]])
                            S.op("act", lambda e: e.copy(ygb[tb2][:], yb[tb2][:]), Rd=[ybR[tb2]], Wr=[ygR[tb2]])
                            S.op("pe", lambda e: e.matmul(banks[gk][:], Gbd[:, k, :], ygb[tb2][:], start=True, stop=True), Rd=[wtR, ygR[tb2]], Wr=[bR[gk]])
                            S.op("act", lambda e: e.activation(out=gt[tb2][:], in_=banks[gk][:], func=AF.Sigmoid, bias=pc("glub", k)), Rd=[bR[gk], parR], Wr=[gtR[tb2]])
                            S.op("pool", lambda e: e.tensor_tensor(osg[tb2][:], yb[tb2][:], gt[tb2][:], ALU.mult), Rd=[ybR[tb2], gtR[tb2]], Wr=[osR[tb2]])
                            S.dma("sp", o_scr[512 + k * 128:512 + (k + 1) * 128, ts], osg[tb2][:], Rd=[osR[tb2]], Wr=[ors[4 + k]])

                    for slot in range(-1, NSS + 3):
                        if 0 <= slot + 1 < NSS:
                            c_bu(slot + 1)
                        if 0 <= slot < NSS:
                            c_derot(slot)
                        if 0 <= slot - 1 < NSS:
                            c_scan(slot - 1)
                        if 0 <= slot - 2 < NSS:
                            c_rot(slot - 2)
                        if 0 <= slot < NSS:
                            c_derot2(slot)
                        if 0 <= slot - 3 < NSS:
                            c_out(slot - 3)
                    S.barrier()

                with ExitStack() as ph:
                    wo = sb("woD", [128, KC, D], BF16, ph)
                    woR = [R() for _ in range(4)]
                    wst = [sb("wstD%d" % i, [128, KC, 256], F32, ph) for i in range(2)]
                    wstR = [R(), R()]
                    ot = [sb("otD%d" % i, [128, KC, TT], F32, ph) for i in range(2)]
                    otR = [R(), R()]
                    xt = [sb("xtD%d" % i, [128, KC, TT], F32, ph) for i in range(2)]
                    xtR = [R(), R()]
                    sqb = sb("sqbD", [128, KC, TT], BF16, ph)
                    sqR = R()
                    stda = sb("stdaD", [128, TT], F32, ph)
                    stds = sb("stdsD", [128, TT], F32, ph)
                    stdR = R()
                    on = [sb("onD%d" % i, [128, KC, TT], BF16, ph) for i in range(2)]
                    onR = [R(), R()]
                    NTD = NT if "D" in phases else 0

                    def d_load(tt):
                        b = tt % 2
                        ts = slice(tt * TT, (tt + 1) * TT)
                        S.dma("sp", ot[b][:], o_scr[:, ts].rearrange("(kc p) t -> p kc t", p=128), Rd=ors, Wr=[otR[b]])
                        S.dma("sp", xt[b][:], x_src[:, ts].rearrange("(kc p) t -> p kc t", p=128), Rd=([xr[tt]] if l > 0 else []), Wr=[xtR[b]])

                    def d_pro(tt):
                        b = tt % 2
                        S.op("act", lambda e: e.activation(out=sqb[:], in_=ot[b][:], func=AF.Square), Rd=[otR[b]], Wr=[sqR])
                        for c in range(8):
                            bk = 0 if c < 4 else 1
                            S.op("pe", lambda e, c=c, bk=bk: e.matmul(banks[bk][:], ones_bf[:], sqb[:, c, :], start=(c % 4 == 0), stop=(c % 4 == 3)), Rd=[sqR, cR], Wr=[bR[bk]])
                        S.op("act", lambda e: e.activation(out=stda[:], in_=banks[0][:], func=AF.Ln, bias=EPS, scale=1.0 / 512), Rd=[bR[0]], Wr=[stdR])
                        S.op("act", lambda e: e.activation(out=stds[:], in_=banks[1][:], func=AF.Ln, bias=EPS, scale=1.0 / 512), Rd=[bR[1]], Wr=[stdR])
                        S.op("act", lambda e: e.activation(out=stda[:], in_=stda[:], func=AF.Exp, scale=-0.5), Rd=[stdR], Wr=[stdR])
                        S.op("act", lambda e: e.activation(out=stds[:], in_=stds[:], func=AF.Exp, scale=-0.5), Rd=[stdR], Wr=[stdR])
                        for c in range(8):
                            gcol = pc("aog", c) if c < 4 else pc("sog", c - 4)
                            sd = stda if c < 4 else stds
                            S.op("dve", lambda e, c=c, gcol=gcol, sd=sd: e.scalar_tensor_tensor(on[b][:, c, :], ot[b][:, c, :], gcol, sd[:], ALU.mult, ALU.mult), Rd=[otR[b], stdR, parR], Wr=[onR[b]])

                    if NTD:
                        d_load(0)
                    for pi in range(4):
                        b = pi % 2
                        S.dma("sp", wst[b][:], wout_in[l, :, pi * 256:(pi + 1) * 256].rearrange("(kc p) n -> p kc n", p=128), Wr=[wstR[b]])
                        S.op("act", lambda e, b=b, pi=pi: e.copy(wo[:, :, pi * 256:(pi + 1) * 256], wst[b][:]), Rd=[wstR[b]], Wr=[woR[pi]])
                    if NTD:
                        d_load(1)
                        d_pro(0)
                    for tt in range(NTD):
                        b = tt % 2
                        ts = slice(tt * TT, (tt + 1) * TT)
                        for co in range(8):
                            bk = 2 + (co % 4)
                            for kc in range(KC):
                                S.op("pe", lambda e, kc=kc, co=co, bk=bk: e.matmul(banks[bk][:], wo[:, kc, co * 128:(co + 1) * 128], on[b][:, kc, :], start=(kc == 0), stop=(kc == KC - 1)), Rd=[woR[co // 2], onR[b]], Wr=[bR[bk]])
                            S.op("dve", lambda e, co=co, bk=bk, b=b: e.scalar_tensor_tensor(xt[b][:, co, :], banks[bk][:], g1(co), xt[b][:, co, :], ALU.mult, ALU.add), Rd=[bR[bk], modsR, xtR[b]], Wr=[xtR[b]])
                            if co == 3 and tt + 1 < NTD:
                                d_pro(tt + 1)
                        S.dma("pool", x_scr[:, ts].rearrange("(kc p) t -> p kc t", p=128), xt[b][:], Rd=[xtR[b]], Wr=[xr[tt]])
                        if tt + 2 < NTD:
                            d_load(tt + 2)
                    S.barrier()

                with ExitStack() as ph:
                    wup = sb("wupE", [128, KC, 2 * DFF], BF16, ph)
                    wupR = [R() for _ in range(22)]
                    wst = [sb("wstE%d" % i, [128, KC, 256], F32, ph) for i in range(2)]
                    wstR = [R(), R()]
                    xt = [sb("xtE%d" % i, [128, KC, TT], F32, ph) for i in range(2)]
                    xtR = [R(), R()]
                    sqb = sb("sqbE", [128, KC, TT], BF16, ph)
                    sqR = R()
                    std = sb("stdE", [128, TT], F32, ph)
                    stdR = R()
                    xn = [sb("xnE%d" % i, [128, TT], F32, ph) for i in range(2)]
                    xnR = [R(), R()]
                    hb = [sb("hbE%d" % i, [128, KC, TT], BF16, ph) for i in range(2)]
                    hR = [R(), R()]
                    hc = [sb("hcE%d" % i, [128, 2 * NFF, 2], F32, ph) for i in range(2)]
                    hcR = [[R() for _ in range(2 * NFF)] for _ in range(2)]
                    hcn = sb("hcnE", [128, 2 * NFF, 2], F32, ph)
                    hcn2 = sb("hcn2E", [128, 2 * NFF, 2], F32, ph)
                    hcnR = [R() for _ in range(2 * NFF)]
                    tv = [sb("tvE%d" % i, [128, TT], F32, ph) for i in range(3)]
                    tg = [sb("tgE%d" % i, [128, TT], F32, ph) for i in range(3)]
                    tvR = [R() for _ in range(3)]; tgR = [R() for _ in range(3)]
                    actb = [sb("actE%d" % i, [128, TT], BF16, ph) for i in range(4)]
                    actR = [R() for _ in range(4)]
                    S.op("pool", lambda e: e.memset(hc[0][:], 0.0), Wr=hcR[0])
                    S.op("pool", lambda e: e.memset(hcn2[:], 0.0), Wr=hcnR)
                    cwc = lambda i, cidx: pc("cw", i * 44 + cidx)
                    NTE = NT if "E" in phases else 0

                    def e_load(tt):
                        b = tt % 2
                        ts = slice(tt * TT, (tt + 1) * TT)
                        S.dma("sp", xt[b][:], x_scr[:, ts].rearrange("(kc p) t -> p kc t", p=128), Rd=[xr[tt]], Wr=[xtR[b]])

                    def e_pro(tt):
                        b = tt % 2
                        S.op("act", lambda e: e.activation(out=sqb[:], in_=xt[b][:], func=AF.Square), Rd=[xtR[b]], Wr=[sqR])
                        for kc in range(KC):
                            S.op("pe", lambda e, kc=kc: e.matmul(banks[0][:], ones_bf[:], sqb[:, kc, :], start=(kc == 0), stop=(kc == KC - 1)), Rd=[sqR, cR], Wr=[bR[0]])
                        S.op("act", lambda e: e.activation(out=std[:], in_=banks[0][:], func=AF.Ln, bias=EPS, scale=1.0 / D), Rd=[bR[0]], Wr=[stdR])
                        S.op("act", lambda e: e.activation(out=std[:], in_=std[:], func=AF.Exp, scale=-0.5), Rd=[stdR], Wr=[stdR])
                        for kc in range(KC):
                            i2 = kc % 2
                            S.op("dve", lambda e, kc=kc: e.tensor_tensor(xn[i2][:], xt[b][:, kc, :], std[:], ALU.mult), Rd=[xtR[b], stdR], Wr=[xnR[i2]])
                            S.op("act", lambda e, kc=kc: e.activation(out=hb[b][:, kc, :], in_=xn[i2][:], func=AF.Identity, bias=sh2(kc), scale=gs2(kc)), Rd=[xnR[i2], derR, modsR], Wr=[hR[b]])

                    if NTE:
                        e_load(0)
                    order = []
                    for q in range(11):
                        order += [q, 11 + q]
                    def e_wload(n_):
                        pi = order[n_]
                        b = n_ % 2
                        S.dma("sp", wst[b][:], wup_in[l, :, pi * 256:(pi + 1) * 256].rearrange("(kc p) n -> p kc n", p=128), Wr=[wstR[b]])
                        S.op("act", lambda e: e.copy(wup[:, :, pi * 256:(pi + 1) * 256], wst[b][:]), Rd=[wstR[b]], Wr=[wupR[pi]])

                    for n_ in range(4):
                        e_wload(n_)
                        if n_ == 1 and NTE:
                            e_load(1)
                            e_pro(0)
                    if not NTE:
                        for n_ in range(4, 22):
                            e_wload(n_)
                    pend = []

                    def e_fin(s, j, tt):
                        ts = slice(tt * TT, (tt + 1) * TT)
                        S.op("act", lambda e: e.activation(out=tg[s][:], in_=tg[s][:], func=AF.Gelu_apprx_tanh), Rd=[tgR[s]], Wr=[tgR[s]])
                        a4 = j % 4
                        S.op("pool", lambda e: e.tensor_tensor(actb[a4][:], tg[s][:], tv[s][:], ALU.mult), Rd=[tgR[s], tvR[s]], Wr=[actR[a4]])
                        S.dma("sp", act_scr[j * 128:(j + 1) * 128, ts], actb[a4][:], Rd=[actR[a4]], Wr=[acr[tt][j]])

                    jstep = 0
                    for tt in range(NTE):
                        b = tt % 2
                        ts = slice(tt * TT, (tt + 1) * TT)
                        for j in range(NFF):
                            s = jstep % 3
                            jstep += 1
                            for which, (cidx, tdst, tdR) in enumerate(((j, tv[s], tvR[s]), (NFF + j, tg[s], tgR[s]))):
                                bk = 1 + 2 * s + which
                                h6 = 2 * s + which
                                for kc in range(KC):
                                    S.op("pe", lambda e, kc=kc, cidx=cidx, bk=bk: e.matmul(banks[bk][:], wup[:, kc, cidx * 128:(cidx + 1) * 128], hb[b][:, kc, :], start=(kc == 0), stop=(kc == KC - 1)), Rd=[wupR[cidx // 2], hR[b]], Wr=[bR[bk]])
                                hp_, hn_ = tt % 2, (tt + 1) % 2
                                S.op("act", lambda e, cidx=cidx, tdst=tdst, bk=bk: e.activation(out=tdst[:], in_=banks[bk][:], func=AF.Identity, bias=pc("cb", cidx), scale=cwc(2, cidx)), Rd=[bR[bk], parR], Wr=[tdR])
                                if tt + 1 < NTE:
                                    S.op("act", lambda e, cidx=cidx, bk=bk: e.activation(out=hcn[:, cidx, 0:2], in_=banks[bk][:, TT - 2:TT], func=AF.Identity, scale=cwc(0, cidx)), Rd=[parR], Wr=[hcnR[cidx], bR[bk]])
                                    S.op("act", lambda e, cidx=cidx, bk=bk: e.activation(out=hcn2[:, cidx, 0:1], in_=banks[bk][:, TT - 1:TT], func=AF.Identity, scale=cwc(1, cidx)), Rd=[parR], Wr=[hcnR[cidx], bR[bk]])
                                S.op("dve", lambda e, cidx=cidx, tdst=tdst, bk=bk: e.scalar_tensor_tensor(tdst[:, 1:TT], banks[bk][:, 0:TT - 1], cwc(1, cidx), tdst[:, 1:TT], ALU.mult, ALU.add), Rd=[parR, tdR], Wr=[tdR, bR[bk]])
                                S.op("dve", lambda e, cidx=cidx, tdst=tdst, bk=bk: e.scalar_tensor_tensor(tdst[:, 2:TT], banks[bk][:, 0:TT - 2], cwc(0, cidx), tdst[:, 2:TT], ALU.mult, ALU.add), Rd=[parR, tdR], Wr=[tdR, bR[bk]])
                                S.op("pool", lambda e, cidx=cidx, tdst=tdst, hp_=hp_: e.tensor_tensor(tdst[:, 0:2], tdst[:, 0:2], hc[hp_][:, cidx, :], ALU.add), Rd=[hcR[hp_][cidx], tdR], Wr=[tdR])
                                if tt + 1 < NTE:
                                    S.op("pool", lambda e, cidx=cidx, hn_=hn_: e.tensor_tensor(hc[hn_][:, cidx, :], hcn[:, cidx, :], hcn2[:, cidx, :], ALU.add), Rd=[hcnR[cidx]], Wr=[hcR[hn_][cidx]])
                            if pend:
                                e_fin(*pend.pop())
                            pend.append((s, j, tt))
                            if tt == 0 and j % 2 == 0 and 4 + j < 22:
                                e_wload(4 + j)
                                e_wload(5 + j)
                            if j == 8 and tt + 1 < NTE:
                                e_pro(tt + 1)
                        if tt + 2 < NTE:
                            e_load(tt + 2)
                    if pend:
                        e_fin(*pend.pop())
                    S.barrier()

                with ExitStack() as ph:
                    wdn = sb("wdnE", [128, NFF, D], BF16, ph)
                    wdnR = [R() for _ in range(11)]
                    wst = [sb("wstF%d" % i, [128, 2, D], F32, ph) for i in range(2)]
                    wstR = [R(), R()]
                    xt = [sb("xtF%d" % i, [128, KC, TT], F32, ph) for i in range(2)]
                    xtR = [R(), R()]
                    actb = [sb("actF%d" % i, [128, NFF, TT], BF16, ph) for i in range(2)]
                    actR = [R(), R()]
                    dst = yT_out if l == n_layers - 1 else x_scr
                    NTE = NT if "F" in phases else 0

                    def f_load(tt):
                        b = tt % 2
                        ts = slice(tt * TT, (tt + 1) * TT)
                        S.dma("sp", actb[b][:], act_scr[:, ts].rearrange("(c p) t -> p c t", p=128), Rd=acr[tt], Wr=[actR[b]])
                        S.dma("sp", xt[b][:], x_scr[:, ts].rearrange("(kc p) t -> p kc t", p=128), Rd=[xr[tt]], Wr=[xtR[b]])

                    if NTE:
                        f_load(0)
                    for pi in range(11):
                        b = pi % 2
                        S.dma("sp", wst[b][:], wdn_in[l, pi * 256:(pi + 1) * 256, :].rearrange("(c p) n -> p c n", p=128), Wr=[wstR[b]])
                        S.op("act", lambda e, b=b, pi=pi: e.copy(wdn[:, 2 * pi:2 * pi + 2, :], wst[b][:]), Rd=[wstR[b]], Wr=[wdnR[pi]])
                    if NTE:
                        f_load(1)
                    for tt in range(NTE):
                        b = tt % 2
                        ts = slice(tt * TT, (tt + 1) * TT)
                        for co in range(8):
                            bk = co % 4
                            for j in range(NFF):
                                S.op("pe", lambda e, j=j, co=co, bk=bk, b=b: e.matmul(banks[bk][:], wdn[:, j, co * 128:(co + 1) * 128], actb[b][:, j, :], start=(j == 0), stop=(j == NFF - 1)), Rd=[wdnR[j // 2], actR[b]], Wr=[bR[bk]])
                            S.op("dve", lambda e, co=co, bk=bk, b=b: e.scalar_tensor_tensor(xt[b][:, co, :], banks[bk][:], g2(co), xt[b][:, co, :], ALU.mult, ALU.add), Rd=[bR[bk], modsR, xtR[b]], Wr=[xtR[b]])
                        S.dma("pool", dst[:, ts].rearrange("(kc p) t -> p kc t", p=128), xt[b][:], Rd=[xtR[b]], Wr=[xr[tt]])
                        if tt + 2 < NTE:
                            f_load(tt + 2)
                    S.barrier()
                S.barrier()
        if dbg:
            with ExitStack() as ph:
                t1 = sb("cp1", [128, 8, 2048], F32, ph)
                tR = R()
                for hhalf in range(2):
                    S.dma("sp", t1[:], o_scr[:, hhalf * 2048:(hhalf + 1) * 2048].rearrange("(c p) t -> p c t", p=128), Rd=ors, Wr=[tR])
                    S.dma("sp", dbg_o[:, hhalf * 2048:(hhalf + 1) * 2048].rearrange("(c p) t -> p c t", p=128), t1[:], Rd=[tR])
                S.barrier()
        S.barrier()
    print("sched: ops=%d waits=%d" % (S.nops, S.nwait))
    return nc


def _pack_params(inp, l):
    P = np.zeros((128, NPAR), np.float32)

    def put(name, arr):
        lo, hi = _cols[name]
        P[:, lo:hi] = np.asarray(arr, np.float32).reshape(128, hi - lo)

    f = lambda k: np.asarray(inp[k][l], np.float32)
    put("ada_b", f("ada_b").reshape(48, 128).T)
    put("n1g", f("norm1_g").reshape(8, 128).T)
    put("n2g", f("norm2_g").reshape(8, 128).T)
    put("qg", np.tile(f("q_norm_g"), 2))
    put("kg", np.tile(f("k_norm_g"), 2))
    put("aog", f("attn_out_g").reshape(4, 128).T)
    put("sog", f("ssm_out_g").reshape(4, 128).T)
    are = f("ssm_a_re").reshape(4, 8, 64)
    aim = f("ssm_a_im").reshape(4, 8, 64)
    ldt = f("ssm_log_dt").reshape(4, 8)
    put("aBre", np.broadcast_to(are.transpose(1, 0, 2)[:, None], (8, 16, 4, 64)))
    put("aBim", np.broadcast_to(aim.transpose(1, 0, 2)[:, None], (8, 16, 4, 64)))
    put("ldtB", np.broadcast_to(ldt.T[:, None], (8, 16, 4)))
    bre = f("ssm_b_re").reshape(4, 8, 64, 16)
    bim = f("ssm_b_im").reshape(4, 8, 64, 16)
    put("BreB", bre.transpose(1, 3, 0, 2))
    put("BimB", bim.transpose(1, 3, 0, 2))
    are2 = f("ssm_a_re").reshape(4, 4, 2, 64)
    aim2 = f("ssm_a_im").reshape(4, 4, 2, 64)
    ldt2 = f("ssm_log_dt").reshape(4, 4, 2)
    put("aSre", are2.transpose(2, 3, 0, 1))
    put("aSim", aim2.transpose(2, 3, 0, 1))
    put("ldtS", np.broadcast_to(ldt2.transpose(2, 0, 1)[:, None], (2, 64, 4, 4)))
    cre = f("ssm_c_re").reshape(4, 4, 2, 16, 64)
    cim = f("ssm_c_im").reshape(4, 4, 2, 16, 64)
    put("CreS", cre.transpose(2, 4, 0, 1, 3))
    put("CimS", cim.transpose(2, 4, 0, 1, 3))
    put("dsk", f("ssm_d").reshape(4, 128).T)
    put("GS", f("glu_w").reshape(4, 8, 16, 16).transpose(1, 2, 0, 3))
    put("glub", f("glu_b").reshape(4, 128).T)
    put("cw", f("ffn_conv_w").reshape(3, 44, 128).transpose(2, 0, 1))
    put("cb", f("ffn_conv_b").reshape(44, 128).T)
    return P


def _in_maps(inp):
    par = np.stack([_pack_params(inp, l) for l in range(DEPTH)], 0)
    shared = {
        "par": par,
        "ada_w": np.ascontiguousarray(inp["ada_w"], np.float32),
        "w_in": np.ascontiguousarray(inp["w_in"], np.float32),
        "w_out": np.ascontiguousarray(inp["w_out"], np.float32),
        "w_up": np.ascontiguousarray(inp["ffn_w_up"], np.float32),
        "w_dn": np.ascontiguousarray(inp["ffn_w_down"], np.float32),
    }
    maps = []
    for b in range(8):
        m = dict(shared)
        m["xT"] = np.ascontiguousarray(np.asarray(inp["x"][b], np.float32).T)
        m["cT"] = np.ascontiguousarray(np.asarray(inp["c"][b], np.float32).reshape(KC, 128).T)
        maps.append(m)
    return maps


def kernel(**inputs):
    inp = {k: np.asarray(v) for k, v in inputs.items()}
    nc = build(DEPTH)
    res = run_bass_kernel_spmd(nc, _in_maps(inp), core_ids=list(range(8)))
    out = np.stack([np.ascontiguousarray(r["yT"].T) for r in res.results], 0)
    return out.astype(np.float32)
```
